# Optimizing a Trainium2 kernel written in Bass

```python
import jax, jax.numpy as jnp
from jax import lax
import numpy as np

D_MODEL = 2048
BATCH = 8
SEQ = 4096
DEPTH = 1
DEC_BATCH = 4
DEC_SEQ = 2048
PAST_LEN = 128

HEAD_DIM = 128
A_Q_HEADS = 8
A_KV_HEADS = 2
A_GROUP = A_Q_HEADS // A_KV_HEADS
WINDOW = 128
BLK = WINDOW
N_BUCKETS = 32
MAX_DISTANCE = 128
B_HEADS = 8
GRID_W = 64
NA_KH = 8
NA_KW = 16
NA_QCB = NA_KW
NA_REG = 2 * NA_KW
D_FF = 4 * D_MODEL
EPS = 1e-6
NEG = -1e30

A_Q_W = A_Q_HEADS * HEAD_DIM
A_KV_W = A_KV_HEADS * HEAD_DIM
B_W = B_HEADS * HEAD_DIM
IN_SPLITS = (A_Q_W, A_KV_W, A_KV_W, B_W, B_W, B_W, D_MODEL, D_MODEL)
IN_WIDTH = A_Q_W + 2 * A_KV_W + 3 * B_W + 2 * D_MODEL

kernel_name = "hybrid_window_gqa_neighbourhood_encoder"


def rmsnorm(x, g):
    xf = x.astype(jnp.float32)
    y = xf * lax.rsqrt(jnp.mean(xf * xf, axis=-1, keepdims=True) + EPS) * g.astype(jnp.float32)
    return y.astype(x.dtype)


def t5_bucket(rel):
    nb = N_BUCKETS // 2
    max_exact = nb // 2
    ret = (rel > 0).astype(np.int32) * nb
    n = np.abs(rel).astype(np.int32)
    nf = np.maximum(n, max_exact).astype(np.float32)
    large = max_exact + (np.log(nf / max_exact) / np.log(MAX_DISTANCE / max_exact) * (nb - max_exact)).astype(np.int32)
    large = np.minimum(large, nb - 1)
    return ret + np.where(n < max_exact, n, large)


def window_attention(q, k, v, q_gain, k_gain, t5_table, sink):
    b, L = q.shape[0], q.shape[1]
    nb = L // BLK
    q = rmsnorm(q, q_gain) * (HEAD_DIM ** -0.5)
    k = rmsnorm(k, k_gain)
    qb = q.reshape(b, nb, BLK, A_KV_HEADS, A_GROUP, HEAD_DIM)

    def band(t):
        tp = jnp.pad(t, ((0, 0), (BLK, BLK), (0, 0), (0, 0))).reshape(b, nb + 2, BLK, A_KV_HEADS, HEAD_DIM)
        return jnp.concatenate([tp[:, :-2], tp[:, 1:-1], tp[:, 2:]], axis=2)

    kb, vb = band(k), band(v)
    s = jnp.einsum('bnqhgd,bnkhd->bnhgqk', qb, kb).astype(jnp.float32)
    rel = np.arange(3 * BLK)[None, :] - BLK - np.arange(BLK)[:, None]
    bias = t5_table[t5_bucket(rel)].astype(jnp.float32)
    bias = jnp.transpose(bias, (2, 0, 1)).reshape(A_KV_HEADS, A_GROUP, BLK, 3 * BLK)
    kpos = np.arange(nb)[:, None] * BLK - BLK + np.arange(3 * BLK)[None, :]
    mask = (np.abs(rel) <= WINDOW)[None] & ((kpos >= 0) & (kpos < L))[:, None, :]
    s = jnp.where(mask[None, :, None, None], s + bias[None, None], NEG)
    sk = sink.astype(jnp.float32).reshape(1, 1, A_KV_HEADS, A_GROUP, 1, 1)
    m = jnp.maximum(jnp.max(s, axis=-1, keepdims=True), sk)
    p = jnp.exp(s - m)
    denom = jnp.sum(p, axis=-1, keepdims=True) + jnp.exp(sk - m)
    o = jnp.einsum('bnhgqk,bnkhd->bnqhgd', (p / denom).astype(v.dtype), vb)
    return o.reshape(b, L, A_Q_W)


def neighbourhood_attention(q, k, v, q_gain, k_gain, rpb):
    b, L = q.shape[0], q.shape[1]
    rows = L // GRID_W
    kh = min(NA_KH, rows)
    ncb = GRID_W // NA_QCB
    q = rmsnorm(q, q_gain) * (HEAD_DIM ** -0.5)
    k = rmsnorm(k, k_gain)
    qg = q.reshape(b, rows, GRID_W, B_HEADS, HEAD_DIM)
    kg = k.reshape(b, rows, GRID_W, B_HEADS, HEAD_DIM)
    vg = v.reshape(b, rows, GRID_W, B_HEADS, HEAD_DIM)
    reg_start = np.clip(np.arange(ncb) * NA_QCB - NA_KW // 2, 0, GRID_W - NA_REG)
    col_idx = reg_start[:, None] + np.arange(NA_REG)[None, :]
    qc = np.arange(GRID_W).reshape(ncb, NA_QCB)
    start_c = np.clip(qc - NA_KW // 2, 0, GRID_W - NA_KW)
    kc = col_idx[:, None, :]
    col_mask = (kc >= start_c[:, :, None]) & (kc < start_c[:, :, None] + NA_KW)
    dcc = np.clip(kc - qc[:, :, None], -(NA_KW - 1), NA_KW - 1) + (NA_KW - 1)

    def row_fn(r):
        rs = jnp.clip(r - kh // 2, 0, rows - kh)
        kr = lax.dynamic_slice_in_dim(kg, rs, kh, axis=1)[:, :, col_idx]
        vr = lax.dynamic_slice_in_dim(vg, rs, kh, axis=1)[:, :, col_idx]
        qr = lax.dynamic_index_in_dim(qg, r, axis=1, keepdims=False).reshape(b, ncb, NA_QCB, B_HEADS, HEAD_DIM)
        s = jnp.einsum('bcqhd,brckhd->bhcqrk', qr, kr).astype(jnp.float32)
        dr = rs + jnp.arange(kh) - r + (NA_KH - 1)
        bias = rpb[:, dr][:, :, dcc].astype(jnp.float32)
        bias = jnp.transpose(bias, (0, 2, 3, 1, 4))
        s = jnp.where(col_mask[None, None, :, :, None, :], s + bias[None], NEG)
        p = jax.nn.softmax(s.reshape(b, B_HEADS, ncb, NA_QCB, kh * NA_REG), axis=-1).reshape(s.shape)
        o = jnp.einsum('bhcqrk,brckhd->bcqhd', p.astype(v.dtype), vr)
        return o.reshape(b, GRID_W, B_W)

    out = lax.map(row_fn, jnp.arange(rows))
    return jnp.transpose(out, (1, 0, 2, 3)).reshape(b, L, B_W)


def trunk(x, norm_mix_g, w_in, q_norm_a, k_norm_a, t5_bias, sink_a, q_norm_b, k_norm_b, rpb_b,
          w_br_a, w_br_b, w_out, norm_mlp_g, w_up, w_down):
    b, L = x.shape[0], x.shape[1]
    offs = list(np.cumsum(IN_SPLITS)[:-1])
    for l in range(DEPTH):
        h = rmsnorm(x, norm_mix_g[l])
        proj = h @ w_in[l]
        qa, ka, va, qb, kb, vb, ga, gb = jnp.split(proj, offs, axis=-1)
        ya = window_attention(qa.reshape(b, L, A_Q_HEADS, HEAD_DIM),
                              ka.reshape(b, L, A_KV_HEADS, HEAD_DIM),
                              va.reshape(b, L, A_KV_HEADS, HEAD_DIM),
                              q_norm_a[l], k_norm_a[l], t5_bias, sink_a[l]) @ w_br_a[l]
        yb = neighbourhood_attention(qb.reshape(b, L, B_HEADS, HEAD_DIM),
                                     kb.reshape(b, L, B_HEADS, HEAD_DIM),
                                     vb.reshape(b, L, B_HEADS, HEAD_DIM),
                                     q_norm_b[l], k_norm_b[l], rpb_b[l]) @ w_br_b[l]
        merged = jax.nn.sigmoid(ga) * ya + jax.nn.sigmoid(gb) * yb
        x = x + merged @ w_out[l]
        h = rmsnorm(x, norm_mlp_g[l])
        x = x + jnp.square(jax.nn.relu(h @ w_up[l])) @ w_down[l]
    return x


def setup_inputs(seed: int = 0) -> dict:
    key = jax.random.key(seed)
    ks = jax.random.split(key, 20)
    f = jnp.float32
    n = lambda k, shape, s: jax.random.normal(k, shape, f) * s
    return {
        "x_prompt": n(ks[0], (BATCH, SEQ, D_MODEL), 1.0),
        "x_sample": n(ks[1], (DEC_BATCH, DEC_SEQ, D_MODEL), 1.0),
        "norm_mix_g": 1.0 + n(ks[2], (DEPTH, D_MODEL), 0.1),
        "w_in": n(ks[3], (DEPTH, D_MODEL, IN_WIDTH), D_MODEL ** -0.5),
        "q_norm_a": 1.0 + n(ks[4], (DEPTH, HEAD_DIM), 0.1),
        "k_norm_a": 1.0 + n(ks[5], (DEPTH, HEAD_DIM), 0.1),
        "t5_bias": n(ks[6], (N_BUCKETS, A_Q_HEADS), 0.5),
        "sink_a": n(ks[7], (DEPTH, A_Q_HEADS), 1.0),
        "q_norm_b": 1.0 + n(ks[8], (DEPTH, HEAD_DIM), 0.1),
        "k_norm_b": 1.0 + n(ks[9], (DEPTH, HEAD_DIM), 0.1),
        "rpb_b": n(ks[10], (DEPTH, B_HEADS, 2 * NA_KH - 1, 2 * NA_KW - 1), 0.5),
        "w_br_a": n(ks[11], (DEPTH, A_Q_W, D_MODEL), A_Q_W ** -0.5),
        "w_br_b": n(ks[12], (DEPTH, B_W, D_MODEL), B_W ** -0.5),
        "w_out": n(ks[13], (DEPTH, D_MODEL, D_MODEL), D_MODEL ** -0.5),
        "norm_mlp_g": 1.0 + n(ks[14], (DEPTH, D_MODEL), 0.1),
        "w_up": n(ks[15], (DEPTH, D_MODEL, D_FF), D_MODEL ** -0.5),
        "w_down": n(ks[16], (DEPTH, D_FF, D_MODEL), D_FF ** -0.5),
    }


def reference(x_prompt, x_sample, norm_mix_g, w_in, q_norm_a, k_norm_a, t5_bias, sink_a, q_norm_b, k_norm_b,
              rpb_b, w_br_a, w_br_b, w_out, norm_mlp_g, w_up, w_down):
    y_prompt = trunk(x_prompt, norm_mix_g, w_in, q_norm_a, k_norm_a, t5_bias, sink_a, q_norm_b, k_norm_b,
                     rpb_b, w_br_a, w_br_b, w_out, norm_mlp_g, w_up, w_down)
    y_sample = trunk(x_sample, norm_mix_g, w_in, q_norm_a, k_norm_a, t5_bias, sink_a, q_norm_b, k_norm_b,
                     rpb_b, w_br_a, w_br_b, w_out, norm_mlp_g, w_up, w_down)
    return (y_prompt, y_sample)
```

```python
from contextlib import ExitStack

import numpy as np
import concourse.bass as bass
import concourse.mybir as mybir
from concourse.bass_utils import run_bass_kernel_spmd

F32 = mybir.dt.float32
BF16 = mybir.dt.bfloat16
AF = mybir.ActivationFunctionType
ALU = mybir.AluOpType

D = 2048
HD = 128
IN_W = 8704
DFF = 8192
EPS = 1e-6
NEGM = -30000.0
T = 512
HALO = 256
NCORES = 8

C_QA, C_KA, C_VA, C_QB, C_KB, C_VB, C_GA, C_GB = 0, 1024, 1280, 1536, 2560, 3584, 4608, 6656

NA_KT = {0: list(range(-2, 4)), 1: list(range(-1, 4)), 2: list(range(0, 5)), 3: list(range(0, 6))}


ATTACH_WAITS = True


class Sem:
    def __init__(self, h):
        self.h = h
        self.count = 0


class Buf:
    __slots__ = ("w", "r")

    def __init__(self):
        self.w = None
        self.r = {}


class Queue:
    def __init__(self, name, sem, is_pe=False, is_dma=False):
        self.name = name
        self.sem = sem
        self.ops = []
        self.waited = {}
        self.pending = []
        self.is_pe = is_pe
        self.is_dma = is_dma

    def wait(self, tok):
        s, v = tok
        if self.waited.get(s, 0) >= v:
            return
        self.waited[s] = v
        if self.is_dma or not ATTACH_WAITS:
            self.ops.append(lambda e, s=s, v=v: e.wait_ge(s.h, v))
        else:
            self.pending.append((s, v))

    def emit(self, fn, sem=None):
        pend = self.pending
        self.pending = []
        for s, v in pend[1:]:
            self.ops.append(lambda e, s=s, v=v: e.wait_ge(s.h, v))
        first = pend[0] if pend else None

        def thunk(e, fn=fn, first=first, sem=sem):
            ins = fn(e)
            if first is not None:
                ins = ins._wait_ge(first[0].h, first[1])
            if sem is not None:
                ins = ins.then_inc(sem.h, 1)
            return ins
        self.ops.append(thunk)


def _deps(q, reads, writes):
    deps = []
    for b in reads:
        if b.w is not None:
            deps.append(b.w)
    for b in writes:
        if b.w is not None:
            deps.append(b.w)
        for s, v in b.r.items():
            deps.append((s, v))
    for tok in deps:
        if tok[0] is q.sem and (q.is_pe or q.is_dma):
            continue
        q.wait(tok)


def _record(tok, reads, writes):
    s, v = tok
    for b in reads:
        if b.r.get(s, 0) < v:
            b.r[s] = v
    for b in writes:
        b.w = tok
        b.r = {}


def op(q, fn, reads=(), writes=(), signal=True):
    _deps(q, reads, writes)
    if signal:
        q.sem.count += 1
        tok = (q.sem, q.sem.count)
        q.emit(fn, q.sem)
    else:
        assert q.is_pe
        tok = (q.sem, q.sem.count + 1)
        q.emit(fn, None)
    _record(tok, reads, writes)
    return tok


def dma(q, dsem, out_ap, in_ap, reads=(), writes=(), **kw):
    _deps(q, reads, writes)
    dsem.count += 16
    tok = (dsem, dsem.count)
    q.ops.append(lambda e, o=out_ap, i=in_ap, s=dsem, kw=kw: e.dma_start(out=o, in_=i, **kw).then_inc(s.h, 16))
    _record(tok, reads, writes)
    return tok


class Rot:
    def __init__(self, items):
        self.items = items
        self.i = 0

    def next(self):
        it = self.items[self.i % len(self.items)]
        self.i += 1
        return it


def build_program(nslots, slot_rows, slot_types, slot_cases=None, dbg=None):
    nc = bass.Bass("TRN2", target_bir_lowering=False)
    NROWS_IN = max(slot_rows) + 1024

    def din(name, shape, dt=F32):
        return nc.dram_tensor(name, list(shape), dt, kind="ExternalInput").ap()

    xin = din("xin", [NROWS_IN, D])
    w_in = din("w_in", [D, IN_W])
    w_bra = din("w_bra", [1024, D])
    w_brb = din("w_brb", [1024, D])
    w_out = din("w_out", [D, D])
    w_up = din("w_up", [D, DFF])
    w_down = din("w_down", [DFF, D])
    pvec_d = din("pvec", [128, 36])
    sink_d = din("sinkrep", [128, 8])
    wbias_d = din("wbias", [128, 8 * 3 * 128])
    nbias_d = din("nbias", [128, 8 * 7 * 128])
    lsel_d = din("lsel", [128, 8 * 128])
    mrow_d = din("mrow", [128, nslots * 8])
    masks_d = din("masks", [128, nslots * 2])
    ident_d = din("ident", [128, 128])
    y = nc.dram_tensor("y", [nslots * T, D], F32, kind="ExternalOutput").ap()
    dbg_out = None
    if dbg is not None:
        dbg_out = nc.dram_tensor("dbg", list(dbg[1]), BF16 if dbg[0] in ("A", "B") else F32, kind="ExternalOutput").ap()

    wb_in = nc.dram_tensor("wb_in", [D, IN_W], BF16).ap()
    wb_bra = nc.dram_tensor("wb_bra", [1024, D], BF16).ap()
    wb_brb = nc.dram_tensor("wb_brb", [1024, D], BF16).ap()
    wb_out = nc.dram_tensor("wb_out", [D, D], BF16).ap()
    wb_up = nc.dram_tensor("wb_up", [D, DFF], BF16).ap()
    wb_down = nc.dram_tensor("wb_down", [DFF, D], BF16).ap()

    with ExitStack() as es:
        def sb(name, shape, dt):
            return es.enter_context(nc.sbuf_tensor(name, list(shape), dt))

        def mksem(name):
            return Sem(es.enter_context(nc.semaphore(name)))

        xres = sb("xres", [128, 4, D], F32)
        xstage = sb("xstage", [128, D], F32)
        hbf = [sb(f"hbf{i}", [128, D], BF16) for i in range(2)]
        hTc = sb("hTc", [128, 16, T], BF16)
        hTh = sb("hTh", [128, 16, 512], BF16)
        arena = sb("arena", [128, 16384], BF16)
        KaT = sb("KaT", [128, 2, 768], BF16)
        Va = sb("Va", [128, 6, 2, 130], BF16)
        Vb = sb("Vb", [128, 8, 8, 130], BF16)
        E_w = sb("E_w", [128, 8, 3, 128], BF16)
        E_n = sb("E_n", [128, 8, 7, 128], BF16)
        masks = sb("masks_sb", [128, nslots * 2], F32)
        Lsel = sb("Lsel", [128, 8, 128], BF16)
        Mrow = sb("Mrow", [128, nslots * 8], BF16)
        pvec = sb("pvecs", [128, 36], F32)
        qsc = sb("qsc", [128, 4], F32)
        expsink = sb("expsink", [128, 8], F32)
        ident = sb("ident_b", [128, 128], BF16)
        ones_b = sb("ones_b", [128, 128], BF16)
        sqh = [sb(f"sqh{i}", [128, 512], BF16) for i in range(2)]
        wbuf = [sb(f"wbuf{i}", [128, 16, 512], BF16) for i in range(2)]
        tmpf = sb("tmpf", [128, 2048], F32)
        sqb = [tmpf[:, 0:512], tmpf[:, 512:1024]]
        lnb = [tmpf[:, 1024:1536], tmpf[:, 1536:2048]]
        expS = [tmpf[:, 0:768], tmpf[:, 1024:1792]]
        PT = [sb(f"PT{i}", [128, 6, 128], BF16) for i in range(3)]
        Onb = [sb(f"On{i}", [128, 128], BF16) for i in range(2)]
        relu_t = sqb
        small = sb("small", [128, 64], F32)
        gt_ap = hTh[:].rearrange("p a b -> p (a b)").bitcast(F32)

        psf = [es.enter_context(nc.psum_tensor(f"psf{i}", [128, 512], F32)) for i in range(6)]
        psb = [es.enter_context(nc.psum_tensor(f"psb{i}", [128, 1024], BF16)) for i in range(2)]
        psf_rot = Rot([(psf[i], Buf()) for i in range(6)])
        psb_rot = Rot([(psb[i], Buf()) for i in range(2)])

        PE = Queue("pe", mksem("s_pe"), is_pe=True)
        ACT = Queue("act", mksem("s_act"))
        DVE = Queue("dve", mksem("s_dve"))
        POOL = Queue("pool", mksem("s_pool"), is_dma=True)
        SP = Queue("sp", mksem("s_sp"), is_dma=True)
        ds_w = [mksem(f"ds_w{i}") for i in range(2)]
        ds_xs = mksem("ds_xs")
        ds_xr = [mksem(f"ds_xr{i}") for i in range(4)]
        ds_st = [mksem(f"ds_st{i}") for i in range(4)]
        ds_c = [mksem(f"ds_c{i}") for i in range(8)]
        ds_dbg = mksem("ds_dbg")

        B_xres = [Buf() for _ in range(4)]
        B_xstage = Buf()
        B_hbf = [Buf(), Buf()]
        B_hTc = [Buf() for _ in range(4)]
        B_hTh = [Buf() for _ in range(4)]
        B_QT = [[Buf() for _ in range(4)] for _ in range(16)]
        B_KbT_lo = [Buf() for _ in range(8)]
        B_KbT_hi = [Buf() for _ in range(8)]
        B_KbT = B_KbT_lo + B_KbT_hi
        B_KaT = [Buf() for _ in range(2)]
        B_Va = [Buf() for _ in range(6)]
        B_Vb = [[Buf() for _ in range(2)] for _ in range(8)]
        B_wbuf = [Buf(), Buf()]
        B_sq = [Buf(), Buf()]
        B_sqh = [Buf(), Buf()]
        B_ln = [Buf(), Buf()]
        B_expS = [B_sq, B_ln]
        B_PT = [Buf(), Buf(), Buf()]
        B_On = [Buf(), Buf()]
        B_relu = B_sq
        B_small = [Buf() for _ in range(16)]
        B_const = Buf()
        B_gt = [Buf() for _ in range(8)]
        all_QT = [b for hb in B_QT for b in hb]

        QT = arena[:, 0:8192].rearrange("p (h t) -> p h t", h=16)
        KbT = arena[:, 8192:16384].rearrange("p (h t) -> p h t", h=8)
        actT = arena[:, 0:8192].rearrange("p (c t) -> p c t", c=16)
        Vb_lo_flat = Vb[:, 0:4, :, :].rearrange("p a b c -> p (a b c)")
        ds_sh = [mksem(f"ds_sh{i}") for i in range(4)]

        def merged_ap(c):
            if c < 8:
                return KbT[:, c, 0:512]
            return Vb_lo_flat[:, (c - 8) * 512:(c - 7) * 512]

        B_Vb_lo = [b_ for t_ in range(4) for b_ in B_Vb[t_]]

        def B_merged(c):
            return [B_KbT_lo[c]] if c < 8 else B_Vb_lo

        B_merged_all = B_KbT_lo + B_Vb_lo
        B_act = all_QT

        CAST = {}
        cast_jobs = []

        def cjob(key, dst, src):
            b = Buf()
            CAST[key] = b
            cast_jobs.append((key, dst, src, b))

        for i in range(9):
            cjob(("in", i), wb_in[:, i * 512:(i + 1) * 512], w_in[:, i * 512:(i + 1) * 512])
        for n in range(4):
            cjob(("in", 9 + n), wb_in[:, C_GA + n * 512: C_GA + (n + 1) * 512], w_in[:, C_GA + n * 512: C_GA + (n + 1) * 512])
            cjob(("in", 13 + n), wb_in[:, C_GB + n * 512: C_GB + (n + 1) * 512], w_in[:, C_GB + n * 512: C_GB + (n + 1) * 512])
            cjob(("bra", n), wb_bra[:, n * 512:(n + 1) * 512], w_bra[:, n * 512:(n + 1) * 512])
            cjob(("brb", n), wb_brb[:, n * 512:(n + 1) * 512], w_brb[:, n * 512:(n + 1) * 512])
        for n in range(4):
            cjob(("out", n), wb_out[:, n * 512:(n + 1) * 512], w_out[:, n * 512:(n + 1) * 512])
        for fq in range(4):
            for i in range(4):
                c0 = fq * 2048 + i * 512
                cjob(("up", fq * 4 + i), wb_up[:, c0:c0 + 512], w_up[:, c0:c0 + 512])
            for n in range(4):
                cjob(("down", fq * 4 + n), wb_down[fq * 2048:(fq + 1) * 2048, n * 512:(n + 1) * 512],
                     w_down[fq * 2048:(fq + 1) * 2048, n * 512:(n + 1) * 512])

        def issue_casts():
            toks = []
            for k, (key, dst, src, b) in enumerate(cast_jobs):
                if k >= 12:
                    POOL.wait(toks[k - 12])
                sm_ = mksem(f"ds_cast_{key[0]}{key[1]}")
                toks.append(dma(POOL, sm_, dst, src, writes=[b]))

        for tt in range(4):
            dma(POOL, ds_xr[tt], xres[:, tt, :], xin[slot_rows[0] + (tt + 2) * 128: slot_rows[0] + (tt + 3) * 128, :],
                writes=[B_xres[tt]])
        P_ = []

        def cb():
            b_ = Buf()
            P_.append(b_)
            return b_

        b_pvec, b_sink = cb(), cb()
        dma(SP, ds_c[0], pvec[:], pvec_d, writes=[b_pvec])
        dma(SP, ds_c[1], expsink[:], sink_d, writes=[b_sink])
        dma(SP, ds_c[2], masks[:], masks_d, writes=[cb()])
        dma(POOL, ds_c[3], ident[:], ident_d, writes=[cb()])
        dma(POOL, ds_c[4], E_w[:].rearrange("p a b c -> p (a b c)"), wbias_d, writes=[cb()], max_dma_last_dim=4096)
        dma(POOL, ds_c[5], E_n[:].rearrange("p a b c -> p (a b c)"), nbias_d, writes=[cb()], max_dma_last_dim=4096)
        dma(POOL, ds_c[6], Lsel[:].rearrange("p a b -> p (a b)"), lsel_d, writes=[cb()])
        dma(POOL, ds_c[7], Mrow[:], mrow_d, writes=[cb()])
        op(ACT, lambda e: e.activation(out=expsink[:], in_=expsink[:], func=AF.Exp), reads=[b_sink], writes=[b_sink])
        op(DVE, lambda e: e.memset(ones_b[:], 1.0), writes=[cb()])
        op(DVE, lambda e: e.memset(Va[:].rearrange("p a b c -> p (a b c)"), 1.0), writes=B_Va)
        op(DVE, lambda e: e.memset(Vb[:].rearrange("p a b c -> p (a b c)"), 1.0), writes=[b for x in B_Vb for b in x])
        sc = float(HD) ** -0.5
        b_qsc = cb()
        op(DVE, lambda e: e.tensor_scalar(out=qsc[:, 0:1], in0=pvec[:, 32:33], scalar1=sc, scalar2=None, op0=ALU.mult),
           reads=[b_pvec], writes=[b_qsc])
        op(DVE, lambda e: e.tensor_copy(out=qsc[:, 1:2], in_=pvec[:, 33:34]), reads=[b_pvec], writes=[b_qsc])
        op(DVE, lambda e: e.tensor_scalar(out=qsc[:, 2:3], in0=pvec[:, 34:35], scalar1=sc, scalar2=None, op0=ALU.mult),
           reads=[b_pvec], writes=[b_qsc])
        op(DVE, lambda e: e.tensor_copy(out=qsc[:, 3:4], in_=pvec[:, 35:36]), reads=[b_pvec], writes=[b_qsc])

        wctr = [0]

        def load_w(src_ap, kcs, cast_key):
            i = wctr[0] % 2
            wctr[0] += 1
            dma(SP, ds_w[i], wbuf[i][:, 0:kcs, :], src_ap.rearrange("(kc p) n -> p kc n", p=128),
                reads=[CAST[cast_key]], writes=[B_wbuf[i]])
            return wbuf[i], B_wbuf[i]

        sctr = [0]

        def small_col():
            i = sctr[0] % 16
            sctr[0] += 1
            return small[:, 4 * i:4 * i + 4], B_small[i]

        rot2 = {"sq": 0, "ln": 0, "expS": 0, "PT": 0, "On": 0, "relu": 0, "hbf": 0}

        def nxt(k):
            i = rot2[k] % 2
            rot2[k] += 1
            return i

        pend = []

        def flush(keep=0):
            while len(pend) > keep:
                pend.pop(0)()

        def rms1(x_ap, xbuf):
            hi = nxt("hbf")
            sm, smb = small_col()
            op(ACT, lambda e: e.activation(out=hbf[hi][:], in_=x_ap, func=AF.Square, accum_out=sm[:, 0:1]),
               reads=[xbuf], writes=[B_hbf[hi], smb])
            op(ACT, lambda e: e.activation(out=sm[:, 1:2], in_=sm[:, 0:1], func=AF.Ln, scale=1.0 / D, bias=sm[:, 3:4]),
               reads=[smb], writes=[smb])
            op(ACT, lambda e: e.activation(out=sm[:, 2:3], in_=sm[:, 1:2], func=AF.Exp, scale=-0.5),
               reads=[smb], writes=[smb])
            op(DVE, lambda e: e.tensor_scalar(out=hbf[hi][:], in0=x_ap, scalar1=sm[:, 2:3], scalar2=None, op0=ALU.mult),
               reads=[xbuf, smb], writes=[B_hbf[hi]])
            return hi

        def rms2(hi, gcol0, dst_ap_fn, dst_bufs):
            for half in range(2):
                pb, pbb = psb_rot.next()
                for cc in range(8):
                    ch = half * 8 + cc
                    op(PE, lambda e, pb=pb, cc=cc, ch=ch: e.transpose(out=pb[:, cc * 128:(cc + 1) * 128],
                                                                     in_=hbf[hi][:, ch * 128:(ch + 1) * 128],
                                                                     identity=ident[:]),
                       reads=[B_hbf[hi], B_const], writes=[pbb], signal=(cc == 7))
                g_b = pvec[:, gcol0 + half * 8: gcol0 + half * 8 + 8].unsqueeze(2).to_broadcast([128, 8, 128])
                op(DVE, lambda e, pb=pb, half=half, g_b=g_b: e.tensor_tensor(
                    out=dst_ap_fn(half), in0=pb[:].rearrange("p (c t) -> p c t", c=8), in1=g_b, op=ALU.mult),
                   reads=[pbb, B_const], writes=dst_bufs)

        def prologue_steps(sn):
            r0n = slot_rows[sn]
            tiles = (0, 1, 6, 7, 2, 3, 4, 5) if slot_types[sn] == "F" else (6, 7, 2, 3, 4, 5)
            nt = len(tiles)
            p1s, p2s = [], []
            for t in tiles:
                def p1(t=t):
                    if sn == 0 and 2 <= t <= 5:
                        return rms1(xres[:, t - 2, :], B_xres[t - 2])
                    dma(SP, ds_xs, xstage[:], xin[r0n + t * 128: r0n + (t + 1) * 128, :], writes=[B_xstage])
                    return rms1(xstage[:], B_xstage)

                def p2(hi, t=t):
                    if 2 <= t <= 5:
                        tt = t - 2
                        rms2(hi, 0, lambda half: hTc[:, half * 8:(half + 1) * 8, tt * 128:(tt + 1) * 128], [B_hTc[tt]])
                    else:
                        hh = t if t < 2 else t - 4
                        rms2(hi, 0, lambda half: hTh[:, half * 8:(half + 1) * 8, hh * 128:(hh + 1) * 128],
                             [B_hTh[hh]] + B_gt)
                p1s.append(p1)
                p2s.append(p2)
            ctx = {}
            steps = []
            for k in range(nt + 1):
                def step(k=k):
                    if 1 <= k <= nt:
                        p2s[k - 1](ctx[k - 1])
                    if k <= nt - 1:
                        ctx[k] = p1s[k]()
                steps.append(step)
            return steps

        def qknorm(ps, psbuf, n, gcol, out_ap, out_bufs):
            si = nxt("sq")
            li = nxt("ln")
            op(ACT, lambda e: e.activation(out=sqh[si][:, 0:n], in_=ps[:, 0:n], func=AF.Square),
               reads=[psbuf], writes=[B_sqh[si]])
            ps2, ps2b = psf_rot.next()

            def pe_part():
                op(PE, lambda e: e.matmul(ps2[:, 0:n], lhsT=ones_b[:], rhs=sqh[si][:, 0:n], start=True, stop=True),
                   reads=[B_sqh[si], B_const], writes=[ps2b])
                op(ACT, lambda e: e.activation(out=lnb[li][:, 0:n], in_=ps2[:, 0:n], func=AF.Ln, scale=1.0 / HD,
                                               bias=small[:, 63:64]),
                   reads=[ps2b, B_const], writes=[B_ln[li]])
                op(ACT, lambda e: e.activation(out=lnb[li][:, 0:n], in_=lnb[li][:, 0:n], func=AF.Exp, scale=-0.5),
                   reads=[B_ln[li]], writes=[B_ln[li]])
                op(DVE, lambda e: e.scalar_tensor_tensor(out=out_ap, in0=ps[:, 0:n], scalar=qsc[:, gcol:gcol + 1],
                                                          in1=lnb[li][:, 0:n], op0=ALU.mult, op1=ALU.mult),
                   reads=[psbuf, B_ln[li], B_const], writes=out_bufs)
            pend.append(pe_part)

        op(DVE, lambda e: e.memset(small[:], EPS), writes=B_small)
        op(DVE, lambda e: e.memset(relu_t[0][:, 0:1], 0.0), reads=P_, writes=[B_const, B_sq[0]])

        deferred_stores = []
        for s in range(nslots):
            r0 = slot_rows[s]
            mw0 = s * 2
            mn0 = nslots * 2 + s * 64

            if s == 0:
                for stp in prologue_steps(0):
                    stp()
                issue_casts()
            nxt_steps = prologue_steps(s + 1) if s + 1 < nslots else []
            early_steps = nxt_steps[:-4]
            late_steps = nxt_steps[-4:]
            assert len(early_steps) <= 5

            def pro_early():
                if early_steps:
                    early_steps.pop(0)()

            def pro_late():
                if late_steps:
                    late_steps.pop(0)()

            seg_c = (lambda kc: hTc[:, kc, :], 512, B_hTc)
            seg_b = (lambda kc: hTh[:, kc, 0:256], 256, B_hTh[0:2])
            seg_a = (lambda kc: hTh[:, kc, 256:512], 256, B_hTh[2:4])

            def proj_fm(wt, wtb, j, seg, then):
                rhs_fn, n, hb = seg
                ps, psbuf = psf_rot.next()
                for kc in range(16):
                    op(PE, lambda e, kc=kc: e.matmul(ps[:, 0:n], lhsT=wt[:, kc, j * 128:(j + 1) * 128], rhs=rhs_fn(kc),
                                                     start=(kc == 0), stop=(kc == 15)),
                       reads=[wtb] + list(hb), writes=[psbuf], signal=(kc == 15))
                flush(0)
                then(ps, psbuf, n)

            for wi in range(2):
                wt, wtb = load_w(wb_in[:, C_QA + wi * 512: C_QA + (wi + 1) * 512], 16, ("in", wi))
                for j in range(4):
                    h = wi * 4 + j
                    proj_fm(wt, wtb, j, seg_c,
                            lambda ps, pb, n, h=h: qknorm(ps, pb, n, 0, QT[:, h, :], B_QT[h]))
            isF = slot_types[s] == "F"
            if not isF:
                assert slot_rows[s] == slot_rows[s - 1] + 512
                dma(SP, ds_sh[0], KbT[:, :, 0:512], KbT[:, :, 512:1024], reads=B_KbT_hi, writes=B_KbT_lo)
                dma(SP, ds_sh[1], Vb[:, 0:4, :, :], Vb[:, 4:8, :, :], reads=[b_ for t_ in range(4, 8) for b_ in B_Vb[t_]],
                    writes=B_Vb_lo)
                dma(SP, ds_sh[2], KaT[:, :, 0:256], KaT[:, :, 512:768], reads=B_KaT, writes=B_KaT)
                dma(SP, ds_sh[3], Va[:, 0:2, :, :], Va[:, 4:6, :, :], reads=B_Va[4:6], writes=B_Va[0:2])
            else:
                op(DVE, lambda e: e.memset(Vb[:, 0:4, :, 128:130], 1.0), writes=B_Vb_lo)
            if deferred_stores:
                POOL.wait(wtb.w)
                deferred_stores.pop(0)()
            wt, wtb = load_w(wb_in[:, C_KA: C_KA + 512], 16, ("in", 2))
            for j in range(2):
                if isF:
                    proj_fm(wt, wtb, j, (lambda kc: hTh[:, kc, 128:256], 128, [B_hTh[1]]),
                            lambda ps, pb, n, j=j: qknorm(ps, pb, n, 1, KaT[:, j, 0:128], [B_KaT[j]]))
                    proj_fm(wt, wtb, j, seg_c,
                            lambda ps, pb, n, j=j: qknorm(ps, pb, n, 1, KaT[:, j, 128:640], [B_KaT[j]]))
                else:
                    proj_fm(wt, wtb, j, (lambda kc: hTc[:, kc, 128:512], 384, B_hTc[1:4]),
                            lambda ps, pb, n, j=j: qknorm(ps, pb, n, 1, KaT[:, j, 256:640], [B_KaT[j]]))
                proj_fm(wt, wtb, j, (lambda kc: hTh[:, kc, 256:384], 128, [B_hTh[2]]),
                        lambda ps, pb, n, j=j: qknorm(ps, pb, n, 1, KaT[:, j, 640:768], [B_KaT[j]]))

            def tok_lhsT(t):
                if t < 2:
                    return (lambda kc: hTh[:, kc, t * 128:(t + 1) * 128]), B_hTh[t]
                if t >= 6:
                    return (lambda kc: hTh[:, kc, (t - 4) * 128:(t - 3) * 128]), B_hTh[t - 4]
                return (lambda kc: hTc[:, kc, (t - 2) * 128:(t - 1) * 128]), B_hTc[t - 2]

            flush(0)
            for vt in (range(6) if isF else range(2, 6)):
                lf, lb = tok_lhsT(vt + 1)
                ps, psbuf = psf_rot.next()
                for kc in range(16):
                    op(PE, lambda e, kc=kc, lf=lf, ps=ps, wt=wt: e.matmul(ps[:, 0:256], lhsT=lf(kc), rhs=wt[:, kc, 256:512],
                                                                  start=(kc == 0), stop=(kc == 15)),
                       reads=[wtb, lb], writes=[psbuf], signal=(kc == 15))
                op(ACT, lambda e, ps=ps, vt=vt: e.activation(out=Va[:, vt, :, 0:128],
                                                            in_=ps[:, 0:256].rearrange("p (h d) -> p h d", h=2),
                                                            func=AF.Copy),
                   reads=[psbuf], writes=[B_Va[vt]])
            for wi in range(2):
                wt, wtb = load_w(wb_in[:, C_QB + wi * 512: C_QB + (wi + 1) * 512], 16, ("in", 3 + wi))
                for j in range(4):
                    h = 8 + wi * 4 + j
                    proj_fm(wt, wtb, j, seg_c,
                            lambda ps, pb, n, h=h: qknorm(ps, pb, n, 2, QT[:, h, :], B_QT[h]))
            for wi in range(2):
                wt, wtb = load_w(wb_in[:, C_KB + wi * 512: C_KB + (wi + 1) * 512], 16, ("in", 5 + wi))
                for j in range(4):
                    h = wi * 4 + j
                    if isF:
                        proj_fm(wt, wtb, j, seg_b,
                                lambda ps, pb, n, h=h: qknorm(ps, pb, n, 3, KbT[:, h, 0:256], [B_KbT_lo[h]]))
                        proj_fm(wt, wtb, j, seg_c,
                                lambda ps, pb, n, h=h: qknorm(ps, pb, n, 3, KbT[:, h, 256:768], [B_KbT_lo[h], B_KbT_hi[h]]))
                    else:
                        proj_fm(wt, wtb, j, (lambda kc: hTc[:, kc, 256:512], 256, B_hTc[2:4]),
                                lambda ps, pb, n, h=h: qknorm(ps, pb, n, 3, KbT[:, h, 512:768], [B_KbT_hi[h]]))
                    proj_fm(wt, wtb, j, seg_a,
                            lambda ps, pb, n, h=h: qknorm(ps, pb, n, 3, KbT[:, h, 768:1024], [B_KbT_hi[h]]))
            flush(0)
            for wi in range(2):
                wt, wtb = load_w(wb_in[:, C_VB + wi * 512: C_VB + (wi + 1) * 512], 16, ("in", 7 + wi))
                for vt in (range(8) if isF else range(4, 8)):
                    lf, lb = tok_lhsT(vt)
                    ps, psbuf = psf_rot.next()
                    for kc in range(16):
                        op(PE, lambda e, kc=kc, lf=lf, ps=ps, wt=wt: e.matmul(ps[:, 0:512], lhsT=lf(kc), rhs=wt[:, kc, :],
                                                                      start=(kc == 0), stop=(kc == 15)),
                           reads=[wtb, lb], writes=[psbuf], signal=(kc == 15))
                    op(ACT, lambda e, ps=ps, vt=vt, wi=wi: e.activation(
                        out=Vb[:, vt, wi * 4:(wi + 1) * 4, 0:128],
                        in_=ps[:, 0:512].rearrange("p (h d) -> p h d", h=4), func=AF.Copy),
                       reads=[psbuf], writes=[B_Vb[vt][wi]])

            if dbg is not None and dbg[0] == "A" and s == dbg[2]:
                dma(POOL, ds_dbg, dbg_out[:, 0:16384], arena[:], reads=all_QT + B_KbT)
                dma(POOL, ds_dbg, dbg_out[:, 16384:16384 + 1536], KaT[:].rearrange("p a b -> p (a b)"), reads=B_KaT)
                dma(POOL, ds_dbg, dbg_out[:, 17920:17920 + 1560], Va[:].rearrange("p a b c -> p (a b c)"), reads=B_Va)
                dma(POOL, ds_dbg, dbg_out[:, 19480:19480 + 8320], Vb[:].rearrange("p a b c -> p (a b c)"),
                    reads=[b for x in B_Vb for b in x])

            if s > 0:
                for tt in range(4):
                    dma(POOL, ds_xr[tt], xres[:, tt, :], xin[r0 + (tt + 2) * 128: r0 + (tt + 3) * 128, :], writes=[B_xres[tt]])

            items = []
            for h in range(8):
                for qb in range(4):
                    items.append(("w", h, qb))
            for h in range(8):
                for qt in range(4):
                    items.append(("n", h, qt))
            st = {}

            def stage1(it):
                kind, h, q = it
                d = {}
                st[it] = d
                ei = nxt("expS")
                pi = rot2["PT"] % 3
                rot2["PT"] += 1
                d["ei"], d["pi"] = ei, pi
                tmp = expS[ei]
                if kind == "w":
                    g = h // 4
                    kbs = [q - 1, q, q + 1]
                    ps, psbuf = psf_rot.next()
                    for i, kb in enumerate(kbs):
                        op(PE, lambda e, i=i, kb=kb: e.matmul(ps[:, i * 128:(i + 1) * 128],
                                                              lhsT=KaT[:, g, (kb + 1) * 128:(kb + 2) * 128],
                                                              rhs=QT[:, h, q * 128:(q + 1) * 128], start=True, stop=True),
                           reads=[B_KaT[g], B_QT[h][q]], writes=[psbuf], signal=(i == 2))
                    op(DVE, lambda e: e.tensor_tensor(out=tmp[:, 0:384], in0=ps[:, 0:384],
                                                      in1=E_w[:, h, :, :].rearrange("p a b -> p (a b)"), op=ALU.add),
                       reads=[psbuf, B_const], writes=B_expS[ei])
                    if q == 0:
                        segs = [(0, 1, masks[:, mw0:mw0 + 1]), (1, 3, None)]
                    elif q == 3:
                        segs = [(0, 2, None), (2, 3, masks[:, mw0 + 1:mw0 + 2])]
                    else:
                        segs = [(0, 3, None)]
                    for a_, b_, bias in segs:
                        if bias is None:
                            op(ACT, lambda e, a_=a_, b_=b_: e.activation(
                                out=PT[pi][:, a_:b_, :].rearrange("p a b -> p (a b)"), in_=tmp[:, a_ * 128:b_ * 128], func=AF.Exp),
                               reads=B_expS[ei], writes=[B_PT[pi]])
                        else:
                            op(ACT, lambda e, a_=a_, b_=b_, bias=bias: e.activation(
                                out=PT[pi][:, a_:b_, :].rearrange("p a b -> p (a b)"), in_=tmp[:, a_ * 128:b_ * 128], func=AF.Exp,
                                bias=bias),
                               reads=B_expS[ei] + [B_const], writes=[B_PT[pi]])
                    d["nk"] = 3
                    d["v"] = [(Va[:, kb + 1, g, 0:129], B_Va[kb + 1]) for kb in kbs]
                else:
                    kts = NA_KT[q]
                    need_mask = {kt: True for kt in kts}
                    case = None if slot_cases is None else slot_cases[s]
                    if case is not None:
                        def _valid(kr, r):
                            rs = r - 4
                            if "S" in case:
                                rs = max(rs, 0)
                            if "E" in case:
                                rs = min(rs, 0)
                            return rs <= kr < rs + 8
                        keep = []
                        for kt in kts:
                            v = [_valid(2 * kt + a_, 2 * q + rr_) for a_ in range(2) for rr_ in range(2)]
                            if any(v):
                                keep.append(kt)
                                need_mask[kt] = not all(v)
                        assert keep == list(range(keep[0], keep[-1] + 1))
                        kts = keep
                    nk = len(kts)
                    banks = [psf_rot.next(), psf_rot.next()]
                    mr = Mrow[:, s * 8 + 2 * q: s * 8 + 2 * q + 2].unsqueeze(2).to_broadcast([128, 2, 64])
                    for i, kt in enumerate(kts):
                        ps, psbuf = banks[i // 4]
                        co = (i % 4) * 128
                        op(PE, lambda e, ps=ps, co=co, kt=kt: e.matmul(ps[:, co:co + 128],
                                                                      lhsT=KbT[:, h, (kt + 2) * 128:(kt + 3) * 128],
                                                                      rhs=QT[:, 8 + h, q * 128:(q + 1) * 128],
                                                                      start=True, stop=(not need_mask[kt])),
                           reads=[(B_KbT_lo if kt + 2 < 4 else B_KbT_hi)[h], B_QT[8 + h][q]], writes=[psbuf],
                           signal=((not need_mask[kt]) and (i % 4 == 3 or i == nk - 1)))
                        if not need_mask[kt]:
                            continue
                        op(PE, lambda e, ps=ps, co=co, kt=kt: e.matmul(ps[:, co:co + 128].rearrange("p (a b) -> p a b", a=2),
                                                                      lhsT=Lsel[:, kt + 2, :], rhs=mr,
                                                                      start=False, stop=True),
                           reads=[B_const], writes=[psbuf], signal=(i % 4 == 3 or i == nk - 1))
                    ur0 = 3 - q + kts[0]
                    for bi_ in range(2):
                        n_ = min(nk - bi_ * 4, 4)
                        if n_ <= 0:
                            continue
                        ps, psbuf = banks[bi_]
                        op(DVE, lambda e, ps=ps, bi_=bi_, n_=n_: e.tensor_tensor(
                            out=tmp[:, bi_ * 512: bi_ * 512 + n_ * 128], in0=ps[:, 0:n_ * 128],
                            in1=E_n[:, h, ur0 + bi_ * 4: ur0 + bi_ * 4 + n_, :].rearrange("p a b -> p (a b)"), op=ALU.add),
                           reads=[psbuf, B_const], writes=B_expS[ei])
                    op(ACT, lambda e: e.activation(out=PT[pi][:, 0:nk, :].rearrange("p a b -> p (a b)"),
                                                   in_=tmp[:, 0:nk * 128], func=AF.Exp),
                       reads=B_expS[ei], writes=[B_PT[pi]])
                    d["nk"] = nk
                    d["v"] = [(Vb[:, kt + 2, h, 0:129], B_Vb[kt + 2][h // 4]) for kt in kts]

            def stage2(it):
                kind, h, q = it
                d = st[it]
                pi = d["pi"]
                ps, psbuf = psf_rot.next()
                nk = d["nk"]
                for i in range(nk):
                    vap, vb = d["v"][i]
                    op(PE, lambda e, i=i, vap=vap: e.matmul(ps[:, 0:129], lhsT=PT[pi][:, i, :], rhs=vap,
                                                           start=(i == 0), stop=(i == nk - 1)),
                       reads=[B_PT[pi], vb], writes=[psbuf], signal=(i == nk - 1))
                sm, smb = small_col()
                if kind == "w":
                    op(DVE, lambda e: e.tensor_scalar(out=sm[:, 0:1], in0=ps[:, 128:129], scalar1=expsink[:, h:h + 1],
                                                      scalar2=None, op0=ALU.add),
                       reads=[psbuf, B_const], writes=[smb])
                    op(DVE, lambda e: e.reciprocal(out=sm[:, 1:2], in_=sm[:, 0:1]), reads=[smb], writes=[smb])
                else:
                    op(DVE, lambda e: e.reciprocal(out=sm[:, 1:2], in_=ps[:, 128:129]), reads=[psbuf], writes=[smb])
                oi = nxt("On")
                d["oi"] = oi
                op(DVE, lambda e: e.tensor_scalar(out=Onb[oi][:], in0=ps[:, 0:128], scalar1=sm[:, 1:2], scalar2=None,
                                                  op0=ALU.mult),
                   reads=[psbuf, smb], writes=[B_On[oi]])

            def stage3(it):
                kind, h, q = it
                d = st[it]
                oi = d["oi"]
                hh = h if kind == "w" else 8 + h
                pb, pbb = psb_rot.next()
                op(PE, lambda e: e.transpose(out=pb[:, 0:128], in_=Onb[oi][:], identity=ident[:]),
                   reads=[B_On[oi], B_const], writes=[pbb])
                op(ACT, lambda e: e.activation(out=QT[:, hh, q * 128:(q + 1) * 128], in_=pb[:, 0:128], func=AF.Copy),
                   reads=[pbb], writes=[B_QT[hh][q]])
                del st[it]

            n_it = len(items)
            for step in range(n_it + 3):
                if step < n_it:
                    stage1(items[step])
                if 0 <= step - 2 < n_it:
                    stage2(items[step - 2])
                if 0 <= step - 3 < n_it:
                    stage3(items[step - 3])

            if dbg is not None and dbg[0] == "B" and s == dbg[2]:
                dma(POOL, ds_dbg, dbg_out, arena[:, 0:8192], reads=all_QT)

            gt = gt_ap.rearrange("p (g j t) -> p g j t", g=2, j=4)
            for n in range(4):
                for gi, c0 in enumerate((C_GA, C_GB)):
                    wt, wtb = load_w(wb_in[:, c0 + n * 512: c0 + (n + 1) * 512], 16, ("in", 9 + 4 * gi + n))
                    for j in range(4):
                        ps, psbuf = psf_rot.next()
                        for kc in range(16):
                            op(PE, lambda e, kc=kc, ps=ps, j=j, wt=wt: e.matmul(ps[:, :], lhsT=wt[:, kc, j * 128:(j + 1) * 128],
                                                                              rhs=hTc[:, kc, :], start=(kc == 0),
                                                                              stop=(kc == 15)),
                               reads=[wtb] + B_hTc, writes=[psbuf], signal=(kc == 15))
                        gb_ = B_gt[gi * 4 + j]
                        gap = gt[:, gi, j, :]
                        op(ACT, lambda e, ps=ps, gap=gap: e.activation(out=gap, in_=ps[:, :], func=AF.Exp, scale=-1.0),
                           reads=[psbuf], writes=[gb_] + B_hTh)
                        op(DVE, lambda e, gap=gap: e.tensor_scalar(out=gap, in0=gap, scalar1=1.0, scalar2=None, op0=ALU.add),
                           reads=[gb_], writes=[gb_])
                        op(DVE, lambda e, gap=gap: e.reciprocal(out=gap, in_=gap), reads=[gb_], writes=[gb_])
                for bi, wsrc in enumerate((wb_bra, wb_brb)):
                    wt, wtb = load_w(wsrc[:, n * 512:(n + 1) * 512], 8, (("bra", "brb")[bi], n))
                    for j in range(4):
                        ps, psbuf = psf_rot.next()
                        for hh in range(8):
                            op(PE, lambda e, hh=hh, ps=ps, j=j, wt=wt, bi=bi: e.matmul(
                                ps[:, :], lhsT=wt[:, hh, j * 128:(j + 1) * 128], rhs=QT[:, bi * 8 + hh, :],
                                start=(hh == 0), stop=(hh == 7)),
                               reads=[wtb] + B_QT[bi * 8 + hh], writes=[psbuf],
                               signal=(hh == 7))
                        ga_b, gb_b = B_gt[j], B_gt[4 + j]
                        if bi == 0:
                            op(DVE, lambda e, ps=ps, j=j: e.tensor_tensor(out=gt[:, 0, j, :], in0=ps[:, :], in1=gt[:, 0, j, :],
                                                                        op=ALU.mult),
                               reads=[psbuf, ga_b], writes=[ga_b])
                        else:
                            cm = n * 4 + j
                            op(DVE, lambda e, ps=ps, j=j: e.tensor_tensor(out=gt[:, 1, j, :], in0=ps[:, :], in1=gt[:, 1, j, :],
                                                                        op=ALU.mult),
                               reads=[psbuf, gb_b], writes=[gb_b])
                            op(DVE, lambda e, j=j, cm=cm: e.tensor_tensor(out=merged_ap(cm), in0=gt[:, 0, j, :],
                                                                        in1=gt[:, 1, j, :], op=ALU.add),
                               reads=[ga_b, gb_b], writes=B_merged(cm))

            ctx2 = {}

            def mlp_rms2(tt):
                rms2(ctx2[tt], 16, lambda half: hTc[:, half * 8:(half + 1) * 8, tt * 128:(tt + 1) * 128], [B_hTc[tt]])

            for n in range(4):
                wt, wtb = load_w(wb_out[:, n * 512:(n + 1) * 512], 16, ("out", n))
                for tt in range(4):
                    ps, psbuf = psf_rot.next()
                    for kc in range(16):
                        op(PE, lambda e, kc=kc, ps=ps, tt=tt, wt=wt: e.matmul(ps[:, :], lhsT=merged_ap(kc)[:, tt * 128:(tt + 1) * 128],
                                                                            rhs=wt[:, kc, :], start=(kc == 0), stop=(kc == 15)),
                           reads=[wtb] + B_merged(kc), writes=[psbuf], signal=(kc == 15))
                    if n == 3 and tt >= 2:
                        mlp_rms2(tt - 2)
                    op(DVE, lambda e, ps=ps, tt=tt, n=n: e.tensor_tensor(out=xres[:, tt, n * 512:(n + 1) * 512], in0=ps[:, :],
                                                                       in1=xres[:, tt, n * 512:(n + 1) * 512], op=ALU.add),
                       reads=[psbuf, B_xres[tt]], writes=[B_xres[tt]])
                    if n == 3:
                        ctx2[tt] = rms1(xres[:, tt, :], B_xres[tt])

            if dbg is not None and dbg[0] == "C" and s == dbg[2]:
                dma(POOL, ds_dbg, dbg_out.rearrange("(t p) d -> p t d", p=128), xres[:], reads=B_xres)

            mlp_rms2(2)
            mlp_rms2(3)
            for fq in range(4):
                for i in range(4):
                    wt, wtb = load_w(wb_up[:, fq * 2048 + i * 512: fq * 2048 + (i + 1) * 512], 16, ("up", fq * 4 + i))
                    for j in range(4):
                        ps, psbuf = psf_rot.next()
                        for kc in range(16):
                            op(PE, lambda e, kc=kc, ps=ps, j=j, wt=wt: e.matmul(ps[:, :], lhsT=wt[:, kc, j * 128:(j + 1) * 128],
                                                                              rhs=hTc[:, kc, :], start=(kc == 0),
                                                                              stop=(kc == 15)),
                               reads=[wtb] + B_hTc, writes=[psbuf], signal=(kc == 15))
                        ri = nxt("relu")
                        op(ACT, lambda e, ps=ps, ri=ri: e.activation(out=relu_t[ri][:], in_=ps[:, :], func=AF.Relu),
                           reads=[psbuf], writes=[B_relu[ri]])
                        op(DVE, lambda e, ps=ps, ri=ri, cc=i * 4 + j: e.tensor_tensor(out=actT[:, cc, :], in0=ps[:, :],
                                                                                          in1=relu_t[ri][:], op=ALU.mult),
                           reads=[psbuf, B_relu[ri]], writes=B_QT[i * 4 + j])
                    if fq == 1 or (fq == 2 and i == 0):
                        pro_early()
                    if fq == 3 and i == 3:
                        pro_late()
                for n in range(4):
                    wt, wtb = load_w(wb_down[fq * 2048:(fq + 1) * 2048, n * 512:(n + 1) * 512], 16, ("down", fq * 4 + n))
                    for tt in range(4):
                        ps, psbuf = psf_rot.next()
                        for kc in range(16):
                            op(PE, lambda e, kc=kc, ps=ps, tt=tt, wt=wt: e.matmul(
                                ps[:, :], lhsT=actT[:, kc, tt * 128:(tt + 1) * 128], rhs=wt[:, kc, :],
                                start=(kc == 0), stop=(kc == 15)),
                               reads=[wtb] + B_QT[kc], writes=[psbuf], signal=(kc == 15))
                        op(DVE, lambda e, ps=ps, tt=tt, n=n: e.tensor_tensor(out=xres[:, tt, n * 512:(n + 1) * 512], in0=ps[:, :],
                                                                           in1=xres[:, tt, n * 512:(n + 1) * 512], op=ALU.add),
                           reads=[psbuf, B_xres[tt]], writes=[B_xres[tt]])
                    if fq == 3 and n < 3:
                        pro_late()
            def emit_stores(s=s):
                for tt in range(4):
                    dma(POOL, ds_st[tt], y[s * T + tt * 128: s * T + (tt + 1) * 128, :], xres[:, tt, :], reads=[B_xres[tt]])
            if s + 1 < nslots:
                deferred_stores.append(emit_stores)
            else:
                emit_stores()
            assert not early_steps and not late_steps

        for tt in range(4):
            POOL.wait((ds_st[tt], ds_st[tt].count))
        if dbg is not None:
            POOL.wait((ds_dbg, ds_dbg.count))

        block = es.enter_context(nc.Block())

        @block.tensor
        def _(e):
            for f in PE.ops:
                f(e)

        @block.scalar
        def _(e):
            for f in ACT.ops:
                f(e)

        @block.vector
        def _(e):
            for f in DVE.ops:
                f(e)

        @block.gpsimd
        def _(e):
            for f in POOL.ops:
                f(e)

        @block.sync
        def _(e):
            for f in SP.ops:
                f(e)

    return nc


def _t5_bucket(rel):
    nb = 16
    max_exact = 8
    ret = (rel > 0).astype(np.int32) * nb
    n = np.abs(rel).astype(np.int32)
    nf = np.maximum(n, max_exact).astype(np.float32)
    large = max_exact + (np.log(nf / max_exact) / np.log(128 / max_exact) * (nb - max_exact)).astype(np.int32)
    large = np.minimum(large, nb - 1)
    return ret + np.where(n < max_exact, n, large)


def _window_bias_layout(t5_bias):
    kl = np.arange(128)[:, None, None]
    dl = np.arange(-1, 2)[None, :, None]
    ql = np.arange(128)[None, None, :]
    rel = dl * 128 + kl - ql
    idx = _t5_bucket(rel)
    out = t5_bias[idx]
    out = np.where((np.abs(rel) <= 128)[..., None], out, np.float32(NEGM))
    return np.ascontiguousarray(out.transpose(0, 3, 1, 2)).reshape(128, -1).astype(np.float32)


def _na_bias_layout(rpb):
    a = (np.arange(128) // 64)[:, None, None, None]
    kc = (np.arange(128) % 64)[:, None, None, None]
    ur = np.arange(7)[None, :, None, None]
    rr = np.arange(2)[None, None, :, None]
    qc = np.arange(64)[None, None, None, :]
    j = 13 - 2 * ur + rr - a
    dr = 14 - j
    ok_r = (j >= 0) & (j <= 14)
    dcc = np.clip(kc - qc, -15, 15) + 15
    start_c = np.clip(qc - 8, 0, 64 - 16)
    ok_c = (kc >= start_c) & (kc < start_c + 16)
    drc = np.clip(dr, 0, 14)
    drc, dccb = np.broadcast_arrays(drc, dcc)
    g = rpb[:, drc, dccb]
    ok = np.broadcast_to(ok_r & ok_c, g.shape[1:])
    g = np.where(ok[None], g, np.float32(NEGM))
    return np.ascontiguousarray(g.transpose(1, 0, 2, 3, 4)).reshape(128, -1).astype(np.float32)


def _lsel_layout():
    L = np.zeros((128, 8, 128), np.float32)
    for kt in range(8):
        for a in range(2):
            L[2 * kt + a, kt, a * 64:(a + 1) * 64] = 1.0
    return L.reshape(128, -1)


def _mask_tables(cases):
    ns = len(cases)
    wm = np.zeros((128, ns, 2), np.float32)
    mrow = np.zeros((128, ns, 8), np.float32)
    for s, cs in enumerate(cases):
        if "S" in cs:
            wm[:, s, 0] = NEGM
        if "E" in cs:
            wm[:, s, 1] = NEGM
        for r in range(8):
            rs = r - 4
            if "S" in cs:
                rs = max(rs, 0)
            if "E" in cs:
                rs = min(rs, 0)
            for i in range(16):
                kr = i - 4
                mrow[i, s, r] = 0.0 if (rs <= kr < rs + 8) else NEGM
    return wm.reshape(128, -1).astype(np.float32), mrow.reshape(128, -1).astype(np.float32)


_PROG_CACHE = {}


def _get_prog(nslots, slot_rows, slot_types, slot_cases, dbg=None):
    key = (nslots, tuple(slot_rows), tuple(slot_types), tuple(slot_cases), None if dbg is None else (dbg[0], tuple(dbg[1]), dbg[2]))
    if key not in _PROG_CACHE:
        _PROG_CACHE[key] = build_program(nslots, list(slot_rows), list(slot_types), list(slot_cases), dbg)
    return _PROG_CACHE[key]


def _common_inputs(norm_mix_g, w_in, q_norm_a, k_norm_a, t5_bias, sink_a, q_norm_b, k_norm_b, rpb_b,
                   w_br_a, w_br_b, w_out, norm_mlp_g, w_up, w_down):
    f = lambda a: np.ascontiguousarray(np.asarray(a, dtype=np.float32))
    pvec = np.zeros((128, 36), np.float32)
    pvec[:, 0:16] = f(norm_mix_g).reshape(16, 128).T
    pvec[:, 16:32] = f(norm_mlp_g).reshape(16, 128).T
    pvec[:, 32] = f(q_norm_a).reshape(128)
    pvec[:, 33] = f(k_norm_a).reshape(128)
    pvec[:, 34] = f(q_norm_b).reshape(128)
    pvec[:, 35] = f(k_norm_b).reshape(128)
    return {
        "w_in": f(w_in).reshape(D, IN_W),
        "w_bra": f(w_br_a).reshape(1024, D),
        "w_brb": f(w_br_b).reshape(1024, D),
        "w_out": f(w_out).reshape(D, D),
        "w_up": f(w_up).reshape(D, DFF),
        "w_down": f(w_down).reshape(DFF, D),
        "pvec": pvec,
        "sinkrep": np.ascontiguousarray(np.broadcast_to(f(sink_a).reshape(1, 8), (128, 8))),
        "wbias": _window_bias_layout(f(t5_bias)),
        "nbias": _na_bias_layout(f(rpb_b).reshape(8, 15, 31)),
        "ident": np.eye(128, dtype=np.float32),
        "lsel": _lsel_layout(),
    }


def kernel(x_prompt, x_sample, norm_mix_g, w_in, q_norm_a, k_norm_a, t5_bias, sink_a, q_norm_b, k_norm_b,
           rpb_b, w_br_a, w_br_b, w_out, norm_mlp_g, w_up, w_down):
    x_prompt = np.asarray(x_prompt, dtype=np.float32)
    x_sample = np.asarray(x_sample, dtype=np.float32)
    common = _common_inputs(norm_mix_g, w_in, q_norm_a, k_norm_a, t5_bias, sink_a, q_norm_b, k_norm_b, rpb_b,
                            w_br_a, w_br_b, w_out, norm_mlp_g, w_up, w_down)
    nslots = 10
    slot_rows = [512 * s for s in range(8)] + [4608, 4608 + 512]
    in_maps = []
    for c in range(NCORES):
        xin = np.zeros((4608 + 1536, D), np.float32)
        xin[256:256 + 4096] = x_prompt[c]
        sq, half = c // 2, c % 2
        if half == 0:
            xin[4608 + 256: 4608 + 1536] = x_sample[sq, 0:1280]
            scases = ["S", "I"]
        else:
            xin[4608: 4608 + 1280] = x_sample[sq, 768:2048]
            scases = ["I", "E"]
        cases = ["S"] + ["I"] * 6 + ["E"] + scases
        m = dict(common)
        m["xin"] = xin
        m["masks"], m["mrow"] = _mask_tables(cases)
        in_maps.append(m)
    slot_types = ["F"] + ["C"] * 7 + ["F", "C"]
    slot_cases = ["S"] + ["I"] * 6 + ["E"] + [None, None]
    nc = _get_prog(nslots, slot_rows, slot_types, slot_cases)
    res = run_bass_kernel_spmd(nc, in_maps, core_ids=list(range(NCORES)))
    y_prompt = np.empty((8, 4096, D), np.float32)
    y_sample = np.empty((4, 2048, D), np.float32)
    for c in range(NCORES):
        yc = np.asarray(res.results[c]["y"], dtype=np.float32)
        y_prompt[c] = yc[0:4096]
        sq, half = c // 2, c % 2
        y_sample[sq, half * 1024:(half + 1) * 1024] = yc[4096:5120]
    return (y_prompt, y_sample)
```

```python
from contextlib import ExitStack

import numpy as np
import concourse.bass as bass
import concourse.mybir as mybir
from concourse.bass_utils import run_bass_kernel_spmd

F32 = mybir.dt.float32
BF16 = mybir.dt.bfloat16
AF = mybir.ActivationFunctionType
ALU = mybir.AluOpType

D = 2048
HD = 128
IN_W = 8704
DFF = 8192
EPS = 1e-6
NEGM = -30000.0
T = 512
HALO = 256
NCORES = 8

C_QA, C_KA, C_VA, C_QB, C_KB, C_VB, C_GA, C_GB = 0, 1024, 1280, 1536, 2560, 3584, 4608, 6656

NA_KT = {0: list(range(-2, 4)), 1: list(range(-1, 4)), 2: list(range(0, 5)), 3: list(range(0, 6))}


ATTACH_WAITS = True


class Sem:
    def __init__(self, h):
        self.h = h
        self.count = 0


class Buf:
    __slots__ = ("w", "r")

    def __init__(self):
        self.w = None
        self.r = {}


class Queue:
    def __init__(self, name, sem, is_pe=False, is_dma=False):
        self.name = name
        self.sem = sem
        self.ops = []
        self.waited = {}
        self.pending = []
        self.is_pe = is_pe
        self.is_dma = is_dma

    def wait(self, tok):
        s, v = tok
        if self.waited.get(s, 0) >= v:
            return
        self.waited[s] = v
        if self.is_dma or not ATTACH_WAITS:
            self.ops.append(lambda e, s=s, v=v: e.wait_ge(s.h, v))
        else:
            self.pending.append((s, v))

    def emit(self, fn, sem=None):
        pend = self.pending
        self.pending = []
        for s, v in pend[1:]:
            self.ops.append(lambda e, s=s, v=v: e.wait_ge(s.h, v))
        first = pend[0] if pend else None

        def thunk(e, fn=fn, first=first, sem=sem):
            ins = fn(e)
            if first is not None:
                ins = ins._wait_ge(first[0].h, first[1])
            if sem is not None:
                ins = ins.then_inc(sem.h, 1)
            return ins
        self.ops.append(thunk)


def _deps(q, reads, writes):
    deps = []
    for b in reads:
        if b.w is not None:
            deps.append(b.w)
    for b in writes:
        if b.w is not None:
            deps.append(b.w)
        for s, v in b.r.items():
            deps.append((s, v))
    for tok in deps:
        if tok[0] is q.sem and (q.is_pe or q.is_dma):
            continue
        q.wait(tok)


def _record(tok, reads, writes):
    s, v = tok
    for b in reads:
        if b.r.get(s, 0) < v:
            b.r[s] = v
    for b in writes:
        b.w = tok
        b.r = {}


def op(q, fn, reads=(), writes=(), signal=True):
    _deps(q, reads, writes)
    if signal:
        q.sem.count += 1
        tok = (q.sem, q.sem.count)
        q.emit(fn, q.sem)
    else:
        assert q.is_pe
        tok = (q.sem, q.sem.count + 1)
        q.emit(fn, None)
    _record(tok, reads, writes)
    return tok


def dma(q, dsem, out_ap, in_ap, reads=(), writes=(), **kw):
    _deps(q, reads, writes)
    dsem.count += 16
    tok = (dsem, dsem.count)
    q.ops.append(lambda e, o=out_ap, i=in_ap, s=dsem, kw=kw: e.dma_start(out=o, in_=i, **kw).then_inc(s.h, 16))
    _record(tok, reads, writes)
    return tok


class Rot:
    def __init__(self, items):
        self.items = items
        self.i = 0

    def next(self):
        it = self.items[self.i % len(self.items)]
        self.i += 1
        return it


def build_program(nslots, slot_rows, slot_types, slot_cases=None, dbg=None):
    nc = bass.Bass("TRN2", target_bir_lowering=False)
    NROWS_IN = max(slot_rows) + 1024

    def din(name, shape, dt=F32):
        return nc.dram_tensor(name, list(shape), dt, kind="ExternalInput").ap()

    xin = din("xin", [NROWS_IN, D])
    w_in = din("w_in", [D, IN_W])
    w_bra = din("w_bra", [1024, D])
    w_brb = din("w_brb", [1024, D])
    w_out = din("w_out", [D, D])
    w_up = din("w_up", [D, DFF])
    w_down = din("w_down", [DFF, D])
    pvec_d = din("pvec", [128, 36])
    sink_d = din("sinkrep", [128, 8])
    wbias_d = din("wbias", [128, 8 * 3 * 128])
    nbias_d = din("nbias", [128, 8 * 7 * 128])
    lsel_d = din("lsel", [128, 8 * 128])
    mrow_d = din("mrow", [128, nslots * 8])
    masks_d = din("masks", [128, nslots * 2])
    ident_d = din("ident", [128, 128])
    y = nc.dram_tensor("y", [nslots * T, D], F32, kind="ExternalOutput").ap()
    dbg_out = None
    if dbg is not None:
        dbg_out = nc.dram_tensor("dbg", list(dbg[1]), BF16 if dbg[0] in ("A", "B") else F32, kind="ExternalOutput").ap()

    wb_in = nc.dram_tensor("wb_in", [D, IN_W], BF16).ap()
    wb_bra = nc.dram_tensor("wb_bra", [1024, D], BF16).ap()
    wb_brb = nc.dram_tensor("wb_brb", [1024, D], BF16).ap()
    wb_out = nc.dram_tensor("wb_out", [D, D], BF16).ap()
    wb_up = nc.dram_tensor("wb_up", [D, DFF], BF16).ap()
    wb_down = nc.dram_tensor("wb_down", [DFF, D], BF16).ap()

    with ExitStack() as es:
        def sb(name, shape, dt):
            return es.enter_context(nc.sbuf_tensor(name, list(shape), dt))

        def mksem(name):
            return Sem(es.enter_context(nc.semaphore(name)))

        xres = sb("xres", [128, 4, D], F32)
        xstage = sb("xstage", [128, D], F32)
        hbf = [sb(f"hbf{i}", [128, D], BF16) for i in range(2)]
        hTc = sb("hTc", [128, 16, T], BF16)
        hTh = sb("hTh", [128, 16, 512], BF16)
        arena = sb("arena", [128, 16384], BF16)
        KaT = sb("KaT", [128, 2, 768], BF16)
        Va = sb("Va", [128, 6, 2, 130], BF16)
        Vb = sb("Vb", [128, 8, 8, 130], BF16)
        E_w = sb("E_w", [128, 8, 3, 128], BF16)
        E_n = sb("E_n", [128, 8, 7, 128], BF16)
        masks = sb("masks_sb", [128, nslots * 2], F32)
        Lsel = sb("Lsel", [128, 8, 128], BF16)
        Mrow = sb("Mrow", [128, nslots * 8], BF16)
        pvec = sb("pvecs", [128, 36], F32)
        qsc = sb("qsc", [128, 4], F32)
        expsink = sb("expsink", [128, 8], F32)
        ident = sb("ident_b", [128, 128], BF16)
        ones_b = sb("ones_b", [128, 128], BF16)
        sqh = [sb(f"sqh{i}", [128, 512], BF16) for i in range(2)]
        wbuf = [sb(f"wbuf{i}", [128, 16, 512], BF16) for i in range(2)]
        tmpf = sb("tmpf", [128, 2048], F32)
        sqb = [tmpf[:, 0:512], tmpf[:, 512:1024]]
        lnb = [tmpf[:, 1024:1536], tmpf[:, 1536:2048]]
        expS = [tmpf[:, 0:768], tmpf[:, 1024:1792]]
        PT = [sb(f"PT{i}", [128, 6, 128], BF16) for i in range(3)]
        Onb = [sb(f"On{i}", [128, 128], BF16) for i in range(2)]
        relu_t = sqb
        small = sb("small", [128, 64], F32)
        gt_ap = hTh[:].rearrange("p a b -> p (a b)").bitcast(F32)

        psf = [es.enter_context(nc.psum_tensor(f"psf{i}", [128, 512], F32)) for i in range(6)]
        psb = [es.enter_context(nc.psum_tensor(f"psb{i}", [128, 1024], BF16)) for i in range(2)]
        psf_rot = Rot([(psf[i], Buf()) for i in range(6)])
        psb_rot = Rot([(psb[i], Buf()) for i in range(2)])

        PE = Queue("pe", mksem("s_pe"), is_pe=True)
        ACT = Queue("act", mksem("s_act"))
        DVE = Queue("dve", mksem("s_dve"))
        POOL = Queue("pool", mksem("s_pool"), is_dma=True)
        SP = Queue("sp", mksem("s_sp"), is_dma=True)
        ds_w = [mksem(f"ds_w{i}") for i in range(2)]
        ds_xs = mksem("ds_xs")
        ds_xr = [mksem(f"ds_xr{i}") for i in range(4)]
        ds_st = [mksem(f"ds_st{i}") for i in range(4)]
        ds_c = [mksem(f"ds_c{i}") for i in range(8)]
        ds_dbg = mksem("ds_dbg")

        B_xres = [Buf() for _ in range(4)]
        B_xstage = Buf()
        B_hbf = [Buf(), Buf()]
        B_hTc = [Buf() for _ in range(4)]
        B_hTh = [Buf() for _ in range(4)]
        B_QT = [[Buf() for _ in range(4)] for _ in range(16)]
        B_KbT_lo = [Buf() for _ in range(8)]
        B_KbT_hi = [Buf() for _ in range(8)]
        B_KbT = B_KbT_lo + B_KbT_hi
        B_KaT = [Buf() for _ in range(2)]
        B_Va = [Buf() for _ in range(6)]
        B_Vb = [[Buf() for _ in range(2)] for _ in range(8)]
        B_wbuf = [Buf(), Buf()]
        B_sq = [Buf(), Buf()]
        B_sqh = [Buf(), Buf()]
        B_ln = [Buf(), Buf()]
        B_expS = [B_sq, B_ln]
        B_PT = [Buf(), Buf(), Buf()]
        B_On = [Buf(), Buf()]
        B_relu = B_sq
        B_small = [Buf() for _ in range(16)]
        B_const = Buf()
        B_gt = [Buf() for _ in range(8)]
        all_QT = [b for hb in B_QT for b in hb]

        QT = arena[:, 0:8192].rearrange("p (h t) -> p h t", h=16)
        KbT = arena[:, 8192:16384].rearrange("p (h t) -> p h t", h=8)
        actT = arena[:, 0:8192].rearrange("p (c t) -> p c t", c=16)
        Vb_lo_flat = Vb[:, 0:4, :, :].rearrange("p a b c -> p (a b c)")
        ds_sh = [mksem(f"ds_sh{i}") for i in range(4)]

        def merged_ap(c):
            if c < 8:
                return KbT[:, c, 0:512]
            return Vb_lo_flat[:, (c - 8) * 512:(c - 7) * 512]

        B_Vb_lo = [b_ for t_ in range(4) for b_ in B_Vb[t_]]

        def B_merged(c):
            return [B_KbT_lo[c]] if c < 8 else B_Vb_lo

        B_merged_all = B_KbT_lo + B_Vb_lo
        B_act = all_QT

        CAST = {}
        cast_jobs = []

        def cjob(key, dst, src):
            b = Buf()
            CAST[key] = b
            cast_jobs.append((key, dst, src, b))

        for i in range(9):
            cjob(("in", i), wb_in[:, i * 512:(i + 1) * 512], w_in[:, i * 512:(i + 1) * 512])
        for n in range(4):
            cjob(("in", 9 + n), wb_in[:, C_GA + n * 512: C_GA + (n + 1) * 512], w_in[:, C_GA + n * 512: C_GA + (n + 1) * 512])
            cjob(("in", 13 + n), wb_in[:, C_GB + n * 512: C_GB + (n + 1) * 512], w_in[:, C_GB + n * 512: C_GB + (n + 1) * 512])
            cjob(("bra", n), wb_bra[:, n * 512:(n + 1) * 512], w_bra[:, n * 512:(n + 1) * 512])
            cjob(("brb", n), wb_brb[:, n * 512:(n + 1) * 512], w_brb[:, n * 512:(n + 1) * 512])
        for n in range(4):
            cjob(("out", n), wb_out[:, n * 512:(n + 1) * 512], w_out[:, n * 512:(n + 1) * 512])
        for fq in range(4):
            for i in range(4):
                c0 = fq * 2048 + i * 512
                cjob(("up", fq * 4 + i), wb_up[:, c0:c0 + 512], w_up[:, c0:c0 + 512])
            for n in range(4):
                cjob(("down", fq * 4 + n), wb_down[fq * 2048:(fq + 1) * 2048, n * 512:(n + 1) * 512],
                     w_down[fq * 2048:(fq + 1) * 2048, n * 512:(n + 1) * 512])

        def issue_casts():
            toks = []
            for k, (key, dst, src, b) in enumerate(cast_jobs):
                if k >= 12:
                    POOL.wait(toks[k - 12])
                sm_ = mksem(f"ds_cast_{key[0]}{key[1]}")
                toks.append(dma(POOL, sm_, dst, src, writes=[b]))

        for tt in range(4):
            dma(POOL, ds_xr[tt], xres[:, tt, :], xin[slot_rows[0] + (tt + 2) * 128: slot_rows[0] + (tt + 3) * 128, :],
                writes=[B_xres[tt]])
        P_ = []

        def cb():
            b_ = Buf()
            P_.append(b_)
            return b_

        b_pvec, b_sink = cb(), cb()
        dma(SP, ds_c[0], pvec[:], pvec_d, writes=[b_pvec])
        dma(SP, ds_c[1], expsink[:], sink_d, writes=[b_sink])
        dma(SP, ds_c[2], masks[:], masks_d, writes=[cb()])
        dma(POOL, ds_c[3], ident[:], ident_d, writes=[cb()])
        dma(POOL, ds_c[4], E_w[:].rearrange("p a b c -> p (a b c)"), wbias_d, writes=[cb()], max_dma_last_dim=4096)
        dma(POOL, ds_c[5], E_n[:].rearrange("p a b c -> p (a b c)"), nbias_d, writes=[cb()], max_dma_last_dim=4096)
        dma(POOL, ds_c[6], Lsel[:].rearrange("p a b -> p (a b)"), lsel_d, writes=[cb()])
        dma(POOL, ds_c[7], Mrow[:], mrow_d, writes=[cb()])
        op(ACT, lambda e: e.activation(out=expsink[:], in_=expsink[:], func=AF.Exp), reads=[b_sink], writes=[b_sink])
        op(DVE, lambda e: e.memset(ones_b[:], 1.0), writes=[cb()])
        op(DVE, lambda e: e.memset(Va[:].rearrange("p a b c -> p (a b c)"), 1.0), writes=B_Va)
        op(DVE, lambda e: e.memset(Vb[:].rearrange("p a b c -> p (a b c)"), 1.0), writes=[b for x in B_Vb for b in x])
        sc = float(HD) ** -0.5
        b_qsc = cb()
        op(DVE, lambda e: e.tensor_scalar(out=qsc[:, 0:1], in0=pvec[:, 32:33], scalar1=sc, scalar2=None, op0=ALU.mult),
           reads=[b_pvec], writes=[b_qsc])
        op(DVE, lambda e: e.tensor_copy(out=qsc[:, 1:2], in_=pvec[:, 33:34]), reads=[b_pvec], writes=[b_qsc])
        op(DVE, lambda e: e.tensor_scalar(out=qsc[:, 2:3], in0=pvec[:, 34:35], scalar1=sc, scalar2=None, op0=ALU.mult),
           reads=[b_pvec], writes=[b_qsc])
        op(DVE, lambda e: e.tensor_copy(out=qsc[:, 3:4], in_=pvec[:, 35:36]), reads=[b_pvec], writes=[b_qsc])

        wctr = [0]

        def load_w(src_ap, kcs, cast_key):
            i = wctr[0] % 2
            wctr[0] += 1
            dma(SP, ds_w[i], wbuf[i][:, 0:kcs, :], src_ap.rearrange("(kc p) n -> p kc n", p=128),
                reads=[CAST[cast_key]], writes=[B_wbuf[i]])
            return wbuf[i], B_wbuf[i]

        sctr = [0]

        def small_col():
            i = sctr[0] % 16
            sctr[0] += 1
            return small[:, 4 * i:4 * i + 4], B_small[i]

        rot2 = {"sq": 0, "ln": 0, "expS": 0, "PT": 0, "On": 0, "relu": 0, "hbf": 0}

        def nxt(k):
            i = rot2[k] % 2
            rot2[k] += 1
            return i

        pend = []

        def flush(keep=0):
            while len(pend) > keep:
                pend.pop(0)()

        def rms1(x_ap, xbuf):
            hi = nxt("hbf")
            sm, smb = small_col()
            op(ACT, lambda e: e.activation(out=hbf[hi][:], in_=x_ap, func=AF.Square, accum_out=sm[:, 0:1]),
               reads=[xbuf], writes=[B_hbf[hi], smb])
            op(ACT, lambda e: e.activation(out=sm[:, 1:2], in_=sm[:, 0:1], func=AF.Ln, scale=1.0 / D, bias=sm[:, 3:4]),
               reads=[smb], writes=[smb])
            op(ACT, lambda e: e.activation(out=sm[:, 2:3], in_=sm[:, 1:2], func=AF.Exp, scale=-0.5),
               reads=[smb], writes=[smb])
            op(DVE, lambda e: e.tensor_scalar(out=hbf[hi][:], in0=x_ap, scalar1=sm[:, 2:3], scalar2=None, op0=ALU.mult),
               reads=[xbuf, smb], writes=[B_hbf[hi]])
            return hi

        def rms2(hi, gcol0, dst_ap_fn, dst_bufs):
            for half in range(2):
                pb, pbb = psb_rot.next()
                for cc in range(8):
                    ch = half * 8 + cc
                    op(PE, lambda e, pb=pb, cc=cc, ch=ch: e.transpose(out=pb[:, cc * 128:(cc + 1) * 128],
                                                                     in_=hbf[hi][:, ch * 128:(ch + 1) * 128],
                                                                     identity=ident[:]),
                       reads=[B_hbf[hi], B_const], writes=[pbb], signal=(cc == 7))
                g_b = pvec[:, gcol0 + half * 8: gcol0 + half * 8 + 8].unsqueeze(2).to_broadcast([128, 8, 128])
                op(DVE, lambda e, pb=pb, half=half, g_b=g_b: e.tensor_tensor(
                    out=dst_ap_fn(half), in0=pb[:].rearrange("p (c t) -> p c t", c=8), in1=g_b, op=ALU.mult),
                   reads=[pbb, B_const], writes=dst_bufs)

        def prologue_steps(sn):
            r0n = slot_rows[sn]
            tiles = (0, 1, 6, 7, 2, 3, 4, 5) if slot_types[sn] == "F" else (6, 7, 2, 3, 4, 5)
            nt = len(tiles)
            p1s, p2s = [], []
            for t in tiles:
                def p1(t=t):
                    if sn == 0 and 2 <= t <= 5:
                        return rms1(xres[:, t - 2, :], B_xres[t - 2])
                    dma(SP, ds_xs, xstage[:], xin[r0n + t * 128: r0n + (t + 1) * 128, :], writes=[B_xstage])
                    return rms1(xstage[:], B_xstage)

                def p2(hi, t=t):
                    if 2 <= t <= 5:
                        tt = t - 2
                        rms2(hi, 0, lambda half: hTc[:, half * 8:(half + 1) * 8, tt * 128:(tt + 1) * 128], [B_hTc[tt]])
                    else:
                        hh = t if t < 2 else t - 4
                        rms2(hi, 0, lambda half: hTh[:, half * 8:(half + 1) * 8, hh * 128:(hh + 1) * 128],
                             [B_hTh[hh]] + B_gt)
                p1s.append(p1)
                p2s.append(p2)
            ctx = {}
            steps = []
            for k in range(nt + 1):
                def step(k=k):
                    if 1 <= k <= nt:
                        p2s[k - 1](ctx[k - 1])
                    if k <= nt - 1:
                        ctx[k] = p1s[k]()
                steps.append(step)
            return steps

        def qknorm(ps, psbuf, n, gcol, out_ap, out_bufs):
            si = nxt("sq")
            li = nxt("ln")
            op(ACT, lambda e: e.activation(out=sqh[si][:, 0:n], in_=ps[:, 0:n], func=AF.Square),
               reads=[psbuf], writes=[B_sqh[si]])
            ps2, ps2b = psf_rot.next()

            def pe_part():
                op(PE, lambda e: e.matmul(ps2[:, 0:n], lhsT=ones_b[:], rhs=sqh[si][:, 0:n], start=True, stop=True),
                   reads=[B_sqh[si], B_const], writes=[ps2b])
                op(ACT, lambda e: e.activation(out=lnb[li][:, 0:n], in_=ps2[:, 0:n], func=AF.Ln, scale=1.0 / HD,
                                               bias=small[:, 63:64]),
                   reads=[ps2b, B_const], writes=[B_ln[li]])
                op(ACT, lambda e: e.activation(out=lnb[li][:, 0:n], in_=lnb[li][:, 0:n], func=AF.Exp, scale=-0.5),
                   reads=[B_ln[li]], writes=[B_ln[li]])
                op(DVE, lambda e: e.scalar_tensor_tensor(out=out_ap, in0=ps[:, 0:n], scalar=qsc[:, gcol:gcol + 1],
                                                          in1=lnb[li][:, 0:n], op0=ALU.mult, op1=ALU.mult),
                   reads=[psbuf, B_ln[li], B_const], writes=out_bufs)
            pend.append(pe_part)

        op(DVE, lambda e: e.memset(small[:], EPS), writes=B_small)
        op(DVE, lambda e: e.memset(relu_t[0][:, 0:1], 0.0), reads=P_, writes=[B_const, B_sq[0]])

        deferred_stores = []
        for s in range(nslots):
            r0 = slot_rows[s]
            mw0 = s * 2
            mn0 = nslots * 2 + s * 64

            if s == 0:
                for stp in prologue_steps(0):
                    stp()
                issue_casts()
            nxt_steps = prologue_steps(s + 1) if s + 1 < nslots else []
            early_steps = nxt_steps[:-4]
            late_steps = nxt_steps[-4:]
            assert len(early_steps) <= 5

            def pro_early():
                if early_steps:
                    early_steps.pop(0)()

            def pro_late():
                if late_steps:
                    late_steps.pop(0)()

            seg_c = (lambda kc: hTc[:, kc, :], 512, B_hTc)
            seg_b = (lambda kc: hTh[:, kc, 0:256], 256, B_hTh[0:2])
            seg_a = (lambda kc: hTh[:, kc, 256:512], 256, B_hTh[2:4])

            def proj_fm(wt, wtb, j, seg, then):
                rhs_fn, n, hb = seg
                ps, psbuf = psf_rot.next()
                for kc in range(16):
                    op(PE, lambda e, kc=kc: e.matmul(ps[:, 0:n], lhsT=wt[:, kc, j * 128:(j + 1) * 128], rhs=rhs_fn(kc),
                                                     start=(kc == 0), stop=(kc == 15)),
                       reads=[wtb] + list(hb), writes=[psbuf], signal=(kc == 15))
                flush(0)
                then(ps, psbuf, n)

            for wi in range(2):
                wt, wtb = load_w(wb_in[:, C_QA + wi * 512: C_QA + (wi + 1) * 512], 16, ("in", wi))
                for j in range(4):
                    h = wi * 4 + j
                    proj_fm(wt, wtb, j, seg_c,
                            lambda ps, pb, n, h=h: qknorm(ps, pb, n, 0, QT[:, h, :], B_QT[h]))
            isF = slot_types[s] == "F"
            if not isF:
                assert slot_rows[s] == slot_rows[s - 1] + 512
                dma(SP, ds_sh[0], KbT[:, :, 0:512], KbT[:, :, 512:1024], reads=B_KbT_hi, writes=B_KbT_lo)
                dma(SP, ds_sh[1], Vb[:, 0:4, :, :], Vb[:, 4:8, :, :], reads=[b_ for t_ in range(4, 8) for b_ in B_Vb[t_]],
                    writes=B_Vb_lo)
                dma(SP, ds_sh[2], KaT[:, :, 0:256], KaT[:, :, 512:768], reads=B_KaT, writes=B_KaT)
                dma(SP, ds_sh[3], Va[:, 0:2, :, :], Va[:, 4:6, :, :], reads=B_Va[4:6], writes=B_Va[0:2])
            else:
                op(DVE, lambda e: e.memset(Vb[:, 0:4, :, 128:130], 1.0), writes=B_Vb_lo)
            wt, wtb = load_w(wb_in[:, C_KA: C_KA + 512], 16, ("in", 2))
            for j in range(2):
                if isF:
                    proj_fm(wt, wtb, j, (lambda kc: hTh[:, kc, 128:256], 128, [B_hTh[1]]),
                            lambda ps, pb, n, j=j: qknorm(ps, pb, n, 1, KaT[:, j, 0:128], [B_KaT[j]]))
                    proj_fm(wt, wtb, j, seg_c,
                            lambda ps, pb, n, j=j: qknorm(ps, pb, n, 1, KaT[:, j, 128:640], [B_KaT[j]]))
                else:
                    proj_fm(wt, wtb, j, (lambda kc: hTc[:, kc, 128:512], 384, B_hTc[1:4]),
                            lambda ps, pb, n, j=j: qknorm(ps, pb, n, 1, KaT[:, j, 256:640], [B_KaT[j]]))
                proj_fm(wt, wtb, j, (lambda kc: hTh[:, kc, 256:384], 128, [B_hTh[2]]),
                        lambda ps, pb, n, j=j: qknorm(ps, pb, n, 1, KaT[:, j, 640:768], [B_KaT[j]]))

            def tok_lhsT(t):
                if t < 2:
                    return (lambda kc: hTh[:, kc, t * 128:(t + 1) * 128]), B_hTh[t]
                if t >= 6:
                    return (lambda kc: hTh[:, kc, (t - 4) * 128:(t - 3) * 128]), B_hTh[t - 4]
                return (lambda kc: hTc[:, kc, (t - 2) * 128:(t - 1) * 128]), B_hTc[t - 2]

            flush(0)
            for vt in (range(6) if isF else range(2, 6)):
                lf, lb = tok_lhsT(vt + 1)
                ps, psbuf = psf_rot.next()
                for kc in range(16):
                    op(PE, lambda e, kc=kc, lf=lf, ps=ps, wt=wt: e.matmul(ps[:, 0:256], lhsT=lf(kc), rhs=wt[:, kc, 256:512],
                                                                  start=(kc == 0), stop=(kc == 15)),
                       reads=[wtb, lb], writes=[psbuf], signal=(kc == 15))
                op(ACT, lambda e, ps=ps, vt=vt: e.activation(out=Va[:, vt, :, 0:128],
                                                            in_=ps[:, 0:256].rearrange("p (h d) -> p h d", h=2),
                                                            func=AF.Copy),
                   reads=[psbuf], writes=[B_Va[vt]])
            for wi in range(2):
                wt, wtb = load_w(wb_in[:, C_QB + wi * 512: C_QB + (wi + 1) * 512], 16, ("in", 3 + wi))
                for j in range(4):
                    h = 8 + wi * 4 + j
                    proj_fm(wt, wtb, j, seg_c,
                            lambda ps, pb, n, h=h: qknorm(ps, pb, n, 2, QT[:, h, :], B_QT[h]))
            for wi in range(2):
                wt, wtb = load_w(wb_in[:, C_KB + wi * 512: C_KB + (wi + 1) * 512], 16, ("in", 5 + wi))
                for j in range(4):
                    h = wi * 4 + j
                    if isF:
                        proj_fm(wt, wtb, j, seg_b,
                                lambda ps, pb, n, h=h: qknorm(ps, pb, n, 3, KbT[:, h, 0:256], [B_KbT_lo[h]]))
                        proj_fm(wt, wtb, j, seg_c,
                                lambda ps, pb, n, h=h: qknorm(ps, pb, n, 3, KbT[:, h, 256:768], [B_KbT_lo[h], B_KbT_hi[h]]))
                    else:
                        proj_fm(wt, wtb, j, (lambda kc: hTc[:, kc, 256:512], 256, B_hTc[2:4]),
                                lambda ps, pb, n, h=h: qknorm(ps, pb, n, 3, KbT[:, h, 512:768], [B_KbT_hi[h]]))
                    proj_fm(wt, wtb, j, seg_a,
                            lambda ps, pb, n, h=h: qknorm(ps, pb, n, 3, KbT[:, h, 768:1024], [B_KbT_hi[h]]))
            flush(0)
            for wi in range(2):
                wt, wtb = load_w(wb_in[:, C_VB + wi * 512: C_VB + (wi + 1) * 512], 16, ("in", 7 + wi))
                for vt in (range(8) if isF else range(4, 8)):
                    lf, lb = tok_lhsT(vt)
                    ps, psbuf = psf_rot.next()
                    for kc in range(16):
                        op(PE, lambda e, kc=kc, lf=lf, ps=ps, wt=wt: e.matmul(ps[:, 0:512], lhsT=lf(kc), rhs=wt[:, kc, :],
                                                                      start=(kc == 0), stop=(kc == 15)),
                           reads=[wtb, lb], writes=[psbuf], signal=(kc == 15))
                    op(ACT, lambda e, ps=ps, vt=vt, wi=wi: e.activation(
                        out=Vb[:, vt, wi * 4:(wi + 1) * 4, 0:128],
                        in_=ps[:, 0:512].rearrange("p (h d) -> p h d", h=4), func=AF.Copy),
                       reads=[psbuf], writes=[B_Vb[vt][wi]])

            if dbg is not None and dbg[0] == "A" and s == dbg[2]:
                dma(POOL, ds_dbg, dbg_out[:, 0:16384], arena[:], reads=all_QT + B_KbT)
                dma(POOL, ds_dbg, dbg_out[:, 16384:16384 + 1536], KaT[:].rearrange("p a b -> p (a b)"), reads=B_KaT)
                dma(POOL, ds_dbg, dbg_out[:, 17920:17920 + 1560], Va[:].rearrange("p a b c -> p (a b c)"), reads=B_Va)
                dma(POOL, ds_dbg, dbg_out[:, 19480:19480 + 8320], Vb[:].rearrange("p a b c -> p (a b c)"),
                    reads=[b for x in B_Vb for b in x])

            if deferred_stores:
                POOL.wait(wtb.w)
                deferred_stores.pop(0)()
            if s > 0:
                for tt in range(4):
                    dma(POOL, ds_xr[tt], xres[:, tt, :], xin[r0 + (tt + 2) * 128: r0 + (tt + 3) * 128, :], writes=[B_xres[tt]])

            items = []
            for h in range(8):
                for qb in range(4):
                    items.append(("w", h, qb))
            for h in range(8):
                for qt in range(4):
                    items.append(("n", h, qt))
            st = {}

            def stage1(it):
                kind, h, q = it
                d = {}
                st[it] = d
                ei = nxt("expS")
                pi = rot2["PT"] % 3
                rot2["PT"] += 1
                d["ei"], d["pi"] = ei, pi
                tmp = expS[ei]
                if kind == "w":
                    g = h // 4
                    kbs = [q - 1, q, q + 1]
                    ps, psbuf = psf_rot.next()
                    for i, kb in enumerate(kbs):
                        op(PE, lambda e, i=i, kb=kb: e.matmul(ps[:, i * 128:(i + 1) * 128],
                                                              lhsT=KaT[:, g, (kb + 1) * 128:(kb + 2) * 128],
                                                              rhs=QT[:, h, q * 128:(q + 1) * 128], start=True, stop=True),
                           reads=[B_KaT[g], B_QT[h][q]], writes=[psbuf], signal=(i == 2))
                    op(DVE, lambda e: e.tensor_tensor(out=tmp[:, 0:384], in0=ps[:, 0:384],
                                                      in1=E_w[:, h, :, :].rearrange("p a b -> p (a b)"), op=ALU.add),
                       reads=[psbuf, B_const], writes=B_expS[ei])
                    if q == 0:
                        segs = [(0, 1, masks[:, mw0:mw0 + 1]), (1, 3, None)]
                    elif q == 3:
                        segs = [(0, 2, None), (2, 3, masks[:, mw0 + 1:mw0 + 2])]
                    else:
                        segs = [(0, 3, None)]
                    for a_, b_, bias in segs:
                        if bias is None:
                            op(ACT, lambda e, a_=a_, b_=b_: e.activation(
                                out=PT[pi][:, a_:b_, :].rearrange("p a b -> p (a b)"), in_=tmp[:, a_ * 128:b_ * 128], func=AF.Exp),
                               reads=B_expS[ei], writes=[B_PT[pi]])
                        else:
                            op(ACT, lambda e, a_=a_, b_=b_, bias=bias: e.activation(
                                out=PT[pi][:, a_:b_, :].rearrange("p a b -> p (a b)"), in_=tmp[:, a_ * 128:b_ * 128], func=AF.Exp,
                                bias=bias),
                               reads=B_expS[ei] + [B_const], writes=[B_PT[pi]])
                    d["nk"] = 3
                    d["v"] = [(Va[:, kb + 1, g, 0:129], B_Va[kb + 1]) for kb in kbs]
                else:
                    kts = NA_KT[q]
                    need_mask = {kt: True for kt in kts}
                    case = None if slot_cases is None else slot_cases[s]
                    if case is not None:
                        def _valid(kr, r):
                            rs = r - 4
                            if "S" in case:
                                rs = max(rs, 0)
                            if "E" in case:
                                rs = min(rs, 0)
                            return rs <= kr < rs + 8
                        keep = []
                        for kt in kts:
                            v = [_valid(2 * kt + a_, 2 * q + rr_) for a_ in range(2) for rr_ in range(2)]
                            if any(v):
                                keep.append(kt)
                                need_mask[kt] = not all(v)
                        assert keep == list(range(keep[0], keep[-1] + 1))
                        kts = keep
                    nk = len(kts)
                    banks = [psf_rot.next(), psf_rot.next()]
                    mr = Mrow[:, s * 8 + 2 * q: s * 8 + 2 * q + 2].unsqueeze(2).to_broadcast([128, 2, 64])
                    for i, kt in enumerate(kts):
                        ps, psbuf = banks[i // 4]
                        co = (i % 4) * 128
                        op(PE, lambda e, ps=ps, co=co, kt=kt: e.matmul(ps[:, co:co + 128],
                                                                      lhsT=KbT[:, h, (kt + 2) * 128:(kt + 3) * 128],
                                                                      rhs=QT[:, 8 + h, q * 128:(q + 1) * 128],
                                                                      start=True, stop=(not need_mask[kt])),
                           reads=[(B_KbT_lo if kt + 2 < 4 else B_KbT_hi)[h], B_QT[8 + h][q]], writes=[psbuf],
                           signal=((not need_mask[kt]) and (i % 4 == 3 or i == nk - 1)))
                        if not need_mask[kt]:
                            continue
                        op(PE, lambda e, ps=ps, co=co, kt=kt: e.matmul(ps[:, co:co + 128].rearrange("p (a b) -> p a b", a=2),
                                                                      lhsT=Lsel[:, kt + 2, :], rhs=mr,
                                                                      start=False, stop=True),
                           reads=[B_const], writes=[psbuf], signal=(i % 4 == 3 or i == nk - 1))
                    ur0 = 3 - q + kts[0]
                    for bi_ in range(2):
                        n_ = min(nk - bi_ * 4, 4)
                        if n_ <= 0:
                            continue
                        ps, psbuf = banks[bi_]
                        op(DVE, lambda e, ps=ps, bi_=bi_, n_=n_: e.tensor_tensor(
                            out=tmp[:, bi_ * 512: bi_ * 512 + n_ * 128], in0=ps[:, 0:n_ * 128],
                            in1=E_n[:, h, ur0 + bi_ * 4: ur0 + bi_ * 4 + n_, :].rearrange("p a b -> p (a b)"), op=ALU.add),
                           reads=[psbuf, B_const], writes=B_expS[ei])
                    op(ACT, lambda e: e.activation(out=PT[pi][:, 0:nk, :].rearrange("p a b -> p (a b)"),
                                                   in_=tmp[:, 0:nk * 128], func=AF.Exp),
                       reads=B_expS[ei], writes=[B_PT[pi]])
                    d["nk"] = nk
                    d["v"] = [(Vb[:, kt + 2, h, 0:129], B_Vb[kt + 2][h // 4]) for kt in kts]

            def stage2(it):
                kind, h, q = it
                d = st[it]
                pi = d["pi"]
                ps, psbuf = psf_rot.next()
                nk = d["nk"]
                for i in range(nk):
                    vap, vb = d["v"][i]
                    op(PE, lambda e, i=i, vap=vap: e.matmul(ps[:, 0:129], lhsT=PT[pi][:, i, :], rhs=vap,
                                                           start=(i == 0), stop=(i == nk - 1)),
                       reads=[B_PT[pi], vb], writes=[psbuf], signal=(i == nk - 1))
                sm, smb = small_col()
                if kind == "w":
                    op(DVE, lambda e: e.tensor_scalar(out=sm[:, 0:1], in0=ps[:, 128:129], scalar1=expsink[:, h:h + 1],
                                                      scalar2=None, op0=ALU.add),
                       reads=[psbuf, B_const], writes=[smb])
                    op(DVE, lambda e: e.reciprocal(out=sm[:, 1:2], in_=sm[:, 0:1]), reads=[smb], writes=[smb])
                else:
                    op(DVE, lambda e: e.reciprocal(out=sm[:, 1:2], in_=ps[:, 128:129]), reads=[psbuf], writes=[smb])
                oi = nxt("On")
                d["oi"] = oi
                op(DVE, lambda e: e.tensor_scalar(out=Onb[oi][:], in0=ps[:, 0:128], scalar1=sm[:, 1:2], scalar2=None,
                                                  op0=ALU.mult),
                   reads=[psbuf, smb], writes=[B_On[oi]])

            def stage3(it):
                kind, h, q = it
                d = st[it]
                oi = d["oi"]
                hh = h if kind == "w" else 8 + h
                pb, pbb = psb_rot.next()
                op(PE, lambda e: e.transpose(out=pb[:, 0:128], in_=Onb[oi][:], identity=ident[:]),
                   reads=[B_On[oi], B_const], writes=[pbb])
                op(ACT, lambda e: e.activation(out=QT[:, hh, q * 128:(q + 1) * 128], in_=pb[:, 0:128], func=AF.Copy),
                   reads=[pbb], writes=[B_QT[hh][q]])
                del st[it]

            n_it = len(items)
            for step in range(n_it + 3):
                if step < n_it:
                    stage1(items[step])
                if 0 <= step - 2 < n_it:
                    stage2(items[step - 2])
                if 0 <= step - 3 < n_it:
                    stage3(items[step - 3])

            if dbg is not None and dbg[0] == "B" and s == dbg[2]:
                dma(POOL, ds_dbg, dbg_out, arena[:, 0:8192], reads=all_QT)

            gt = gt_ap.rearrange("p (g j t) -> p g j t", g=2, j=4)
            for n in range(4):
                for gi, c0 in enumerate((C_GA, C_GB)):
                    wt, wtb = load_w(wb_in[:, c0 + n * 512: c0 + (n + 1) * 512], 16, ("in", 9 + 4 * gi + n))
                    for j in range(4):
                        ps, psbuf = psf_rot.next()
                        for kc in range(16):
                            op(PE, lambda e, kc=kc, ps=ps, j=j, wt=wt: e.matmul(ps[:, :], lhsT=wt[:, kc, j * 128:(j + 1) * 128],
                                                                              rhs=hTc[:, kc, :], start=(kc == 0),
                                                                              stop=(kc == 15)),
                               reads=[wtb] + B_hTc, writes=[psbuf], signal=(kc == 15))
                        gb_ = B_gt[gi * 4 + j]
                        gap = gt[:, gi, j, :]
                        op(ACT, lambda e, ps=ps, gap=gap: e.activation(out=gap, in_=ps[:, :], func=AF.Exp, scale=-1.0),
                           reads=[psbuf], writes=[gb_] + B_hTh)
                        op(DVE, lambda e, gap=gap: e.tensor_scalar(out=gap, in0=gap, scalar1=1.0, scalar2=None, op0=ALU.add),
                           reads=[gb_], writes=[gb_])
                        op(DVE, lambda e, gap=gap: e.reciprocal(out=gap, in_=gap), reads=[gb_], writes=[gb_])
                for bi, wsrc in enumerate((wb_bra, wb_brb)):
                    wt, wtb = load_w(wsrc[:, n * 512:(n + 1) * 512], 8, (("bra", "brb")[bi], n))
                    for j in range(4):
                        ps, psbuf = psf_rot.next()
                        for hh in range(8):
                            op(PE, lambda e, hh=hh, ps=ps, j=j, wt=wt, bi=bi: e.matmul(
                                ps[:, :], lhsT=wt[:, hh, j * 128:(j + 1) * 128], rhs=QT[:, bi * 8 + hh, :],
                                start=(hh == 0), stop=(hh == 7)),
                               reads=[wtb] + B_QT[bi * 8 + hh], writes=[psbuf],
                               signal=(hh == 7))
                        ga_b, gb_b = B_gt[j], B_gt[4 + j]
                        if bi == 0:
                            op(DVE, lambda e, ps=ps, j=j: e.tensor_tensor(out=gt[:, 0, j, :], in0=ps[:, :], in1=gt[:, 0, j, :],
                                                                        op=ALU.mult),
                               reads=[psbuf, ga_b], writes=[ga_b])
                        else:
                            cm = n * 4 + j
                            op(DVE, lambda e, ps=ps, j=j: e.tensor_tensor(out=gt[:, 1, j, :], in0=ps[:, :], in1=gt[:, 1, j, :],
                                                                        op=ALU.mult),
                               reads=[psbuf, gb_b], writes=[gb_b])
                            op(DVE, lambda e, j=j, cm=cm: e.tensor_tensor(out=merged_ap(cm), in0=gt[:, 0, j, :],
                                                                        in1=gt[:, 1, j, :], op=ALU.add),
                               reads=[ga_b, gb_b], writes=B_merged(cm))

            ctx2 = {}

            def mlp_rms2(tt):
                rms2(ctx2[tt], 16, lambda half: hTc[:, half * 8:(half + 1) * 8, tt * 128:(tt + 1) * 128], [B_hTc[tt]])

            for n in range(4):
                wt, wtb = load_w(wb_out[:, n * 512:(n + 1) * 512], 16, ("out", n))
                for tt in range(4):
                    ps, psbuf = psf_rot.next()
                    for kc in range(16):
                        op(PE, lambda e, kc=kc, ps=ps, tt=tt, wt=wt: e.matmul(ps[:, :], lhsT=merged_ap(kc)[:, tt * 128:(tt + 1) * 128],
                                                                            rhs=wt[:, kc, :], start=(kc == 0), stop=(kc == 15)),
                           reads=[wtb] + B_merged(kc), writes=[psbuf], signal=(kc == 15))
                    if n == 3 and tt >= 2:
                        mlp_rms2(tt - 2)
                    op(DVE, lambda e, ps=ps, tt=tt, n=n: e.tensor_tensor(out=xres[:, tt, n * 512:(n + 1) * 512], in0=ps[:, :],
                                                                       in1=xres[:, tt, n * 512:(n + 1) * 512], op=ALU.add),
                       reads=[psbuf, B_xres[tt]], writes=[B_xres[tt]])
                    if n == 3:
                        ctx2[tt] = rms1(xres[:, tt, :], B_xres[tt])

            if dbg is not None and dbg[0] == "C" and s == dbg[2]:
                dma(POOL, ds_dbg, dbg_out.rearrange("(t p) d -> p t d", p=128), xres[:], reads=B_xres)

            mlp_rms2(2)
            mlp_rms2(3)
            for fq in range(4):
                for i in range(4):
                    wt, wtb = load_w(wb_up[:, fq * 2048 + i * 512: fq * 2048 + (i + 1) * 512], 16, ("up", fq * 4 + i))
                    for j in range(4):
                        ps, psbuf = psf_rot.next()
                        for kc in range(16):
                            op(PE, lambda e, kc=kc, ps=ps, j=j, wt=wt: e.matmul(ps[:, :], lhsT=wt[:, kc, j * 128:(j + 1) * 128],
                                                                              rhs=hTc[:, kc, :], start=(kc == 0),
                                                                              stop=(kc == 15)),
                               reads=[wtb] + B_hTc, writes=[psbuf], signal=(kc == 15))
                        ri = nxt("relu")
                        op(ACT, lambda e, ps=ps, ri=ri: e.activation(out=relu_t[ri][:], in_=ps[:, :], func=AF.Relu),
                           reads=[psbuf], writes=[B_relu[ri]])
                        op(DVE, lambda e, ps=ps, ri=ri, cc=i * 4 + j: e.tensor_tensor(out=actT[:, cc, :], in0=ps[:, :],
                                                                                          in1=relu_t[ri][:], op=ALU.mult),
                           reads=[psbuf, B_relu[ri]], writes=B_QT[i * 4 + j])
                    if fq == 1 or (fq == 2 and i == 0):
                        pro_early()
                    if fq == 3 and i == 3:
                        pro_late()
                for n in range(4):
                    wt, wtb = load_w(wb_down[fq * 2048:(fq + 1) * 2048, n * 512:(n + 1) * 512], 16, ("down", fq * 4 + n))
                    for tt in range(4):
                        ps, psbuf = psf_rot.next()
                        for kc in range(16):
                            op(PE, lambda e, kc=kc, ps=ps, tt=tt, wt=wt: e.matmul(
                                ps[:, :], lhsT=actT[:, kc, tt * 128:(tt + 1) * 128], rhs=wt[:, kc, :],
                                start=(kc == 0), stop=(kc == 15)),
                               reads=[wtb] + B_QT[kc], writes=[psbuf], signal=(kc == 15))
                        op(DVE, lambda e, ps=ps, tt=tt, n=n: e.tensor_tensor(out=xres[:, tt, n * 512:(n + 1) * 512], in0=ps[:, :],
                                                                           in1=xres[:, tt, n * 512:(n + 1) * 512], op=ALU.add),
                           reads=[psbuf, B_xres[tt]], writes=[B_xres[tt]])
                    if fq == 3 and n < 3:
                        pro_late()
            def emit_stores(s=s):
                for tt in range(4):
                    dma(POOL, ds_st[tt], y[s * T + tt * 128: s * T + (tt + 1) * 128, :], xres[:, tt, :], reads=[B_xres[tt]])
            if s + 1 < nslots:
                deferred_stores.append(emit_stores)
            else:
                emit_stores()
            assert not early_steps and not late_steps

        for tt in range(4):
            POOL.wait((ds_st[tt], ds_st[tt].count))
        if dbg is not None:
            POOL.wait((ds_dbg, ds_dbg.count))

        block = es.enter_context(nc.Block())

        @block.tensor
        def _(e):
            for f in PE.ops:
                f(e)

        @block.scalar
        def _(e):
            for f in ACT.ops:
                f(e)

        @block.vector
        def _(e):
            for f in DVE.ops:
                f(e)

        @block.gpsimd
        def _(e):
            for f in POOL.ops:
                f(e)

        @block.sync
        def _(e):
            for f in SP.ops:
                f(e)

    return nc


def _t5_bucket(rel):
    nb = 16
    max_exact = 8
    ret = (rel > 0).astype(np.int32) * nb
    n = np.abs(rel).astype(np.int32)
    nf = np.maximum(n, max_exact).astype(np.float32)
    large = max_exact + (np.log(nf / max_exact) / np.log(128 / max_exact) * (nb - max_exact)).astype(np.int32)
    large = np.minimum(large, nb - 1)
    return ret + np.where(n < max_exact, n, large)


def _window_bias_layout(t5_bias):
    kl = np.arange(128)[:, None, None]
    dl = np.arange(-1, 2)[None, :, None]
    ql = np.arange(128)[None, None, :]
    rel = dl * 128 + kl - ql
    idx = _t5_bucket(rel)
    out = t5_bias[idx]
    out = np.where((np.abs(rel) <= 128)[..., None], out, np.float32(NEGM))
    return np.ascontiguousarray(out.transpose(0, 3, 1, 2)).reshape(128, -1).astype(np.float32)


def _na_bias_layout(rpb):
    a = (np.arange(128) // 64)[:, None, None, None]
    kc = (np.arange(128) % 64)[:, None, None, None]
    ur = np.arange(7)[None, :, None, None]
    rr = np.arange(2)[None, None, :, None]
    qc = np.arange(64)[None, None, None, :]
    j = 13 - 2 * ur + rr - a
    dr = 14 - j
    ok_r = (j >= 0) & (j <= 14)
    dcc = np.clip(kc - qc, -15, 15) + 15
    start_c = np.clip(qc - 8, 0, 64 - 16)
    ok_c = (kc >= start_c) & (kc < start_c + 16)
    drc = np.clip(dr, 0, 14)
    drc, dccb = np.broadcast_arrays(drc, dcc)
    g = rpb[:, drc, dccb]
    ok = np.broadcast_to(ok_r & ok_c, g.shape[1:])
    g = np.where(ok[None], g, np.float32(NEGM))
    return np.ascontiguousarray(g.transpose(1, 0, 2, 3, 4)).reshape(128, -1).astype(np.float32)


def _lsel_layout():
    L = np.zeros((128, 8, 128), np.float32)
    for kt in range(8):
        for a in range(2):
            L[2 * kt + a, kt, a * 64:(a + 1) * 64] = 1.0
    return L.reshape(128, -1)


def _mask_tables(cases):
    ns = len(cases)
    wm = np.zeros((128, ns, 2), np.float32)
    mrow = np.zeros((128, ns, 8), np.float32)
    for s, cs in enumerate(cases):
        if "S" in cs:
            wm[:, s, 0] = NEGM
        if "E" in cs:
            wm[:, s, 1] = NEGM
        for r in range(8):
            rs = r - 4
            if "S" in cs:
                rs = max(rs, 0)
            if "E" in cs:
                rs = min(rs, 0)
            for i in range(16):
                kr = i - 4
                mrow[i, s, r] = 0.0 if (rs <= kr < rs + 8) else NEGM
    return wm.reshape(128, -1).astype(np.float32), mrow.reshape(128, -1).astype(np.float32)


_PROG_CACHE = {}


def _get_prog(nslots, slot_rows, slot_types, slot_cases, dbg=None):
    key = (nslots, tuple(slot_rows), tuple(slot_types), tuple(slot_cases), None if dbg is None else (dbg[0], tuple(dbg[1]), dbg[2]))
    if key not in _PROG_CACHE:
        _PROG_CACHE[key] = build_program(nslots, list(slot_rows), list(slot_types), list(slot_cases), dbg)
    return _PROG_CACHE[key]


def _common_inputs(norm_mix_g, w_in, q_norm_a, k_norm_a, t5_bias, sink_a, q_norm_b, k_norm_b, rpb_b,
                   w_br_a, w_br_b, w_out, norm_mlp_g, w_up, w_down):
    f = lambda a: np.ascontiguousarray(np.asarray(a, dtype=np.float32))
    pvec = np.zeros((128, 36), np.float32)
    pvec[:, 0:16] = f(norm_mix_g).reshape(16, 128).T
    pvec[:, 16:32] = f(norm_mlp_g).reshape(16, 128).T
    pvec[:, 32] = f(q_norm_a).reshape(128)
    pvec[:, 33] = f(k_norm_a).reshape(128)
    pvec[:, 34] = f(q_norm_b).reshape(128)
    pvec[:, 35] = f(k_norm_b).reshape(128)
    return {
        "w_in": f(w_in).reshape(D, IN_W),
        "w_bra": f(w_br_a).reshape(1024, D),
        "w_brb": f(w_br_b).reshape(1024, D),
        "w_out": f(w_out).reshape(D, D),
        "w_up": f(w_up).reshape(D, DFF),
        "w_down": f(w_down).reshape(DFF, D),
        "pvec": pvec,
        "sinkrep": np.ascontiguousarray(np.broadcast_to(f(sink_a).reshape(1, 8), (128, 8))),
        "wbias": _window_bias_layout(f(t5_bias)),
        "nbias": _na_bias_layout(f(rpb_b).reshape(8, 15, 31)),
        "ident": np.eye(128, dtype=np.float32),
        "lsel": _lsel_layout(),
    }


def kernel(x_prompt, x_sample, norm_mix_g, w_in, q_norm_a, k_norm_a, t5_bias, sink_a, q_norm_b, k_norm_b,
           rpb_b, w_br_a, w_br_b, w_out, norm_mlp_g, w_up, w_down):
    x_prompt = np.asarray(x_prompt, dtype=np.float32)
    x_sample = np.asarray(x_sample, dtype=np.float32)
    common = _common_inputs(norm_mix_g, w_in, q_norm_a, k_norm_a, t5_bias, sink_a, q_norm_b, k_norm_b, rpb_b,
                            w_br_a, w_br_b, w_out, norm_mlp_g, w_up, w_down)
    nslots = 10
    slot_rows = [512 * s for s in range(8)] + [4608, 4608 + 512]
    in_maps = []
    for c in range(NCORES):
        xin = np.zeros((4608 + 1536, D), np.float32)
        xin[256:256 + 4096] = x_prompt[c]
        sq, half = c // 2, c % 2
        if half == 0:
            xin[4608 + 256: 4608 + 1536] = x_sample[sq, 0:1280]
            scases = ["S", "I"]
        else:
            xin[4608: 4608 + 1280] = x_sample[sq, 768:2048]
            scases = ["I", "E"]
        cases = ["S"] + ["I"] * 6 + ["E"] + scases
        m = dict(common)
        m["xin"] = xin
        m["masks"], m["mrow"] = _mask_tables(cases)
        in_maps.append(m)
    slot_types = ["F"] + ["C"] * 7 + ["F", "C"]
    slot_cases = ["S"] + ["I"] * 6 + ["E"] + [None, None]
    nc = _get_prog(nslots, slot_rows, slot_types, slot_cases)
    res = run_bass_kernel_spmd(nc, in_maps, core_ids=list(range(NCORES)))
    y_prompt = np.empty((8, 4096, D), np.float32)
    y_sample = np.empty((4, 2048, D), np.float32)
    for c in range(NCORES):
        yc = np.asarray(res.results[c]["y"], dtype=np.float32)
        y_prompt[c] = yc[0:4096]
        sq, half = c // 2, c % 2
        y_sample[sq, half * 1024:(half + 1) * 1024] = yc[4096:5120]
    return (y_prompt, y_sample)
```

```python
from contextlib import ExitStack

import numpy as np
import concourse.bass as bass
import concourse.mybir as mybir
from concourse.bass_utils import run_bass_kernel_spmd

F32 = mybir.dt.float32
BF16 = mybir.dt.bfloat16
AF = mybir.ActivationFunctionType
ALU = mybir.AluOpType

D = 2048
HD = 128
IN_W = 8704
DFF = 8192
EPS = 1e-6
NEGM = -30000.0
T = 512
HALO = 256
NCORES = 8

C_QA, C_KA, C_VA, C_QB, C_KB, C_VB, C_GA, C_GB = 0, 1024, 1280, 1536, 2560, 3584, 4608, 6656

NA_KT = {0: list(range(-2, 4)), 1: list(range(-1, 4)), 2: list(range(0, 5)), 3: list(range(0, 6))}


ATTACH_WAITS = True


class Sem:
    def __init__(self, h):
        self.h = h
        self.count = 0


class Buf:
    __slots__ = ("w", "r")

    def __init__(self):
        self.w = None
        self.r = {}


class Queue:
    def __init__(self, name, sem, is_pe=False, is_dma=False):
        self.name = name
        self.sem = sem
        self.ops = []
        self.waited = {}
        self.pending = []
        self.is_pe = is_pe
        self.is_dma = is_dma

    def wait(self, tok):
        s, v = tok
        if self.waited.get(s, 0) >= v:
            return
        self.waited[s] = v
        if self.is_dma or not ATTACH_WAITS:
            self.ops.append(lambda e, s=s, v=v: e.wait_ge(s.h, v))
        else:
            self.pending.append((s, v))

    def emit(self, fn, sem=None):
        pend = self.pending
        self.pending = []
        for s, v in pend[1:]:
            self.ops.append(lambda e, s=s, v=v: e.wait_ge(s.h, v))
        first = pend[0] if pend else None

        def thunk(e, fn=fn, first=first, sem=sem):
            ins = fn(e)
            if first is not None:
                ins = ins._wait_ge(first[0].h, first[1])
            if sem is not None:
                ins = ins.then_inc(sem.h, 1)
            return ins
        self.ops.append(thunk)


def _deps(q, reads, writes):
    deps = []
    for b in reads:
        if b.w is not None:
            deps.append(b.w)
    for b in writes:
        if b.w is not None:
            deps.append(b.w)
        for s, v in b.r.items():
            deps.append((s, v))
    for tok in deps:
        if tok[0] is q.sem and (q.is_pe or q.is_dma):
            continue
        q.wait(tok)


def _record(tok, reads, writes):
    s, v = tok
    for b in reads:
        if b.r.get(s, 0) < v:
            b.r[s] = v
    for b in writes:
        b.w = tok
        b.r = {}


def op(q, fn, reads=(), writes=(), signal=True):
    _deps(q, reads, writes)
    if signal:
        q.sem.count += 1
        tok = (q.sem, q.sem.count)
        q.emit(fn, q.sem)
    else:
        assert q.is_pe
        tok = (q.sem, q.sem.count + 1)
        q.emit(fn, None)
    _record(tok, reads, writes)
    return tok


def dma(q, dsem, out_ap, in_ap, reads=(), writes=(), **kw):
    _deps(q, reads, writes)
    dsem.count += 16
    tok = (dsem, dsem.count)
    q.ops.append(lambda e, o=out_ap, i=in_ap, s=dsem, kw=kw: e.dma_start(out=o, in_=i, **kw).then_inc(s.h, 16))
    _record(tok, reads, writes)
    return tok


class Rot:
    def __init__(self, items):
        self.items = items
        self.i = 0

    def next(self):
        it = self.items[self.i % len(self.items)]
        self.i += 1
        return it


def build_program(nslots, slot_rows, slot_types, slot_cases=None, dbg=None):
    nc = bass.Bass("TRN2", target_bir_lowering=False)
    NROWS_IN = max(slot_rows) + 1024

    def din(name, shape, dt=F32):
        return nc.dram_tensor(name, list(shape), dt, kind="ExternalInput").ap()

    xin = din("xin", [NROWS_IN, D])
    w_in = din("w_in", [D, IN_W])
    w_bra = din("w_bra", [1024, D])
    w_brb = din("w_brb", [1024, D])
    w_out = din("w_out", [D, D])
    w_up = din("w_up", [D, DFF])
    w_down = din("w_down", [DFF, D])
    pvec_d = din("pvec", [128, 36])
    sink_d = din("sinkrep", [128, 8])
    wbias_d = din("wbias", [128, 8 * 3 * 128])
    nbias_d = din("nbias", [128, 8 * 7 * 128])
    lsel_d = din("lsel", [128, 8 * 128])
    mrow_d = din("mrow", [128, nslots * 8])
    masks_d = din("masks", [128, nslots * 2])
    ident_d = din("ident", [128, 128])
    y = nc.dram_tensor("y", [nslots * T, D], F32, kind="ExternalOutput").ap()
    dbg_out = None
    if dbg is not None:
        dbg_out = nc.dram_tensor("dbg", list(dbg[1]), BF16 if dbg[0] in ("A", "B") else F32, kind="ExternalOutput").ap()

    wb_in = nc.dram_tensor("wb_in", [D, IN_W], BF16).ap()
    wb_bra = nc.dram_tensor("wb_bra", [1024, D], BF16).ap()
    wb_brb = nc.dram_tensor("wb_brb", [1024, D], BF16).ap()
    wb_out = nc.dram_tensor("wb_out", [D, D], BF16).ap()
    wb_up = nc.dram_tensor("wb_up", [D, DFF], BF16).ap()
    wb_down = nc.dram_tensor("wb_down", [DFF, D], BF16).ap()

    with ExitStack() as es:
        def sb(name, shape, dt):
            return es.enter_context(nc.sbuf_tensor(name, list(shape), dt))

        def mksem(name):
            return Sem(es.enter_context(nc.semaphore(name)))

        xres = sb("xres", [128, 4, D], F32)
        xstage = sb("xstage", [128, D], F32)
        hbf = [sb(f"hbf{i}", [128, D], BF16) for i in range(2)]
        hTc = sb("hTc", [128, 16, T], BF16)
        hTh = sb("hTh", [128, 16, 512], BF16)
        arena = sb("arena", [128, 16384], BF16)
        KaT = sb("KaT", [128, 2, 768], BF16)
        Va = sb("Va", [128, 6, 2, 130], BF16)
        Vb = sb("Vb", [128, 8, 8, 130], BF16)
        E_w = sb("E_w", [128, 8, 3, 128], BF16)
        E_n = sb("E_n", [128, 8, 7, 128], BF16)
        masks = sb("masks_sb", [128, nslots * 2], F32)
        Lsel = sb("Lsel", [128, 8, 128], BF16)
        Mrow = sb("Mrow", [128, nslots * 8], BF16)
        pvec = sb("pvecs", [128, 36], F32)
        qsc = sb("qsc", [128, 4], F32)
        expsink = sb("expsink", [128, 8], F32)
        ident = sb("ident_b", [128, 128], BF16)
        ones_b = sb("ones_b", [128, 128], BF16)
        sqh = [sb(f"sqh{i}", [128, 512], BF16) for i in range(2)]
        wbuf = [sb(f"wbuf{i}", [128, 16, 512], BF16) for i in range(2)]
        tmpf = sb("tmpf", [128, 2048], F32)
        sqb = [tmpf[:, 0:512], tmpf[:, 512:1024]]
        lnb = [tmpf[:, 1024:1536], tmpf[:, 1536:2048]]
        expS = [tmpf[:, 0:768], tmpf[:, 1024:1792]]
        PT = [sb(f"PT{i}", [128, 6, 128], BF16) for i in range(3)]
        Onb = [sb(f"On{i}", [128, 128], BF16) for i in range(2)]
        relu_t = sqb
        small = sb("small", [128, 64], F32)
        gt_ap = hTh[:].rearrange("p a b -> p (a b)").bitcast(F32)

        psf = [es.enter_context(nc.psum_tensor(f"psf{i}", [128, 512], F32)) for i in range(6)]
        psb = [es.enter_context(nc.psum_tensor(f"psb{i}", [128, 1024], BF16)) for i in range(2)]
        psf_rot = Rot([(psf[i], Buf()) for i in range(6)])
        psb_rot = Rot([(psb[i], Buf()) for i in range(2)])

        PE = Queue("pe", mksem("s_pe"), is_pe=True)
        ACT = Queue("act", mksem("s_act"))
        DVE = Queue("dve", mksem("s_dve"))
        POOL = Queue("pool", mksem("s_pool"), is_dma=True)
        SP = Queue("sp", mksem("s_sp"), is_dma=True)
        ds_w = [mksem(f"ds_w{i}") for i in range(2)]
        ds_xs = mksem("ds_xs")
        ds_xr = [mksem(f"ds_xr{i}") for i in range(4)]
        ds_st = [mksem(f"ds_st{i}") for i in range(4)]
        ds_c = [mksem(f"ds_c{i}") for i in range(8)]
        ds_dbg = mksem("ds_dbg")

        B_xres = [Buf() for _ in range(4)]
        B_xstage = Buf()
        B_hbf = [Buf(), Buf()]
        B_hTc = [Buf() for _ in range(4)]
        B_hTh = [Buf() for _ in range(4)]
        B_QT = [[Buf() for _ in range(4)] for _ in range(16)]
        B_KbT_lo = [Buf() for _ in range(8)]
        B_KbT_hi = [Buf() for _ in range(8)]
        B_KbT = B_KbT_lo + B_KbT_hi
        B_KaT = [Buf() for _ in range(2)]
        B_Va = [Buf() for _ in range(6)]
        B_Vb = [[Buf() for _ in range(2)] for _ in range(8)]
        B_wbuf = [Buf(), Buf()]
        B_sq = [Buf(), Buf()]
        B_sqh = [Buf(), Buf()]
        B_ln = [Buf(), Buf()]
        B_expS = [B_sq, B_ln]
        B_PT = [Buf(), Buf(), Buf()]
        B_On = [Buf(), Buf()]
        B_relu = B_sq
        B_small = [Buf() for _ in range(16)]
        B_const = Buf()
        B_gt = [Buf() for _ in range(8)]
        all_QT = [b for hb in B_QT for b in hb]

        QT = arena[:, 0:8192].rearrange("p (h t) -> p h t", h=16)
        KbT = arena[:, 8192:16384].rearrange("p (h t) -> p h t", h=8)
        actT = arena[:, 0:8192].rearrange("p (c t) -> p c t", c=16)
        Vb_lo_flat = Vb[:, 0:4, :, :].rearrange("p a b c -> p (a b c)")
        ds_sh = [mksem(f"ds_sh{i}") for i in range(4)]

        def merged_ap(c):
            if c < 8:
                return KbT[:, c, 0:512]
            return Vb_lo_flat[:, (c - 8) * 512:(c - 7) * 512]

        B_Vb_lo = [b_ for t_ in range(4) for b_ in B_Vb[t_]]

        def B_merged(c):
            return [B_KbT_lo[c]] if c < 8 else B_Vb_lo

        B_merged_all = B_KbT_lo + B_Vb_lo
        B_act = all_QT

        CAST = {}
        cast_jobs = []

        def cjob(key, dst, src):
            b = Buf()
            CAST[key] = b
            cast_jobs.append((key, dst, src, b))

        for i in range(9):
            cjob(("in", i), wb_in[:, i * 512:(i + 1) * 512], w_in[:, i * 512:(i + 1) * 512])
        for n in range(4):
            cjob(("in", 9 + n), wb_in[:, C_GA + n * 512: C_GA + (n + 1) * 512], w_in[:, C_GA + n * 512: C_GA + (n + 1) * 512])
            cjob(("in", 13 + n), wb_in[:, C_GB + n * 512: C_GB + (n + 1) * 512], w_in[:, C_GB + n * 512: C_GB + (n + 1) * 512])
            cjob(("bra", n), wb_bra[:, n * 512:(n + 1) * 512], w_bra[:, n * 512:(n + 1) * 512])
            cjob(("brb", n), wb_brb[:, n * 512:(n + 1) * 512], w_brb[:, n * 512:(n + 1) * 512])
        for n in range(4):
            cjob(("out", n), wb_out[:, n * 512:(n + 1) * 512], w_out[:, n * 512:(n + 1) * 512])
        for fq in range(4):
            for i in range(4):
                c0 = fq * 2048 + i * 512
                cjob(("up", fq * 4 + i), wb_up[:, c0:c0 + 512], w_up[:, c0:c0 + 512])
            for n in range(4):
                cjob(("down", fq * 4 + n), wb_down[fq * 2048:(fq + 1) * 2048, n * 512:(n + 1) * 512],
                     w_down[fq * 2048:(fq + 1) * 2048, n * 512:(n + 1) * 512])

        def issue_casts():
            toks = []
            for k, (key, dst, src, b) in enumerate(cast_jobs):
                if k >= 12:
                    POOL.wait(toks[k - 12])
                sm_ = mksem(f"ds_cast_{key[0]}{key[1]}")
                toks.append(dma(POOL, sm_, dst, src, writes=[b]))

        for tt in range(4):
            dma(POOL, ds_xr[tt], xres[:, tt, :], xin[slot_rows[0] + (tt + 2) * 128: slot_rows[0] + (tt + 3) * 128, :],
                writes=[B_xres[tt]])
        P_ = []

        def cb():
            b_ = Buf()
            P_.append(b_)
            return b_

        b_pvec, b_sink = cb(), cb()
        dma(SP, ds_c[0], pvec[:], pvec_d, writes=[b_pvec])
        dma(SP, ds_c[1], expsink[:], sink_d, writes=[b_sink])
        dma(SP, ds_c[2], masks[:], masks_d, writes=[cb()])
        dma(POOL, ds_c[3], ident[:], ident_d, writes=[cb()])
        dma(POOL, ds_c[4], E_w[:].rearrange("p a b c -> p (a b c)"), wbias_d, writes=[cb()], max_dma_last_dim=4096)
        dma(POOL, ds_c[5], E_n[:].rearrange("p a b c -> p (a b c)"), nbias_d, writes=[cb()], max_dma_last_dim=4096)
        dma(POOL, ds_c[6], Lsel[:].rearrange("p a b -> p (a b)"), lsel_d, writes=[cb()])
        dma(POOL, ds_c[7], Mrow[:], mrow_d, writes=[cb()])
        op(ACT, lambda e: e.activation(out=expsink[:], in_=expsink[:], func=AF.Exp), reads=[b_sink], writes=[b_sink])
        op(DVE, lambda e: e.memset(ones_b[:], 1.0), writes=[cb()])
        op(DVE, lambda e: e.memset(Va[:].rearrange("p a b c -> p (a b c)"), 1.0), writes=B_Va)
        op(DVE, lambda e: e.memset(Vb[:].rearrange("p a b c -> p (a b c)"), 1.0), writes=[b for x in B_Vb for b in x])
        sc = float(HD) ** -0.5
        b_qsc = cb()
        op(DVE, lambda e: e.tensor_scalar(out=qsc[:, 0:1], in0=pvec[:, 32:33], scalar1=sc, scalar2=None, op0=ALU.mult),
           reads=[b_pvec], writes=[b_qsc])
        op(DVE, lambda e: e.tensor_copy(out=qsc[:, 1:2], in_=pvec[:, 33:34]), reads=[b_pvec], writes=[b_qsc])
        op(DVE, lambda e: e.tensor_scalar(out=qsc[:, 2:3], in0=pvec[:, 34:35], scalar1=sc, scalar2=None, op0=ALU.mult),
           reads=[b_pvec], writes=[b_qsc])
        op(DVE, lambda e: e.tensor_copy(out=qsc[:, 3:4], in_=pvec[:, 35:36]), reads=[b_pvec], writes=[b_qsc])

        wctr = [0]

        def load_w(src_ap, kcs, cast_key):
            i = wctr[0] % 2
            wctr[0] += 1
            dma(SP, ds_w[i], wbuf[i][:, 0:kcs, :], src_ap.rearrange("(kc p) n -> p kc n", p=128),
                reads=[CAST[cast_key]], writes=[B_wbuf[i]])
            return wbuf[i], B_wbuf[i]

        sctr = [0]

        def small_col():
            i = sctr[0] % 16
            sctr[0] += 1
            return small[:, 4 * i:4 * i + 4], B_small[i]

        rot2 = {"sq": 0, "ln": 0, "expS": 0, "PT": 0, "On": 0, "relu": 0, "hbf": 0}

        def nxt(k):
            i = rot2[k] % 2
            rot2[k] += 1
            return i

        pend = []

        def flush(keep=0):
            while len(pend) > keep:
                pend.pop(0)()

        def rms1(x_ap, xbuf):
            hi = nxt("hbf")
            sm, smb = small_col()
            op(ACT, lambda e: e.activation(out=hbf[hi][:], in_=x_ap, func=AF.Square, accum_out=sm[:, 0:1]),
               reads=[xbuf], writes=[B_hbf[hi], smb])
            op(ACT, lambda e: e.activation(out=sm[:, 1:2], in_=sm[:, 0:1], func=AF.Ln, scale=1.0 / D, bias=sm[:, 3:4]),
               reads=[smb], writes=[smb])
            op(ACT, lambda e: e.activation(out=sm[:, 2:3], in_=sm[:, 1:2], func=AF.Exp, scale=-0.5),
               reads=[smb], writes=[smb])
            op(DVE, lambda e: e.tensor_scalar(out=hbf[hi][:], in0=x_ap, scalar1=sm[:, 2:3], scalar2=None, op0=ALU.mult),
               reads=[xbuf, smb], writes=[B_hbf[hi]])
            return hi

        def rms2(hi, gcol0, dst_ap_fn, dst_bufs):
            for half in range(2):
                pb, pbb = psb_rot.next()
                for cc in range(8):
                    ch = half * 8 + cc
                    op(PE, lambda e, pb=pb, cc=cc, ch=ch: e.transpose(out=pb[:, cc * 128:(cc + 1) * 128],
                                                                     in_=hbf[hi][:, ch * 128:(ch + 1) * 128],
                                                                     identity=ident[:]),
                       reads=[B_hbf[hi], B_const], writes=[pbb], signal=(cc == 7))
                g_b = pvec[:, gcol0 + half * 8: gcol0 + half * 8 + 8].unsqueeze(2).to_broadcast([128, 8, 128])
                op(DVE, lambda e, pb=pb, half=half, g_b=g_b: e.tensor_tensor(
                    out=dst_ap_fn(half), in0=pb[:].rearrange("p (c t) -> p c t", c=8), in1=g_b, op=ALU.mult),
                   reads=[pbb, B_const], writes=dst_bufs)

        def prologue_steps(sn):
            r0n = slot_rows[sn]
            tiles = (0, 1, 6, 7, 2, 3, 4, 5) if slot_types[sn] == "F" else (6, 7, 2, 3, 4, 5)
            nt = len(tiles)
            p1s, p2s = [], []
            for t in tiles:
                def p1(t=t):
                    if sn == 0 and 2 <= t <= 5:
                        return rms1(xres[:, t - 2, :], B_xres[t - 2])
                    dma(SP, ds_xs, xstage[:], xin[r0n + t * 128: r0n + (t + 1) * 128, :], writes=[B_xstage])
                    return rms1(xstage[:], B_xstage)

                def p2(hi, t=t):
                    if 2 <= t <= 5:
                        tt = t - 2
                        rms2(hi, 0, lambda half: hTc[:, half * 8:(half + 1) * 8, tt * 128:(tt + 1) * 128], [B_hTc[tt]])
                    else:
                        hh = t if t < 2 else t - 4
                        rms2(hi, 0, lambda half: hTh[:, half * 8:(half + 1) * 8, hh * 128:(hh + 1) * 128],
                             [B_hTh[hh]] + B_gt)
                p1s.append(p1)
                p2s.append(p2)
            ctx = {}
            steps = []
            for k in range(nt + 1):
                def step(k=k):
                    if 1 <= k <= nt:
                        p2s[k - 1](ctx[k - 1])
                    if k <= nt - 1:
                        ctx[k] = p1s[k]()
                steps.append(step)
            return steps

        def qknorm(ps, psbuf, n, gcol, out_ap, out_bufs):
            si = nxt("sq")
            li = nxt("ln")
            op(ACT, lambda e: e.activation(out=sqh[si][:, 0:n], in_=ps[:, 0:n], func=AF.Square),
               reads=[psbuf], writes=[B_sqh[si]])
            ps2, ps2b = psf_rot.next()

            def pe_part():
                op(PE, lambda e: e.matmul(ps2[:, 0:n], lhsT=ones_b[:], rhs=sqh[si][:, 0:n], start=True, stop=True),
                   reads=[B_sqh[si], B_const], writes=[ps2b])
                op(ACT, lambda e: e.activation(out=lnb[li][:, 0:n], in_=ps2[:, 0:n], func=AF.Ln, scale=1.0 / HD,
                                               bias=small[:, 63:64]),
                   reads=[ps2b, B_const], writes=[B_ln[li]])
                op(ACT, lambda e: e.activation(out=lnb[li][:, 0:n], in_=lnb[li][:, 0:n], func=AF.Exp, scale=-0.5),
                   reads=[B_ln[li]], writes=[B_ln[li]])
                op(DVE, lambda e: e.scalar_tensor_tensor(out=out_ap, in0=ps[:, 0:n], scalar=qsc[:, gcol:gcol + 1],
                                                          in1=lnb[li][:, 0:n], op0=ALU.mult, op1=ALU.mult),
                   reads=[psbuf, B_ln[li], B_const], writes=out_bufs)
            pend.append(pe_part)

        op(DVE, lambda e: e.memset(small[:], EPS), writes=B_small)
        op(DVE, lambda e: e.memset(relu_t[0][:, 0:1], 0.0), reads=P_, writes=[B_const, B_sq[0]])

        deferred_stores = []
        for s in range(nslots):
            r0 = slot_rows[s]
            mw0 = s * 2
            mn0 = nslots * 2 + s * 64

            if s == 0:
                for stp in prologue_steps(0):
                    stp()
                issue_casts()
            nxt_steps = prologue_steps(s + 1) if s + 1 < nslots else []
            early_steps = nxt_steps[:-4]
            late_steps = nxt_steps[-4:]
            assert len(early_steps) <= 5

            def pro_early():
                if early_steps:
                    early_steps.pop(0)()

            def pro_late():
                if late_steps:
                    late_steps.pop(0)()

            seg_c = (lambda kc: hTc[:, kc, :], 512, B_hTc)
            seg_b = (lambda kc: hTh[:, kc, 0:256], 256, B_hTh[0:2])
            seg_a = (lambda kc: hTh[:, kc, 256:512], 256, B_hTh[2:4])

            def proj_fm(wt, wtb, j, seg, then):
                rhs_fn, n, hb = seg
                ps, psbuf = psf_rot.next()
                for kc in range(16):
                    op(PE, lambda e, kc=kc: e.matmul(ps[:, 0:n], lhsT=wt[:, kc, j * 128:(j + 1) * 128], rhs=rhs_fn(kc),
                                                     start=(kc == 0), stop=(kc == 15)),
                       reads=[wtb] + list(hb), writes=[psbuf], signal=(kc == 15))
                flush(0)
                then(ps, psbuf, n)

            for wi in range(2):
                wt, wtb = load_w(wb_in[:, C_QA + wi * 512: C_QA + (wi + 1) * 512], 16, ("in", wi))
                for j in range(4):
                    h = wi * 4 + j
                    proj_fm(wt, wtb, j, seg_c,
                            lambda ps, pb, n, h=h: qknorm(ps, pb, n, 0, QT[:, h, :], B_QT[h]))
            isF = slot_types[s] == "F"
            if not isF:
                assert slot_rows[s] == slot_rows[s - 1] + 512
                dma(SP, ds_sh[0], KbT[:, :, 0:512], KbT[:, :, 512:1024], reads=B_KbT_hi, writes=B_KbT_lo)
                dma(SP, ds_sh[1], Vb[:, 0:4, :, :], Vb[:, 4:8, :, :], reads=[b_ for t_ in range(4, 8) for b_ in B_Vb[t_]],
                    writes=B_Vb_lo)
                dma(SP, ds_sh[2], KaT[:, :, 0:256], KaT[:, :, 512:768], reads=B_KaT, writes=B_KaT)
                dma(SP, ds_sh[3], Va[:, 0:2, :, :], Va[:, 4:6, :, :], reads=B_Va[4:6], writes=B_Va[0:2])
            else:
                op(DVE, lambda e: e.memset(Vb[:, 0:4, :, 128:130], 1.0), writes=B_Vb_lo)
            wt, wtb = load_w(wb_in[:, C_KA: C_KA + 512], 16, ("in", 2))
            for j in range(2):
                if isF:
                    proj_fm(wt, wtb, j, (lambda kc: hTh[:, kc, 128:256], 128, [B_hTh[1]]),
                            lambda ps, pb, n, j=j: qknorm(ps, pb, n, 1, KaT[:, j, 0:128], [B_KaT[j]]))
                    proj_fm(wt, wtb, j, seg_c,
                            lambda ps, pb, n, j=j: qknorm(ps, pb, n, 1, KaT[:, j, 128:640], [B_KaT[j]]))
                else:
                    proj_fm(wt, wtb, j, (lambda kc: hTc[:, kc, 128:512], 384, B_hTc[1:4]),
                            lambda ps, pb, n, j=j: qknorm(ps, pb, n, 1, KaT[:, j, 256:640], [B_KaT[j]]))
                proj_fm(wt, wtb, j, (lambda kc: hTh[:, kc, 256:384], 128, [B_hTh[2]]),
                        lambda ps, pb, n, j=j: qknorm(ps, pb, n, 1, KaT[:, j, 640:768], [B_KaT[j]]))

            def tok_lhsT(t):
                if t < 2:
                    return (lambda kc: hTh[:, kc, t * 128:(t + 1) * 128]), B_hTh[t]
                if t >= 6:
                    return (lambda kc: hTh[:, kc, (t - 4) * 128:(t - 3) * 128]), B_hTh[t - 4]
                return (lambda kc: hTc[:, kc, (t - 2) * 128:(t - 1) * 128]), B_hTc[t - 2]

            flush(0)
            for vt in (range(6) if isF else range(2, 6)):
                lf, lb = tok_lhsT(vt + 1)
                ps, psbuf = psf_rot.next()
                for kc in range(16):
                    op(PE, lambda e, kc=kc, lf=lf, ps=ps, wt=wt: e.matmul(ps[:, 0:256], lhsT=lf(kc), rhs=wt[:, kc, 256:512],
                                                                  start=(kc == 0), stop=(kc == 15)),
                       reads=[wtb, lb], writes=[psbuf], signal=(kc == 15))
                op(ACT, lambda e, ps=ps, vt=vt: e.activation(out=Va[:, vt, :, 0:128],
                                                            in_=ps[:, 0:256].rearrange("p (h d) -> p h d", h=2),
                                                            func=AF.Copy),
                   reads=[psbuf], writes=[B_Va[vt]])
            for wi in range(2):
                wt, wtb = load_w(wb_in[:, C_QB + wi * 512: C_QB + (wi + 1) * 512], 16, ("in", 3 + wi))
                for j in range(4):
                    h = 8 + wi * 4 + j
                    proj_fm(wt, wtb, j, seg_c,
                            lambda ps, pb, n, h=h: qknorm(ps, pb, n, 2, QT[:, h, :], B_QT[h]))
            for wi in range(2):
                wt, wtb = load_w(wb_in[:, C_KB + wi * 512: C_KB + (wi + 1) * 512], 16, ("in", 5 + wi))
                for j in range(4):
                    h = wi * 4 + j
                    if isF:
                        proj_fm(wt, wtb, j, seg_b,
                                lambda ps, pb, n, h=h: qknorm(ps, pb, n, 3, KbT[:, h, 0:256], [B_KbT_lo[h]]))
                        proj_fm(wt, wtb, j, seg_c,
                                lambda ps, pb, n, h=h: qknorm(ps, pb, n, 3, KbT[:, h, 256:768], [B_KbT_lo[h], B_KbT_hi[h]]))
                    else:
                        proj_fm(wt, wtb, j, (lambda kc: hTc[:, kc, 256:512], 256, B_hTc[2:4]),
                                lambda ps, pb, n, h=h: qknorm(ps, pb, n, 3, KbT[:, h, 512:768], [B_KbT_hi[h]]))
                    proj_fm(wt, wtb, j, seg_a,
                            lambda ps, pb, n, h=h: qknorm(ps, pb, n, 3, KbT[:, h, 768:1024], [B_KbT_hi[h]]))
            flush(0)
            for wi in range(2):
                wt, wtb = load_w(wb_in[:, C_VB + wi * 512: C_VB + (wi + 1) * 512], 16, ("in", 7 + wi))
                for vt in (range(8) if isF else range(4, 8)):
                    lf, lb = tok_lhsT(vt)
                    ps, psbuf = psf_rot.next()
                    for kc in range(16):
                        op(PE, lambda e, kc=kc, lf=lf, ps=ps, wt=wt: e.matmul(ps[:, 0:512], lhsT=lf(kc), rhs=wt[:, kc, :],
                                                                      start=(kc == 0), stop=(kc == 15)),
                           reads=[wtb, lb], writes=[psbuf], signal=(kc == 15))
                    op(ACT, lambda e, ps=ps, vt=vt, wi=wi: e.activation(
                        out=Vb[:, vt, wi * 4:(wi + 1) * 4, 0:128],
                        in_=ps[:, 0:512].rearrange("p (h d) -> p h d", h=4), func=AF.Copy),
                       reads=[psbuf], writes=[B_Vb[vt][wi]])

            if dbg is not None and dbg[0] == "A" and s == dbg[2]:
                dma(POOL, ds_dbg, dbg_out[:, 0:16384], arena[:], reads=all_QT + B_KbT)
                dma(POOL, ds_dbg, dbg_out[:, 16384:16384 + 1536], KaT[:].rearrange("p a b -> p (a b)"), reads=B_KaT)
                dma(POOL, ds_dbg, dbg_out[:, 17920:17920 + 1560], Va[:].rearrange("p a b c -> p (a b c)"), reads=B_Va)
                dma(POOL, ds_dbg, dbg_out[:, 19480:19480 + 8320], Vb[:].rearrange("p a b c -> p (a b c)"),
                    reads=[b for x in B_Vb for b in x])

            if deferred_stores:
                POOL.wait(wtb.w)
                deferred_stores.pop(0)()
            if s > 0:
                for tt in range(4):
                    dma(POOL, ds_xr[tt], xres[:, tt, :], xin[r0 + (tt + 2) * 128: r0 + (tt + 3) * 128, :], writes=[B_xres[tt]])

            items = []
            for h in range(8):
                for qb in range(4):
                    items.append(("w", h, qb))
            for h in range(8):
                for qt in range(4):
                    items.append(("n", h, qt))
            st = {}

            def stage1(it):
                kind, h, q = it
                d = {}
                st[it] = d
                ei = nxt("expS")
                pi = rot2["PT"] % 3
                rot2["PT"] += 1
                d["ei"], d["pi"] = ei, pi
                tmp = expS[ei]
                if kind == "w":
                    g = h // 4
                    kbs = [q - 1, q, q + 1]
                    ps, psbuf = psf_rot.next()
                    for i, kb in enumerate(kbs):
                        op(PE, lambda e, i=i, kb=kb: e.matmul(ps[:, i * 128:(i + 1) * 128],
                                                              lhsT=KaT[:, g, (kb + 1) * 128:(kb + 2) * 128],
                                                              rhs=QT[:, h, q * 128:(q + 1) * 128], start=True, stop=True),
                           reads=[B_KaT[g], B_QT[h][q]], writes=[psbuf], signal=(i == 2))
                    op(DVE, lambda e: e.tensor_tensor(out=tmp[:, 0:384], in0=ps[:, 0:384],
                                                      in1=E_w[:, h, :, :].rearrange("p a b -> p (a b)"), op=ALU.add),
                       reads=[psbuf, B_const], writes=B_expS[ei])
                    if q == 0:
                        segs = [(0, 1, masks[:, mw0:mw0 + 1]), (1, 3, None)]
                    elif q == 3:
                        segs = [(0, 2, None), (2, 3, masks[:, mw0 + 1:mw0 + 2])]
                    else:
                        segs = [(0, 3, None)]
                    for a_, b_, bias in segs:
                        if bias is None:
                            op(ACT, lambda e, a_=a_, b_=b_: e.activation(
                                out=PT[pi][:, a_:b_, :].rearrange("p a b -> p (a b)"), in_=tmp[:, a_ * 128:b_ * 128], func=AF.Exp),
                               reads=B_expS[ei], writes=[B_PT[pi]])
                        else:
                            op(ACT, lambda e, a_=a_, b_=b_, bias=bias: e.activation(
                                out=PT[pi][:, a_:b_, :].rearrange("p a b -> p (a b)"), in_=tmp[:, a_ * 128:b_ * 128], func=AF.Exp,
                                bias=bias),
                               reads=B_expS[ei] + [B_const], writes=[B_PT[pi]])
                    d["nk"] = 3
                    d["v"] = [(Va[:, kb + 1, g, 0:129], B_Va[kb + 1]) for kb in kbs]
                else:
                    kts = NA_KT[q]
                    need_mask = {kt: True for kt in kts}
                    case = None if slot_cases is None else slot_cases[s]
                    if case is not None:
                        def _valid(kr, r):
                            rs = r - 4
                            if "S" in case:
                                rs = max(rs, 0)
                            if "E" in case:
                                rs = min(rs, 0)
                            return rs <= kr < rs + 8
                        keep = []
                        for kt in kts:
                            v = [_valid(2 * kt + a_, 2 * q + rr_) for a_ in range(2) for rr_ in range(2)]
                            if any(v):
                                keep.append(kt)
                                need_mask[kt] = not all(v)
                        assert keep == list(range(keep[0], keep[-1] + 1))
                        kts = keep
                    nk = len(kts)
                    banks = [psf_rot.next(), psf_rot.next()]
                    mr = Mrow[:, s * 8 + 2 * q: s * 8 + 2 * q + 2].unsqueeze(2).to_broadcast([128, 2, 64])
                    for i, kt in enumerate(kts):
                        ps, psbuf = banks[i // 4]
                        co = (i % 4) * 128
                        op(PE, lambda e, ps=ps, co=co, kt=kt: e.matmul(ps[:, co:co + 128],
                                                                      lhsT=KbT[:, h, (kt + 2) * 128:(kt + 3) * 128],
                                                                      rhs=QT[:, 8 + h, q * 128:(q + 1) * 128],
                                                                      start=True, stop=(not need_mask[kt])),
                           reads=[(B_KbT_lo if kt + 2 < 4 else B_KbT_hi)[h], B_QT[8 + h][q]], writes=[psbuf],
                           signal=((not need_mask[kt]) and (i % 4 == 3 or i == nk - 1)))
                        if not need_mask[kt]:
                            continue
                        op(PE, lambda e, ps=ps, co=co, kt=kt: e.matmul(ps[:, co:co + 128].rearrange("p (a b) -> p a b", a=2),
                                                                      lhsT=Lsel[:, kt + 2, :], rhs=mr,
                                                                      start=False, stop=True),
                           reads=[B_const], writes=[psbuf], signal=(i % 4 == 3 or i == nk - 1))
                    ur0 = 3 - q + kts[0]
                    for bi_ in range(2):
                        n_ = min(nk - bi_ * 4, 4)
                        if n_ <= 0:
                            continue
                        ps, psbuf = banks[bi_]
                        op(DVE, lambda e, ps=ps, bi_=bi_, n_=n_: e.tensor_tensor(
                            out=tmp[:, bi_ * 512: bi_ * 512 + n_ * 128], in0=ps[:, 0:n_ * 128],
                            in1=E_n[:, h, ur0 + bi_ * 4: ur0 + bi_ * 4 + n_, :].rearrange("p a b -> p (a b)"), op=ALU.add),
                           reads=[psbuf, B_const], writes=B_expS[ei])
                    op(ACT, lambda e: e.activation(out=PT[pi][:, 0:nk, :].rearrange("p a b -> p (a b)"),
                                                   in_=tmp[:, 0:nk * 128], func=AF.Exp),
                       reads=B_expS[ei], writes=[B_PT[pi]])
                    d["nk"] = nk
                    d["v"] = [(Vb[:, kt + 2, h, 0:129], B_Vb[kt + 2][h // 4]) for kt in kts]

            def stage2(it):
                kind, h, q = it
                d = st[it]
                pi = d["pi"]
                ps, psbuf = psf_rot.next()
                nk = d["nk"]
                for i in range(nk):
                    vap, vb = d["v"][i]
                    op(PE, lambda e, i=i, vap=vap: e.matmul(ps[:, 0:129], lhsT=PT[pi][:, i, :], rhs=vap,
                                                           start=(i == 0), stop=(i == nk - 1)),
                       reads=[B_PT[pi], vb], writes=[psbuf], signal=(i == nk - 1))
                sm, smb = small_col()
                if kind == "w":
                    op(DVE, lambda e: e.tensor_scalar(out=sm[:, 0:1], in0=ps[:, 128:129], scalar1=expsink[:, h:h + 1],
                                                      scalar2=None, op0=ALU.add),
                       reads=[psbuf, B_const], writes=[smb])
                    op(DVE, lambda e: e.reciprocal(out=sm[:, 1:2], in_=sm[:, 0:1]), reads=[smb], writes=[smb])
                else:
                    op(DVE, lambda e: e.reciprocal(out=sm[:, 1:2], in_=ps[:, 128:129]), reads=[psbuf], writes=[smb])
                oi = nxt("On")
                d["oi"] = oi
                op(ACT, lambda e: e.activation(out=Onb[oi][:], in_=ps[:, 0:128], func=AF.Copy, scale=sm[:, 1:2]),
                   reads=[psbuf, smb], writes=[B_On[oi]])

            def stage3(it):
                kind, h, q = it
                d = st[it]
                oi = d["oi"]
                hh = h if kind == "w" else 8 + h
                pb, pbb = psb_rot.next()
                op(PE, lambda e: e.transpose(out=pb[:, 0:128], in_=Onb[oi][:], identity=ident[:]),
                   reads=[B_On[oi], B_const], writes=[pbb])
                op(ACT, lambda e: e.activation(out=QT[:, hh, q * 128:(q + 1) * 128], in_=pb[:, 0:128], func=AF.Copy),
                   reads=[pbb], writes=[B_QT[hh][q]])
                del st[it]

            n_it = len(items)
            for step in range(n_it + 3):
                if step < n_it:
                    stage1(items[step])
                if 0 <= step - 2 < n_it:
                    stage2(items[step - 2])
                if 0 <= step - 3 < n_it:
                    stage3(items[step - 3])

            if dbg is not None and dbg[0] == "B" and s == dbg[2]:
                dma(POOL, ds_dbg, dbg_out, arena[:, 0:8192], reads=all_QT)

            gt = gt_ap.rearrange("p (g j t) -> p g j t", g=2, j=4)
            for n in range(4):
                for gi, c0 in enumerate((C_GA, C_GB)):
                    wt, wtb = load_w(wb_in[:, c0 + n * 512: c0 + (n + 1) * 512], 16, ("in", 9 + 4 * gi + n))
                    for j in range(4):
                        ps, psbuf = psf_rot.next()
                        for kc in range(16):
                            op(PE, lambda e, kc=kc, ps=ps, j=j, wt=wt: e.matmul(ps[:, :], lhsT=wt[:, kc, j * 128:(j + 1) * 128],
                                                                              rhs=hTc[:, kc, :], start=(kc == 0),
                                                                              stop=(kc == 15)),
                               reads=[wtb] + B_hTc, writes=[psbuf], signal=(kc == 15))
                        gb_ = B_gt[gi * 4 + j]
                        gap = gt[:, gi, j, :]
                        op(ACT, lambda e, ps=ps, gap=gap: e.activation(out=gap, in_=ps[:, :], func=AF.Exp, scale=-1.0),
                           reads=[psbuf], writes=[gb_] + B_hTh)
                        op(DVE, lambda e, gap=gap: e.tensor_scalar(out=gap, in0=gap, scalar1=1.0, scalar2=None, op0=ALU.add),
                           reads=[gb_], writes=[gb_])
                        op(DVE, lambda e, gap=gap: e.reciprocal(out=gap, in_=gap), reads=[gb_], writes=[gb_])
                i_ = wctr[0] % 2
                wctr[0] += 1
                dma(SP, ds_w[i_], wbuf[i_][:, 0:8, :], wb_bra[:, n * 512:(n + 1) * 512].rearrange("(kc p) n -> p kc n", p=128),
                    reads=[CAST[("bra", n)]], writes=[B_wbuf[i_]])
                dma(SP, ds_w[i_], wbuf[i_][:, 8:16, :], wb_brb[:, n * 512:(n + 1) * 512].rearrange("(kc p) n -> p kc n", p=128),
                    reads=[CAST[("brb", n)]])
                B_wbuf[i_].w = (ds_w[i_], ds_w[i_].count)
                wt, wtb = wbuf[i_], B_wbuf[i_]
                for bi in range(2):
                    for j in range(4):
                        ps, psbuf = psf_rot.next()
                        for hh in range(8):
                            op(PE, lambda e, hh=hh, ps=ps, j=j, wt=wt, bi=bi: e.matmul(
                                ps[:, :], lhsT=wt[:, bi * 8 + hh, j * 128:(j + 1) * 128], rhs=QT[:, bi * 8 + hh, :],
                                start=(hh == 0), stop=(hh == 7)),
                               reads=[wtb] + B_QT[bi * 8 + hh], writes=[psbuf],
                               signal=(hh == 7))
                        ga_b, gb_b = B_gt[j], B_gt[4 + j]
                        if bi == 0:
                            op(DVE, lambda e, ps=ps, j=j: e.tensor_tensor(out=gt[:, 0, j, :], in0=ps[:, :], in1=gt[:, 0, j, :],
                                                                        op=ALU.mult),
                               reads=[psbuf, ga_b], writes=[ga_b])
                        else:
                            cm = n * 4 + j
                            op(DVE, lambda e, ps=ps, j=j: e.tensor_tensor(out=gt[:, 1, j, :], in0=ps[:, :], in1=gt[:, 1, j, :],
                                                                        op=ALU.mult),
                               reads=[psbuf, gb_b], writes=[gb_b])
                            op(DVE, lambda e, j=j, cm=cm: e.tensor_tensor(out=merged_ap(cm), in0=gt[:, 0, j, :],
                                                                        in1=gt[:, 1, j, :], op=ALU.add),
                               reads=[ga_b, gb_b], writes=B_merged(cm))

            ctx2 = {}

            def mlp_rms2(tt):
                rms2(ctx2[tt], 16, lambda half: hTc[:, half * 8:(half + 1) * 8, tt * 128:(tt + 1) * 128], [B_hTc[tt]])

            for n in range(4):
                wt, wtb = load_w(wb_out[:, n * 512:(n + 1) * 512], 16, ("out", n))
                for tt in range(4):
                    ps, psbuf = psf_rot.next()
                    for kc in range(16):
                        op(PE, lambda e, kc=kc, ps=ps, tt=tt, wt=wt: e.matmul(ps[:, :], lhsT=merged_ap(kc)[:, tt * 128:(tt + 1) * 128],
                                                                            rhs=wt[:, kc, :], start=(kc == 0), stop=(kc == 15)),
                           reads=[wtb] + B_merged(kc), writes=[psbuf], signal=(kc == 15))
                    if n == 3 and tt >= 2:
                        mlp_rms2(tt - 2)
                    op(DVE, lambda e, ps=ps, tt=tt, n=n: e.tensor_tensor(out=xres[:, tt, n * 512:(n + 1) * 512], in0=ps[:, :],
                                                                       in1=xres[:, tt, n * 512:(n + 1) * 512], op=ALU.add),
                       reads=[psbuf, B_xres[tt]], writes=[B_xres[tt]])
                    if n == 3:
                        ctx2[tt] = rms1(xres[:, tt, :], B_xres[tt])

            if dbg is not None and dbg[0] == "C" and s == dbg[2]:
                dma(POOL, ds_dbg, dbg_out.rearrange("(t p) d -> p t d", p=128), xres[:], reads=B_xres)

            mlp_rms2(2)
            mlp_rms2(3)
            for fq in range(4):
                for i in range(4):
                    wt, wtb = load_w(wb_up[:, fq * 2048 + i * 512: fq * 2048 + (i + 1) * 512], 16, ("up", fq * 4 + i))
                    for j in range(4):
                        ps, psbuf = psf_rot.next()
                        for kc in range(16):
                            op(PE, lambda e, kc=kc, ps=ps, j=j, wt=wt: e.matmul(ps[:, :], lhsT=wt[:, kc, j * 128:(j + 1) * 128],
                                                                              rhs=hTc[:, kc, :], start=(kc == 0),
                                                                              stop=(kc == 15)),
                               reads=[wtb] + B_hTc, writes=[psbuf], signal=(kc == 15))
                        ri = nxt("relu")
                        op(ACT, lambda e, ps=ps, ri=ri: e.activation(out=relu_t[ri][:], in_=ps[:, :], func=AF.Relu),
                           reads=[psbuf], writes=[B_relu[ri]])
                        op(DVE, lambda e, ps=ps, ri=ri, cc=i * 4 + j: e.tensor_tensor(out=actT[:, cc, :], in0=ps[:, :],
                                                                                          in1=relu_t[ri][:], op=ALU.mult),
                           reads=[psbuf, B_relu[ri]], writes=B_QT[i * 4 + j])
                    if fq == 1 or (fq == 2 and i == 0):
                        pro_early()
                    if fq == 3 and i == 3:
                        pro_late()
                for n in range(4):
                    wt, wtb = load_w(wb_down[fq * 2048:(fq + 1) * 2048, n * 512:(n + 1) * 512], 16, ("down", fq * 4 + n))
                    for tt in range(4):
                        ps, psbuf = psf_rot.next()
                        for kc in range(16):
                            op(PE, lambda e, kc=kc, ps=ps, tt=tt, wt=wt: e.matmul(
                                ps[:, :], lhsT=actT[:, kc, tt * 128:(tt + 1) * 128], rhs=wt[:, kc, :],
                                start=(kc == 0), stop=(kc == 15)),
                               reads=[wtb] + B_QT[kc], writes=[psbuf], signal=(kc == 15))
                        op(DVE, lambda e, ps=ps, tt=tt, n=n: e.tensor_tensor(out=xres[:, tt, n * 512:(n + 1) * 512], in0=ps[:, :],
                                                                           in1=xres[:, tt, n * 512:(n + 1) * 512], op=ALU.add),
                           reads=[psbuf, B_xres[tt]], writes=[B_xres[tt]])
                    if fq == 3 and n < 3:
                        pro_late()
            def emit_stores(s=s):
                for tt in range(4):
                    dma(POOL, ds_st[tt], y[s * T + tt * 128: s * T + (tt + 1) * 128, :], xres[:, tt, :], reads=[B_xres[tt]])
            if s + 1 < nslots:
                deferred_stores.append(emit_stores)
            else:
                emit_stores()
            assert not early_steps and not late_steps

        for tt in range(4):
            POOL.wait((ds_st[tt], ds_st[tt].count))
        if dbg is not None:
            POOL.wait((ds_dbg, ds_dbg.count))

        block = es.enter_context(nc.Block())

        @block.tensor
        def _(e):
            for f in PE.ops:
                f(e)

        @block.scalar
        def _(e):
            for f in ACT.ops:
                f(e)

        @block.vector
        def _(e):
            for f in DVE.ops:
                f(e)

        @block.gpsimd
        def _(e):
            for f in POOL.ops:
                f(e)

        @block.sync
        def _(e):
            for f in SP.ops:
                f(e)

    return nc


def _t5_bucket(rel):
    nb = 16
    max_exact = 8
    ret = (rel > 0).astype(np.int32) * nb
    n = np.abs(rel).astype(np.int32)
    nf = np.maximum(n, max_exact).astype(np.float32)
    large = max_exact + (np.log(nf / max_exact) / np.log(128 / max_exact) * (nb - max_exact)).astype(np.int32)
    large = np.minimum(large, nb - 1)
    return ret + np.where(n < max_exact, n, large)


def _window_bias_layout(t5_bias):
    kl = np.arange(128)[:, None, None]
    dl = np.arange(-1, 2)[None, :, None]
    ql = np.arange(128)[None, None, :]
    rel = dl * 128 + kl - ql
    idx = _t5_bucket(rel)
    out = t5_bias[idx]
    out = np.where((np.abs(rel) <= 128)[..., None], out, np.float32(NEGM))
    return np.ascontiguousarray(out.transpose(0, 3, 1, 2)).reshape(128, -1).astype(np.float32)


def _na_bias_layout(rpb):
    a = (np.arange(128) // 64)[:, None, None, None]
    kc = (np.arange(128) % 64)[:, None, None, None]
    ur = np.arange(7)[None, :, None, None]
    rr = np.arange(2)[None, None, :, None]
    qc = np.arange(64)[None, None, None, :]
    j = 13 - 2 * ur + rr - a
    dr = 14 - j
    ok_r = (j >= 0) & (j <= 14)
    dcc = np.clip(kc - qc, -15, 15) + 15
    start_c = np.clip(qc - 8, 0, 64 - 16)
    ok_c = (kc >= start_c) & (kc < start_c + 16)
    drc = np.clip(dr, 0, 14)
    drc, dccb = np.broadcast_arrays(drc, dcc)
    g = rpb[:, drc, dccb]
    ok = np.broadcast_to(ok_r & ok_c, g.shape[1:])
    g = np.where(ok[None], g, np.float32(NEGM))
    return np.ascontiguousarray(g.transpose(1, 0, 2, 3, 4)).reshape(128, -1).astype(np.float32)


def _lsel_layout():
    L = np.zeros((128, 8, 128), np.float32)
    for kt in range(8):
        for a in range(2):
            L[2 * kt + a, kt, a * 64:(a + 1) * 64] = 1.0
    return L.reshape(128, -1)


def _mask_tables(cases):
    ns = len(cases)
    wm = np.zeros((128, ns, 2), np.float32)
    mrow = np.zeros((128, ns, 8), np.float32)
    for s, cs in enumerate(cases):
        if "S" in cs:
            wm[:, s, 0] = NEGM
        if "E" in cs:
            wm[:, s, 1] = NEGM
        for r in range(8):
            rs = r - 4
            if "S" in cs:
                rs = max(rs, 0)
            if "E" in cs:
                rs = min(rs, 0)
            for i in range(16):
                kr = i - 4
                mrow[i, s, r] = 0.0 if (rs <= kr < rs + 8) else NEGM
    return wm.reshape(128, -1).astype(np.float32), mrow.reshape(128, -1).astype(np.float32)


_PROG_CACHE = {}


def _get_prog(nslots, slot_rows, slot_types, slot_cases, dbg=None):
    key = (nslots, tuple(slot_rows), tuple(slot_types), tuple(slot_cases), None if dbg is None else (dbg[0], tuple(dbg[1]), dbg[2]))
    if key not in _PROG_CACHE:
        _PROG_CACHE[key] = build_program(nslots, list(slot_rows), list(slot_types), list(slot_cases), dbg)
    return _PROG_CACHE[key]


def _common_inputs(norm_mix_g, w_in, q_norm_a, k_norm_a, t5_bias, sink_a, q_norm_b, k_norm_b, rpb_b,
                   w_br_a, w_br_b, w_out, norm_mlp_g, w_up, w_down):
    f = lambda a: np.ascontiguousarray(np.asarray(a, dtype=np.float32))
    pvec = np.zeros((128, 36), np.float32)
    pvec[:, 0:16] = f(norm_mix_g).reshape(16, 128).T
    pvec[:, 16:32] = f(norm_mlp_g).reshape(16, 128).T
    pvec[:, 32] = f(q_norm_a).reshape(128)
    pvec[:, 33] = f(k_norm_a).reshape(128)
    pvec[:, 34] = f(q_norm_b).reshape(128)
    pvec[:, 35] = f(k_norm_b).reshape(128)
    return {
        "w_in": f(w_in).reshape(D, IN_W),
        "w_bra": f(w_br_a).reshape(1024, D),
        "w_brb": f(w_br_b).reshape(1024, D),
        "w_out": f(w_out).reshape(D, D),
        "w_up": f(w_up).reshape(D, DFF),
        "w_down": f(w_down).reshape(DFF, D),
        "pvec": pvec,
        "sinkrep": np.ascontiguousarray(np.broadcast_to(f(sink_a).reshape(1, 8), (128, 8))),
        "wbias": _window_bias_layout(f(t5_bias)),
        "nbias": _na_bias_layout(f(rpb_b).reshape(8, 15, 31)),
        "ident": np.eye(128, dtype=np.float32),
        "lsel": _lsel_layout(),
    }


def kernel(x_prompt, x_sample, norm_mix_g, w_in, q_norm_a, k_norm_a, t5_bias, sink_a, q_norm_b, k_norm_b,
           rpb_b, w_br_a, w_br_b, w_out, norm_mlp_g, w_up, w_down):
    x_prompt = np.asarray(x_prompt, dtype=np.float32)
    x_sample = np.asarray(x_sample, dtype=np.float32)
    common = _common_inputs(norm_mix_g, w_in, q_norm_a, k_norm_a, t5_bias, sink_a, q_norm_b, k_norm_b, rpb_b,
                            w_br_a, w_br_b, w_out, norm_mlp_g, w_up, w_down)
    nslots = 10
    slot_rows = [512 * s for s in range(8)] + [4608, 4608 + 512]
    in_maps = []
    for c in range(NCORES):
        xin = np.zeros((4608 + 1536, D), np.float32)
        xin[256:256 + 4096] = x_prompt[c]
        sq, half = c // 2, c % 2
        if half == 0:
            xin[4608 + 256: 4608 + 1536] = x_sample[sq, 0:1280]
            scases = ["S", "I"]
        else:
            xin[4608: 4608 + 1280] = x_sample[sq, 768:2048]
            scases = ["I", "E"]
        cases = ["S"] + ["I"] * 6 + ["E"] + scases
        m = dict(common)
        m["xin"] = xin
        m["masks"], m["mrow"] = _mask_tables(cases)
        in_maps.append(m)
    slot_types = ["F"] + ["C"] * 7 + ["F", "C"]
    slot_cases = ["S"] + ["I"] * 6 + ["E"] + [None, None]
    nc = _get_prog(nslots, slot_rows, slot_types, slot_cases)
    res = run_bass_kernel_spmd(nc, in_maps, core_ids=list(range(NCORES)))
    y_prompt = np.empty((8, 4096, D), np.float32)
    y_sample = np.empty((4, 2048, D), np.float32)
    for c in range(NCORES):
        yc = np.asarray(res.results[c]["y"], dtype=np.float32)
        y_prompt[c] = yc[0:4096]
        sq, half = c // 2, c % 2
        y_sample[sq, half * 1024:(half + 1) * 1024] = yc[4096:5120]
    return (y_prompt, y_sample)
```

```python
from contextlib import ExitStack

import numpy as np
import concourse.bass as bass
import concourse.mybir as mybir
from concourse.bass_utils import run_bass_kernel_spmd

F32 = mybir.dt.float32
BF16 = mybir.dt.bfloat16
AF = mybir.ActivationFunctionType
ALU = mybir.AluOpType

D = 2048
HD = 128
IN_W = 8704
DFF = 8192
EPS = 1e-6
NEGM = -30000.0
T = 512
HALO = 256
NCORES = 8

C_QA, C_KA, C_VA, C_QB, C_KB, C_VB, C_GA, C_GB = 0, 1024, 1280, 1536, 2560, 3584, 4608, 6656

NA_KT = {0: list(range(-2, 4)), 1: list(range(-1, 4)), 2: list(range(0, 5)), 3: list(range(0, 6))}


ATTACH_WAITS = True


class Sem:
    def __init__(self, h):
        self.h = h
        self.count = 0


class Buf:
    __slots__ = ("w", "r")

    def __init__(self):
        self.w = None
        self.r = {}


class Queue:
    def __init__(self, name, sem, is_pe=False, is_dma=False):
        self.name = name
        self.sem = sem
        self.ops = []
        self.waited = {}
        self.pending = []
        self.is_pe = is_pe
        self.is_dma = is_dma

    def wait(self, tok):
        s, v = tok
        if self.waited.get(s, 0) >= v:
            return
        self.waited[s] = v
        if self.is_dma or not ATTACH_WAITS:
            self.ops.append(lambda e, s=s, v=v: e.wait_ge(s.h, v))
        else:
            self.pending.append((s, v))

    def emit(self, fn, sem=None):
        pend = self.pending
        self.pending = []
        for s, v in pend[1:]:
            self.ops.append(lambda e, s=s, v=v: e.wait_ge(s.h, v))
        first = pend[0] if pend else None

        def thunk(e, fn=fn, first=first, sem=sem):
            ins = fn(e)
            if first is not None:
                ins = ins._wait_ge(first[0].h, first[1])
            if sem is not None:
                ins = ins.then_inc(sem.h, 1)
            return ins
        self.ops.append(thunk)


def _deps(q, reads, writes):
    deps = []
    for b in reads:
        if b.w is not None:
            deps.append(b.w)
    for b in writes:
        if b.w is not None:
            deps.append(b.w)
        for s, v in b.r.items():
            deps.append((s, v))
    for tok in deps:
        if tok[0] is q.sem and (q.is_pe or q.is_dma):
            continue
        q.wait(tok)


def _record(tok, reads, writes):
    s, v = tok
    for b in reads:
        if b.r.get(s, 0) < v:
            b.r[s] = v
    for b in writes:
        b.w = tok
        b.r = {}


def op(q, fn, reads=(), writes=(), signal=True):
    _deps(q, reads, writes)
    if signal:
        q.sem.count += 1
        tok = (q.sem, q.sem.count)
        q.emit(fn, q.sem)
    else:
        assert q.is_pe
        tok = (q.sem, q.sem.count + 1)
        q.emit(fn, None)
    _record(tok, reads, writes)
    return tok


def dma(q, dsem, out_ap, in_ap, reads=(), writes=(), **kw):
    _deps(q, reads, writes)
    dsem.count += 16
    tok = (dsem, dsem.count)
    q.ops.append(lambda e, o=out_ap, i=in_ap, s=dsem, kw=kw: e.dma_start(out=o, in_=i, **kw).then_inc(s.h, 16))
    _record(tok, reads, writes)
    return tok


class Rot:
    def __init__(self, items):
        self.items = items
        self.i = 0

    def next(self):
        it = self.items[self.i % len(self.items)]
        self.i += 1
        return it


def build_program(nslots, slot_rows, slot_types, slot_cases=None, dbg=None):
    nc = bass.Bass("TRN2", target_bir_lowering=False)
    NROWS_IN = max(slot_rows) + 1024

    def din(name, shape, dt=F32):
        return nc.dram_tensor(name, list(shape), dt, kind="ExternalInput").ap()

    xin = din("xin", [NROWS_IN, D])
    w_in = din("w_in", [D, IN_W])
    w_bra = din("w_bra", [1024, D])
    w_brb = din("w_brb", [1024, D])
    w_out = din("w_out", [D, D])
    w_up = din("w_up", [D, DFF])
    w_down = din("w_down", [DFF, D])
    pvec_d = din("pvec", [128, 36])
    sink_d = din("sinkrep", [128, 8])
    wbias_d = din("wbias", [128, 8 * 3 * 128])
    nbias_d = din("nbias", [128, 8 * 7 * 128])
    lsel_d = din("lsel", [128, 8 * 128])
    mrow_d = din("mrow", [128, nslots * 8])
    masks_d = din("masks", [128, nslots * 2])
    ident_d = din("ident", [128, 128])
    y = nc.dram_tensor("y", [nslots * T, D], F32, kind="ExternalOutput").ap()
    dbg_out = None
    if dbg is not None:
        dbg_out = nc.dram_tensor("dbg", list(dbg[1]), BF16 if dbg[0] in ("A", "B") else F32, kind="ExternalOutput").ap()

    wb_in = nc.dram_tensor("wb_in", [D, IN_W], BF16).ap()
    wb_bra = nc.dram_tensor("wb_bra", [1024, D], BF16).ap()
    wb_brb = nc.dram_tensor("wb_brb", [1024, D], BF16).ap()
    wb_out = nc.dram_tensor("wb_out", [D, D], BF16).ap()
    wb_up = nc.dram_tensor("wb_up", [D, DFF], BF16).ap()
    wb_down = nc.dram_tensor("wb_down", [DFF, D], BF16).ap()

    with ExitStack() as es:
        def sb(name, shape, dt):
            return es.enter_context(nc.sbuf_tensor(name, list(shape), dt))

        def mksem(name):
            return Sem(es.enter_context(nc.semaphore(name)))

        xres = sb("xres", [128, 4, D], F32)
        xstage = sb("xstage", [128, D], F32)
        hbf = [sb(f"hbf{i}", [128, D], BF16) for i in range(2)]
        hTc = sb("hTc", [128, 16, T], BF16)
        hTh = sb("hTh", [128, 16, 512], BF16)
        arena = sb("arena", [128, 16384], BF16)
        KaT = sb("KaT", [128, 2, 768], BF16)
        Va = sb("Va", [128, 6, 2, 130], BF16)
        Vb = sb("Vb", [128, 8, 8, 130], BF16)
        E_w = sb("E_w", [128, 8, 3, 128], BF16)
        E_n = sb("E_n", [128, 8, 7, 128], BF16)
        masks = sb("masks_sb", [128, nslots * 2], F32)
        Lsel = sb("Lsel", [128, 8, 128], BF16)
        Mrow = sb("Mrow", [128, nslots * 8], BF16)
        pvec = sb("pvecs", [128, 36], F32)
        qsc = sb("qsc", [128, 4], F32)
        expsink = sb("expsink", [128, 8], F32)
        ident = sb("ident_b", [128, 128], BF16)
        ones_b = sb("ones_b", [128, 128], BF16)
        onecol = sb("onecol", [128, 2], F32)
        sqh = [sb(f"sqh{i}", [128, 512], BF16) for i in range(2)]
        wbuf = [sb(f"wbuf{i}", [128, 16, 512], BF16) for i in range(2)]
        tmpf = sb("tmpf", [128, 2048], F32)
        sqb = [tmpf[:, 0:512], tmpf[:, 512:1024]]
        lnb = [tmpf[:, 1024:1536], tmpf[:, 1536:2048]]
        expS = [tmpf[:, 0:768], tmpf[:, 1024:1792]]
        PT = [sb(f"PT{i}", [128, 6, 128], BF16) for i in range(3)]
        Onb = [sb(f"On{i}", [128, 128], BF16) for i in range(2)]
        relu_t = sqb
        small = sb("small", [128, 64], F32)
        gt_ap = hTh[:].rearrange("p a b -> p (a b)").bitcast(F32)

        psf = [es.enter_context(nc.psum_tensor(f"psf{i}", [128, 512], F32)) for i in range(6)]
        psb = [es.enter_context(nc.psum_tensor(f"psb{i}", [128, 1024], BF16)) for i in range(2)]
        psf_rot = Rot([(psf[i], Buf()) for i in range(6)])
        psb_rot = Rot([(psb[i], Buf()) for i in range(2)])

        PE = Queue("pe", mksem("s_pe"), is_pe=True)
        ACT = Queue("act", mksem("s_act"))
        DVE = Queue("dve", mksem("s_dve"))
        POOL = Queue("pool", mksem("s_pool"), is_dma=True)
        SP = Queue("sp", mksem("s_sp"), is_dma=True)
        ds_w = [mksem(f"ds_w{i}") for i in range(2)]
        ds_xs = mksem("ds_xs")
        ds_xr = [mksem(f"ds_xr{i}") for i in range(4)]
        ds_st = [mksem(f"ds_st{i}") for i in range(4)]
        ds_c = [mksem(f"ds_c{i}") for i in range(8)]
        ds_dbg = mksem("ds_dbg")

        B_xres = [Buf() for _ in range(4)]
        B_xstage = Buf()
        B_hbf = [Buf(), Buf()]
        B_hTc = [Buf() for _ in range(4)]
        B_hTh = [Buf() for _ in range(4)]
        B_QT = [[Buf() for _ in range(4)] for _ in range(16)]
        B_KbT_lo = [Buf() for _ in range(8)]
        B_KbT_hi = [Buf() for _ in range(8)]
        B_KbT = B_KbT_lo + B_KbT_hi
        B_KaT = [Buf() for _ in range(2)]
        B_Va = [Buf() for _ in range(6)]
        B_Vb = [[Buf() for _ in range(2)] for _ in range(8)]
        B_wbuf = [Buf(), Buf()]
        B_sq = [Buf(), Buf()]
        B_sqh = [Buf(), Buf()]
        B_ln = [Buf(), Buf()]
        B_expS = [B_sq, B_ln]
        B_PT = [Buf(), Buf(), Buf()]
        B_On = [Buf(), Buf()]
        B_relu = B_sq
        B_small = [Buf() for _ in range(16)]
        B_const = Buf()
        B_gt = [Buf() for _ in range(8)]
        all_QT = [b for hb in B_QT for b in hb]

        QT = arena[:, 0:8192].rearrange("p (h t) -> p h t", h=16)
        KbT = arena[:, 8192:16384].rearrange("p (h t) -> p h t", h=8)
        actT = arena[:, 0:8192].rearrange("p (c t) -> p c t", c=16)
        Vb_lo_flat = Vb[:, 0:4, :, :].rearrange("p a b c -> p (a b c)")
        ds_sh = [mksem(f"ds_sh{i}") for i in range(4)]

        def merged_ap(c):
            if c < 8:
                return KbT[:, c, 0:512]
            return Vb_lo_flat[:, (c - 8) * 512:(c - 7) * 512]

        B_Vb_lo = [b_ for t_ in range(4) for b_ in B_Vb[t_]]

        def B_merged(c):
            return [B_KbT_lo[c]] if c < 8 else B_Vb_lo

        B_merged_all = B_KbT_lo + B_Vb_lo
        B_act = all_QT

        CAST = {}
        cast_jobs = []

        def cjob(key, dst, src):
            b = Buf()
            CAST[key] = b
            cast_jobs.append((key, dst, src, b))

        for i in range(9):
            cjob(("in", i), wb_in[:, i * 512:(i + 1) * 512], w_in[:, i * 512:(i + 1) * 512])
        for n in range(4):
            cjob(("in", 9 + n), wb_in[:, C_GA + n * 512: C_GA + (n + 1) * 512], w_in[:, C_GA + n * 512: C_GA + (n + 1) * 512])
            cjob(("in", 13 + n), wb_in[:, C_GB + n * 512: C_GB + (n + 1) * 512], w_in[:, C_GB + n * 512: C_GB + (n + 1) * 512])
            cjob(("bra", n), wb_bra[:, n * 512:(n + 1) * 512], w_bra[:, n * 512:(n + 1) * 512])
            cjob(("brb", n), wb_brb[:, n * 512:(n + 1) * 512], w_brb[:, n * 512:(n + 1) * 512])
        for n in range(4):
            cjob(("out", n), wb_out[:, n * 512:(n + 1) * 512], w_out[:, n * 512:(n + 1) * 512])
        for fq in range(4):
            for i in range(4):
                c0 = fq * 2048 + i * 512
                cjob(("up", fq * 4 + i), wb_up[:, c0:c0 + 512], w_up[:, c0:c0 + 512])
            for n in range(4):
                cjob(("down", fq * 4 + n), wb_down[fq * 2048:(fq + 1) * 2048, n * 512:(n + 1) * 512],
                     w_down[fq * 2048:(fq + 1) * 2048, n * 512:(n + 1) * 512])

        def issue_casts():
            toks = []
            for k, (key, dst, src, b) in enumerate(cast_jobs):
                if k >= 12:
                    POOL.wait(toks[k - 12])
                sm_ = mksem(f"ds_cast_{key[0]}{key[1]}")
                toks.append(dma(POOL, sm_, dst, src, writes=[b]))

        for tt in range(4):
            dma(POOL, ds_xr[tt], xres[:, tt, :], xin[slot_rows[0] + (tt + 2) * 128: slot_rows[0] + (tt + 3) * 128, :],
                writes=[B_xres[tt]])
        P_ = []

        def cb():
            b_ = Buf()
            P_.append(b_)
            return b_

        b_pvec, b_sink = cb(), cb()
        dma(SP, ds_c[0], pvec[:], pvec_d, writes=[b_pvec])
        dma(SP, ds_c[1], expsink[:], sink_d, writes=[b_sink])
        dma(SP, ds_c[2], masks[:], masks_d, writes=[cb()])
        dma(POOL, ds_c[3], ident[:], ident_d, writes=[cb()])
        dma(POOL, ds_c[4], E_w[:].rearrange("p a b c -> p (a b c)"), wbias_d, writes=[cb()], max_dma_last_dim=4096)
        dma(POOL, ds_c[5], E_n[:].rearrange("p a b c -> p (a b c)"), nbias_d, writes=[cb()], max_dma_last_dim=4096)
        dma(POOL, ds_c[6], Lsel[:].rearrange("p a b -> p (a b)"), lsel_d, writes=[cb()])
        dma(POOL, ds_c[7], Mrow[:], mrow_d, writes=[cb()])
        op(ACT, lambda e: e.activation(out=expsink[:], in_=expsink[:], func=AF.Exp), reads=[b_sink], writes=[b_sink])
        op(DVE, lambda e: e.memset(ones_b[:], 1.0), writes=[cb()])
        op(DVE, lambda e: e.memset(onecol[:], 1.0), writes=[cb()])
        op(DVE, lambda e: e.memset(Va[:].rearrange("p a b c -> p (a b c)"), 1.0), writes=B_Va)
        op(DVE, lambda e: e.memset(Vb[:].rearrange("p a b c -> p (a b c)"), 1.0), writes=[b for x in B_Vb for b in x])
        sc = float(HD) ** -0.5
        b_qsc = cb()
        op(DVE, lambda e: e.tensor_scalar(out=qsc[:, 0:1], in0=pvec[:, 32:33], scalar1=sc, scalar2=None, op0=ALU.mult),
           reads=[b_pvec], writes=[b_qsc])
        op(DVE, lambda e: e.tensor_copy(out=qsc[:, 1:2], in_=pvec[:, 33:34]), reads=[b_pvec], writes=[b_qsc])
        op(DVE, lambda e: e.tensor_scalar(out=qsc[:, 2:3], in0=pvec[:, 34:35], scalar1=sc, scalar2=None, op0=ALU.mult),
           reads=[b_pvec], writes=[b_qsc])
        op(DVE, lambda e: e.tensor_copy(out=qsc[:, 3:4], in_=pvec[:, 35:36]), reads=[b_pvec], writes=[b_qsc])

        wctr = [0]

        def load_w(src_ap, kcs, cast_key):
            i = wctr[0] % 2
            wctr[0] += 1
            dma(SP, ds_w[i], wbuf[i][:, 0:kcs, :], src_ap.rearrange("(kc p) n -> p kc n", p=128),
                reads=[CAST[cast_key]], writes=[B_wbuf[i]])
            return wbuf[i], B_wbuf[i]

        sctr = [0]

        def small_col():
            i = sctr[0] % 16
            sctr[0] += 1
            return small[:, 4 * i:4 * i + 4], B_small[i]

        rot2 = {"sq": 0, "ln": 0, "expS": 0, "PT": 0, "On": 0, "relu": 0, "hbf": 0}

        def nxt(k):
            i = rot2[k] % 2
            rot2[k] += 1
            return i

        pend = []

        def flush(keep=0):
            while len(pend) > keep:
                pend.pop(0)()

        def rms1(x_ap, xbuf):
            hi = nxt("hbf")
            sm, smb = small_col()
            op(ACT, lambda e: e.activation(out=hbf[hi][:], in_=x_ap, func=AF.Square, accum_out=sm[:, 0:1]),
               reads=[xbuf], writes=[B_hbf[hi], smb])
            op(ACT, lambda e: e.activation(out=sm[:, 1:2], in_=sm[:, 0:1], func=AF.Ln, scale=1.0 / D, bias=sm[:, 3:4]),
               reads=[smb], writes=[smb])
            op(ACT, lambda e: e.activation(out=sm[:, 2:3], in_=sm[:, 1:2], func=AF.Exp, scale=-0.5),
               reads=[smb], writes=[smb])
            op(DVE, lambda e: e.tensor_scalar(out=hbf[hi][:], in0=x_ap, scalar1=sm[:, 2:3], scalar2=None, op0=ALU.mult),
               reads=[xbuf, smb], writes=[B_hbf[hi]])
            return hi

        def rms2(hi, gcol0, dst_ap_fn, dst_bufs):
            for half in range(2):
                pb, pbb = psb_rot.next()
                for cc in range(8):
                    ch = half * 8 + cc
                    op(PE, lambda e, pb=pb, cc=cc, ch=ch: e.transpose(out=pb[:, cc * 128:(cc + 1) * 128],
                                                                     in_=hbf[hi][:, ch * 128:(ch + 1) * 128],
                                                                     identity=ident[:]),
                       reads=[B_hbf[hi], B_const], writes=[pbb], signal=(cc == 7))
                g_b = pvec[:, gcol0 + half * 8: gcol0 + half * 8 + 8].unsqueeze(2).to_broadcast([128, 8, 128])
                op(DVE, lambda e, pb=pb, half=half, g_b=g_b: e.tensor_tensor(
                    out=dst_ap_fn(half), in0=pb[:].rearrange("p (c t) -> p c t", c=8), in1=g_b, op=ALU.mult),
                   reads=[pbb, B_const], writes=dst_bufs)

        def prologue_steps(sn):
            r0n = slot_rows[sn]
            tiles = (0, 1, 6, 7, 2, 3, 4, 5) if slot_types[sn] == "F" else (6, 7, 2, 3, 4, 5)
            nt = len(tiles)
            p1s, p2s = [], []
            for t in tiles:
                def p1(t=t):
                    if sn == 0 and 2 <= t <= 5:
                        return rms1(xres[:, t - 2, :], B_xres[t - 2])
                    dma(SP, ds_xs, xstage[:], xin[r0n + t * 128: r0n + (t + 1) * 128, :], writes=[B_xstage])
                    return rms1(xstage[:], B_xstage)

                def p2(hi, t=t):
                    if 2 <= t <= 5:
                        tt = t - 2
                        rms2(hi, 0, lambda half: hTc[:, half * 8:(half + 1) * 8, tt * 128:(tt + 1) * 128], [B_hTc[tt]])
                    else:
                        hh = t if t < 2 else t - 4
                        rms2(hi, 0, lambda half: hTh[:, half * 8:(half + 1) * 8, hh * 128:(hh + 1) * 128],
                             [B_hTh[hh]] + B_gt)
                p1s.append(p1)
                p2s.append(p2)
            ctx = {}
            steps = []
            for k in range(nt + 1):
                def step(k=k):
                    if 1 <= k <= nt:
                        p2s[k - 1](ctx[k - 1])
                    if k <= nt - 1:
                        ctx[k] = p1s[k]()
                steps.append(step)
            return steps

        def qknorm(ps, psbuf, n, gcol, out_ap, out_bufs):
            si = nxt("sq")
            li = nxt("ln")
            op(ACT, lambda e: e.activation(out=sqh[si][:, 0:n], in_=ps[:, 0:n], func=AF.Square),
               reads=[psbuf], writes=[B_sqh[si]])
            ps2, ps2b = psf_rot.next()

            def pe_part():
                op(PE, lambda e: e.matmul(ps2[:, 0:n], lhsT=ones_b[:], rhs=sqh[si][:, 0:n], start=True, stop=True),
                   reads=[B_sqh[si], B_const], writes=[ps2b])
                op(ACT, lambda e: e.activation(out=lnb[li][:, 0:n], in_=ps2[:, 0:n], func=AF.Ln, scale=1.0 / HD,
                                               bias=small[:, 63:64]),
                   reads=[ps2b, B_const], writes=[B_ln[li]])
                op(ACT, lambda e: e.activation(out=lnb[li][:, 0:n], in_=lnb[li][:, 0:n], func=AF.Exp, scale=-0.5),
                   reads=[B_ln[li]], writes=[B_ln[li]])
                op(DVE, lambda e: e.scalar_tensor_tensor(out=out_ap, in0=ps[:, 0:n], scalar=qsc[:, gcol:gcol + 1],
                                                          in1=lnb[li][:, 0:n], op0=ALU.mult, op1=ALU.mult),
                   reads=[psbuf, B_ln[li], B_const], writes=out_bufs)
            pend.append(pe_part)

        op(DVE, lambda e: e.memset(small[:], EPS), writes=B_small)
        op(DVE, lambda e: e.memset(relu_t[0][:, 0:1], 0.0), reads=P_, writes=[B_const, B_sq[0]])

        deferred_stores = []
        for s in range(nslots):
            r0 = slot_rows[s]
            mw0 = s * 2
            mn0 = nslots * 2 + s * 64

            if s == 0:
                for stp in prologue_steps(0):
                    stp()
                issue_casts()
            nxt_steps = prologue_steps(s + 1) if s + 1 < nslots else []
            early_steps = nxt_steps[:-4]
            late_steps = nxt_steps[-4:]
            assert len(early_steps) <= 5

            def pro_early():
                if early_steps:
                    early_steps.pop(0)()

            def pro_late():
                if late_steps:
                    late_steps.pop(0)()

            seg_c = (lambda kc: hTc[:, kc, :], 512, B_hTc)
            seg_b = (lambda kc: hTh[:, kc, 0:256], 256, B_hTh[0:2])
            seg_a = (lambda kc: hTh[:, kc, 256:512], 256, B_hTh[2:4])

            def proj_fm(wt, wtb, j, seg, then):
                rhs_fn, n, hb = seg
                ps, psbuf = psf_rot.next()
                for kc in range(16):
                    op(PE, lambda e, kc=kc: e.matmul(ps[:, 0:n], lhsT=wt[:, kc, j * 128:(j + 1) * 128], rhs=rhs_fn(kc),
                                                     start=(kc == 0), stop=(kc == 15)),
                       reads=[wtb] + list(hb), writes=[psbuf], signal=(kc == 15))
                flush(0)
                then(ps, psbuf, n)

            for wi in range(2):
                wt, wtb = load_w(wb_in[:, C_QA + wi * 512: C_QA + (wi + 1) * 512], 16, ("in", wi))
                for j in range(4):
                    h = wi * 4 + j
                    proj_fm(wt, wtb, j, seg_c,
                            lambda ps, pb, n, h=h: qknorm(ps, pb, n, 0, QT[:, h, :], B_QT[h]))
            isF = slot_types[s] == "F"
            if not isF:
                assert slot_rows[s] == slot_rows[s - 1] + 512
                dma(SP, ds_sh[0], KbT[:, :, 0:512], KbT[:, :, 512:1024], reads=B_KbT_hi, writes=B_KbT_lo)
                dma(SP, ds_sh[1], Vb[:, 0:4, :, :], Vb[:, 4:8, :, :], reads=[b_ for t_ in range(4, 8) for b_ in B_Vb[t_]],
                    writes=B_Vb_lo)
                dma(SP, ds_sh[2], KaT[:, :, 0:256], KaT[:, :, 512:768], reads=B_KaT, writes=B_KaT)
                dma(SP, ds_sh[3], Va[:, 0:2, :, :], Va[:, 4:6, :, :], reads=B_Va[4:6], writes=B_Va[0:2])
            else:
                op(DVE, lambda e: e.memset(Vb[:, 0:4, :, 128:130], 1.0), writes=B_Vb_lo)
            wt, wtb = load_w(wb_in[:, C_KA: C_KA + 512], 16, ("in", 2))
            for j in range(2):
                if isF:
                    proj_fm(wt, wtb, j, (lambda kc: hTh[:, kc, 128:256], 128, [B_hTh[1]]),
                            lambda ps, pb, n, j=j: qknorm(ps, pb, n, 1, KaT[:, j, 0:128], [B_KaT[j]]))
                    proj_fm(wt, wtb, j, seg_c,
                            lambda ps, pb, n, j=j: qknorm(ps, pb, n, 1, KaT[:, j, 128:640], [B_KaT[j]]))
                else:
                    proj_fm(wt, wtb, j, (lambda kc: hTc[:, kc, 128:512], 384, B_hTc[1:4]),
                            lambda ps, pb, n, j=j: qknorm(ps, pb, n, 1, KaT[:, j, 256:640], [B_KaT[j]]))
                proj_fm(wt, wtb, j, (lambda kc: hTh[:, kc, 256:384], 128, [B_hTh[2]]),
                        lambda ps, pb, n, j=j: qknorm(ps, pb, n, 1, KaT[:, j, 640:768], [B_KaT[j]]))

            def tok_lhsT(t):
                if t < 2:
                    return (lambda kc: hTh[:, kc, t * 128:(t + 1) * 128]), B_hTh[t]
                if t >= 6:
                    return (lambda kc: hTh[:, kc, (t - 4) * 128:(t - 3) * 128]), B_hTh[t - 4]
                return (lambda kc: hTc[:, kc, (t - 2) * 128:(t - 1) * 128]), B_hTc[t - 2]

            flush(0)
            for vt in (range(6) if isF else range(2, 6)):
                lf, lb = tok_lhsT(vt + 1)
                ps, psbuf = psf_rot.next()
                for kc in range(16):
                    op(PE, lambda e, kc=kc, lf=lf, ps=ps, wt=wt: e.matmul(ps[:, 0:256], lhsT=lf(kc), rhs=wt[:, kc, 256:512],
                                                                  start=(kc == 0), stop=(kc == 15)),
                       reads=[wtb, lb], writes=[psbuf], signal=(kc == 15))
                op(ACT, lambda e, ps=ps, vt=vt: e.activation(out=Va[:, vt, :, 0:128],
                                                            in_=ps[:, 0:256].rearrange("p (h d) -> p h d", h=2),
                                                            func=AF.Copy),
                   reads=[psbuf], writes=[B_Va[vt]])
            for wi in range(2):
                wt, wtb = load_w(wb_in[:, C_QB + wi * 512: C_QB + (wi + 1) * 512], 16, ("in", 3 + wi))
                for j in range(4):
                    h = 8 + wi * 4 + j
                    proj_fm(wt, wtb, j, seg_c,
                            lambda ps, pb, n, h=h: qknorm(ps, pb, n, 2, QT[:, h, :], B_QT[h]))
            for wi in range(2):
                wt, wtb = load_w(wb_in[:, C_KB + wi * 512: C_KB + (wi + 1) * 512], 16, ("in", 5 + wi))
                for j in range(4):
                    h = wi * 4 + j
                    if isF:
                        proj_fm(wt, wtb, j, seg_b,
                                lambda ps, pb, n, h=h: qknorm(ps, pb, n, 3, KbT[:, h, 0:256], [B_KbT_lo[h]]))
                        proj_fm(wt, wtb, j, seg_c,
                                lambda ps, pb, n, h=h: qknorm(ps, pb, n, 3, KbT[:, h, 256:768], [B_KbT_lo[h], B_KbT_hi[h]]))
                    else:
                        proj_fm(wt, wtb, j, (lambda kc: hTc[:, kc, 256:512], 256, B_hTc[2:4]),
                                lambda ps, pb, n, h=h: qknorm(ps, pb, n, 3, KbT[:, h, 512:768], [B_KbT_hi[h]]))
                    proj_fm(wt, wtb, j, seg_a,
                            lambda ps, pb, n, h=h: qknorm(ps, pb, n, 3, KbT[:, h, 768:1024], [B_KbT_hi[h]]))
            flush(0)
            for wi in range(2):
                wt, wtb = load_w(wb_in[:, C_VB + wi * 512: C_VB + (wi + 1) * 512], 16, ("in", 7 + wi))
                for vt in (range(8) if isF else range(4, 8)):
                    lf, lb = tok_lhsT(vt)
                    ps, psbuf = psf_rot.next()
                    for kc in range(16):
                        op(PE, lambda e, kc=kc, lf=lf, ps=ps, wt=wt: e.matmul(ps[:, 0:512], lhsT=lf(kc), rhs=wt[:, kc, :],
                                                                      start=(kc == 0), stop=(kc == 15)),
                           reads=[wtb, lb], writes=[psbuf], signal=(kc == 15))
                    op(ACT, lambda e, ps=ps, vt=vt, wi=wi: e.activation(
                        out=Vb[:, vt, wi * 4:(wi + 1) * 4, 0:128],
                        in_=ps[:, 0:512].rearrange("p (h d) -> p h d", h=4), func=AF.Copy),
                       reads=[psbuf], writes=[B_Vb[vt][wi]])

            if dbg is not None and dbg[0] == "A" and s == dbg[2]:
                dma(POOL, ds_dbg, dbg_out[:, 0:16384], arena[:], reads=all_QT + B_KbT)
                dma(POOL, ds_dbg, dbg_out[:, 16384:16384 + 1536], KaT[:].rearrange("p a b -> p (a b)"), reads=B_KaT)
                dma(POOL, ds_dbg, dbg_out[:, 17920:17920 + 1560], Va[:].rearrange("p a b c -> p (a b c)"), reads=B_Va)
                dma(POOL, ds_dbg, dbg_out[:, 19480:19480 + 8320], Vb[:].rearrange("p a b c -> p (a b c)"),
                    reads=[b for x in B_Vb for b in x])

            if deferred_stores:
                POOL.wait(wtb.w)
                deferred_stores.pop(0)()
            if s > 0:
                for tt in range(4):
                    dma(POOL, ds_xr[tt], xres[:, tt, :], xin[r0 + (tt + 2) * 128: r0 + (tt + 3) * 128, :], writes=[B_xres[tt]])

            items = []
            for h in range(8):
                for qb in range(4):
                    items.append(("w", h, qb))
            for h in range(8):
                for qt in range(4):
                    items.append(("n", h, qt))
            st = {}

            def stage1(it):
                kind, h, q = it
                d = {}
                st[it] = d
                ei = nxt("expS")
                pi = rot2["PT"] % 3
                rot2["PT"] += 1
                d["ei"], d["pi"] = ei, pi
                tmp = expS[ei]
                if kind == "w":
                    g = h // 4
                    kbs = [q - 1, q, q + 1]
                    ps, psbuf = psf_rot.next()
                    for i, kb in enumerate(kbs):
                        op(PE, lambda e, i=i, kb=kb: e.matmul(ps[:, i * 128:(i + 1) * 128],
                                                              lhsT=KaT[:, g, (kb + 1) * 128:(kb + 2) * 128],
                                                              rhs=QT[:, h, q * 128:(q + 1) * 128], start=True, stop=True),
                           reads=[B_KaT[g], B_QT[h][q]], writes=[psbuf], signal=(i == 2))
                    op(DVE, lambda e: e.tensor_tensor(out=tmp[:, 0:384], in0=ps[:, 0:384],
                                                      in1=E_w[:, h, :, :].rearrange("p a b -> p (a b)"), op=ALU.add),
                       reads=[psbuf, B_const], writes=B_expS[ei])
                    if q == 0:
                        segs = [(0, 1, masks[:, mw0:mw0 + 1]), (1, 3, None)]
                    elif q == 3:
                        segs = [(0, 2, None), (2, 3, masks[:, mw0 + 1:mw0 + 2])]
                    else:
                        segs = [(0, 3, None)]
                    for a_, b_, bias in segs:
                        if bias is None:
                            op(ACT, lambda e, a_=a_, b_=b_: e.activation(
                                out=PT[pi][:, a_:b_, :].rearrange("p a b -> p (a b)"), in_=tmp[:, a_ * 128:b_ * 128], func=AF.Exp),
                               reads=B_expS[ei], writes=[B_PT[pi]])
                        else:
                            op(ACT, lambda e, a_=a_, b_=b_, bias=bias: e.activation(
                                out=PT[pi][:, a_:b_, :].rearrange("p a b -> p (a b)"), in_=tmp[:, a_ * 128:b_ * 128], func=AF.Exp,
                                bias=bias),
                               reads=B_expS[ei] + [B_const], writes=[B_PT[pi]])
                    d["nk"] = 3
                    d["v"] = [(Va[:, kb + 1, g, 0:129], B_Va[kb + 1]) for kb in kbs]
                else:
                    kts = NA_KT[q]
                    need_mask = {kt: True for kt in kts}
                    case = None if slot_cases is None else slot_cases[s]
                    if case is not None:
                        def _valid(kr, r):
                            rs = r - 4
                            if "S" in case:
                                rs = max(rs, 0)
                            if "E" in case:
                                rs = min(rs, 0)
                            return rs <= kr < rs + 8
                        keep = []
                        for kt in kts:
                            v = [_valid(2 * kt + a_, 2 * q + rr_) for a_ in range(2) for rr_ in range(2)]
                            if any(v):
                                keep.append(kt)
                                need_mask[kt] = not all(v)
                        assert keep == list(range(keep[0], keep[-1] + 1))
                        kts = keep
                    nk = len(kts)
                    banks = [psf_rot.next(), psf_rot.next()]
                    mr = Mrow[:, s * 8 + 2 * q: s * 8 + 2 * q + 2].unsqueeze(2).to_broadcast([128, 2, 64])
                    for i, kt in enumerate(kts):
                        ps, psbuf = banks[i // 4]
                        co = (i % 4) * 128
                        op(PE, lambda e, ps=ps, co=co, kt=kt: e.matmul(ps[:, co:co + 128],
                                                                      lhsT=KbT[:, h, (kt + 2) * 128:(kt + 3) * 128],
                                                                      rhs=QT[:, 8 + h, q * 128:(q + 1) * 128],
                                                                      start=True, stop=(not need_mask[kt])),
                           reads=[(B_KbT_lo if kt + 2 < 4 else B_KbT_hi)[h], B_QT[8 + h][q]], writes=[psbuf],
                           signal=((not need_mask[kt]) and (i % 4 == 3 or i == nk - 1)))
                        if not need_mask[kt]:
                            continue
                        op(PE, lambda e, ps=ps, co=co, kt=kt: e.matmul(ps[:, co:co + 128].rearrange("p (a b) -> p a b", a=2),
                                                                      lhsT=Lsel[:, kt + 2, :], rhs=mr,
                                                                      start=False, stop=True),
                           reads=[B_const], writes=[psbuf], signal=(i % 4 == 3 or i == nk - 1))
                    ur0 = 3 - q + kts[0]
                    for bi_ in range(2):
                        n_ = min(nk - bi_ * 4, 4)
                        if n_ <= 0:
                            continue
                        ps, psbuf = banks[bi_]
                        op(DVE, lambda e, ps=ps, bi_=bi_, n_=n_: e.tensor_tensor(
                            out=tmp[:, bi_ * 512: bi_ * 512 + n_ * 128], in0=ps[:, 0:n_ * 128],
                            in1=E_n[:, h, ur0 + bi_ * 4: ur0 + bi_ * 4 + n_, :].rearrange("p a b -> p (a b)"), op=ALU.add),
                           reads=[psbuf, B_const], writes=B_expS[ei])
                    op(ACT, lambda e: e.activation(out=PT[pi][:, 0:nk, :].rearrange("p a b -> p (a b)"),
                                                   in_=tmp[:, 0:nk * 128], func=AF.Exp),
                       reads=B_expS[ei], writes=[B_PT[pi]])
                    d["nk"] = nk
                    d["v"] = [(Vb[:, kt + 2, h, 0:129], B_Vb[kt + 2][h // 4]) for kt in kts]

            def stage2(it):
                kind, h, q = it
                d = st[it]
                pi = d["pi"]
                ps, psbuf = psf_rot.next()
                nk = d["nk"]
                for i in range(nk):
                    vap, vb = d["v"][i]
                    op(PE, lambda e, i=i, vap=vap: e.matmul(ps[:, 0:129], lhsT=PT[pi][:, i, :], rhs=vap,
                                                           start=(i == 0), stop=(i == nk - 1)),
                       reads=[B_PT[pi], vb], writes=[psbuf], signal=(i == nk - 1))
                sm, smb = small_col()
                if kind == "w":
                    op(DVE, lambda e: e.tensor_scalar(out=sm[:, 0:1], in0=ps[:, 128:129], scalar1=expsink[:, h:h + 1],
                                                      scalar2=None, op0=ALU.add),
                       reads=[psbuf, B_const], writes=[smb])
                    op(DVE, lambda e: e.reciprocal(out=sm[:, 1:2], in_=sm[:, 0:1]), reads=[smb], writes=[smb])
                else:
                    op(DVE, lambda e: e.reciprocal(out=sm[:, 1:2], in_=ps[:, 128:129]), reads=[psbuf], writes=[smb])
                oi = nxt("On")
                d["oi"] = oi
                op(ACT, lambda e: e.activation(out=Onb[oi][:], in_=ps[:, 0:128], func=AF.Copy, scale=sm[:, 1:2]),
                   reads=[psbuf, smb], writes=[B_On[oi]])

            def stage3(it):
                kind, h, q = it
                d = st[it]
                oi = d["oi"]
                hh = h if kind == "w" else 8 + h
                pb, pbb = psb_rot.next()
                op(PE, lambda e: e.transpose(out=pb[:, 0:128], in_=Onb[oi][:], identity=ident[:]),
                   reads=[B_On[oi], B_const], writes=[pbb])
                op(ACT, lambda e: e.activation(out=QT[:, hh, q * 128:(q + 1) * 128], in_=pb[:, 0:128], func=AF.Copy),
                   reads=[pbb], writes=[B_QT[hh][q]])
                del st[it]

            n_it = len(items)
            for step in range(n_it + 3):
                if step < n_it:
                    stage1(items[step])
                if 0 <= step - 2 < n_it:
                    stage2(items[step - 2])
                if 0 <= step - 3 < n_it:
                    stage3(items[step - 3])

            if dbg is not None and dbg[0] == "B" and s == dbg[2]:
                dma(POOL, ds_dbg, dbg_out, arena[:, 0:8192], reads=all_QT)

            gt = gt_ap.rearrange("p (g j t) -> p g j t", g=2, j=4)
            for n in range(4):
                for gi, c0 in enumerate((C_GA, C_GB)):
                    wt, wtb = load_w(wb_in[:, c0 + n * 512: c0 + (n + 1) * 512], 16, ("in", 9 + 4 * gi + n))
                    for j in range(4):
                        ps, psbuf = psf_rot.next()
                        for kc in range(16):
                            op(PE, lambda e, kc=kc, ps=ps, j=j, wt=wt: e.matmul(ps[:, :], lhsT=wt[:, kc, j * 128:(j + 1) * 128],
                                                                              rhs=hTc[:, kc, :], start=(kc == 0),
                                                                              stop=(kc == 15)),
                               reads=[wtb] + B_hTc, writes=[psbuf], signal=(kc == 15))
                        gb_ = B_gt[gi * 4 + j]
                        gap = gt[:, gi, j, :]
                        op(ACT, lambda e, ps=ps, gap=gap: e.activation(out=gap, in_=ps[:, :], func=AF.Exp, scale=-1.0),
                           reads=[psbuf], writes=[gb_] + B_hTh)
                        op(ACT, lambda e, gap=gap: e.activation(out=gap, in_=gap, func=AF.Ln, bias=onecol[:, 0:1]),
                           reads=[gb_, B_const], writes=[gb_])
                        op(ACT, lambda e, gap=gap: e.activation(out=gap, in_=gap, func=AF.Exp, scale=-1.0),
                           reads=[gb_], writes=[gb_])
                for bi, wsrc in enumerate((wb_bra, wb_brb)):
                    wt, wtb = load_w(wsrc[:, n * 512:(n + 1) * 512], 8, (("bra", "brb")[bi], n))
                    for j in range(4):
                        ps, psbuf = psf_rot.next()
                        for hh in range(8):
                            op(PE, lambda e, hh=hh, ps=ps, j=j, wt=wt, bi=bi: e.matmul(
                                ps[:, :], lhsT=wt[:, hh, j * 128:(j + 1) * 128], rhs=QT[:, bi * 8 + hh, :],
                                start=(hh == 0), stop=(hh == 7)),
                               reads=[wtb] + B_QT[bi * 8 + hh], writes=[psbuf],
                               signal=(hh == 7))
                        ga_b, gb_b = B_gt[j], B_gt[4 + j]
                        if bi == 0:
                            op(DVE, lambda e, ps=ps, j=j: e.tensor_tensor(out=gt[:, 0, j, :], in0=ps[:, :], in1=gt[:, 0, j, :],
                                                                        op=ALU.mult),
                               reads=[psbuf, ga_b], writes=[ga_b])
                        else:
                            cm = n * 4 + j
                            op(DVE, lambda e, ps=ps, j=j: e.tensor_tensor(out=gt[:, 1, j, :], in0=ps[:, :], in1=gt[:, 1, j, :],
                                                                        op=ALU.mult),
                               reads=[psbuf, gb_b], writes=[gb_b])
                            op(DVE, lambda e, j=j, cm=cm: e.tensor_tensor(out=merged_ap(cm), in0=gt[:, 0, j, :],
                                                                        in1=gt[:, 1, j, :], op=ALU.add),
                               reads=[ga_b, gb_b], writes=B_merged(cm))

            ctx2 = {}

            def mlp_rms2(tt):
                rms2(ctx2[tt], 16, lambda half: hTc[:, half * 8:(half + 1) * 8, tt * 128:(tt + 1) * 128], [B_hTc[tt]])

            for n in range(4):
                wt, wtb = load_w(wb_out[:, n * 512:(n + 1) * 512], 16, ("out", n))
                for tt in range(4):
                    ps, psbuf = psf_rot.next()
                    for kc in range(16):
                        op(PE, lambda e, kc=kc, ps=ps, tt=tt, wt=wt: e.matmul(ps[:, :], lhsT=merged_ap(kc)[:, tt * 128:(tt + 1) * 128],
                                                                            rhs=wt[:, kc, :], start=(kc == 0), stop=(kc == 15)),
                           reads=[wtb] + B_merged(kc), writes=[psbuf], signal=(kc == 15))
                    if n == 3 and tt >= 2:
                        mlp_rms2(tt - 2)
                    op(DVE, lambda e, ps=ps, tt=tt, n=n: e.tensor_tensor(out=xres[:, tt, n * 512:(n + 1) * 512], in0=ps[:, :],
                                                                       in1=xres[:, tt, n * 512:(n + 1) * 512], op=ALU.add),
                       reads=[psbuf, B_xres[tt]], writes=[B_xres[tt]])
                    if n == 3:
                        ctx2[tt] = rms1(xres[:, tt, :], B_xres[tt])

            if dbg is not None and dbg[0] == "C" and s == dbg[2]:
                dma(POOL, ds_dbg, dbg_out.rearrange("(t p) d -> p t d", p=128), xres[:], reads=B_xres)

            mlp_rms2(2)
            mlp_rms2(3)
            for fq in range(4):
                for i in range(4):
                    wt, wtb = load_w(wb_up[:, fq * 2048 + i * 512: fq * 2048 + (i + 1) * 512], 16, ("up", fq * 4 + i))
                    for j in range(4):
                        ps, psbuf = psf_rot.next()
                        for kc in range(16):
                            op(PE, lambda e, kc=kc, ps=ps, j=j, wt=wt: e.matmul(ps[:, :], lhsT=wt[:, kc, j * 128:(j + 1) * 128],
                                                                              rhs=hTc[:, kc, :], start=(kc == 0),
                                                                              stop=(kc == 15)),
                               reads=[wtb] + B_hTc, writes=[psbuf], signal=(kc == 15))
                        ri = nxt("relu")
                        op(ACT, lambda e, ps=ps, ri=ri: e.activation(out=relu_t[ri][:], in_=ps[:, :], func=AF.Relu),
                           reads=[psbuf], writes=[B_relu[ri]])
                        op(DVE, lambda e, ps=ps, ri=ri, cc=i * 4 + j: e.tensor_tensor(out=actT[:, cc, :], in0=ps[:, :],
                                                                                          in1=relu_t[ri][:], op=ALU.mult),
                           reads=[psbuf, B_relu[ri]], writes=B_QT[i * 4 + j])
                    if fq == 1 or (fq == 2 and i == 0):
                        pro_early()
                    if fq == 3 and i == 3:
                        pro_late()
                for n in range(4):
                    wt, wtb = load_w(wb_down[fq * 2048:(fq + 1) * 2048, n * 512:(n + 1) * 512], 16, ("down", fq * 4 + n))
                    for tt in range(4):
                        ps, psbuf = psf_rot.next()
                        for kc in range(16):
                            op(PE, lambda e, kc=kc, ps=ps, tt=tt, wt=wt: e.matmul(
                                ps[:, :], lhsT=actT[:, kc, tt * 128:(tt + 1) * 128], rhs=wt[:, kc, :],
                                start=(kc == 0), stop=(kc == 15)),
                               reads=[wtb] + B_QT[kc], writes=[psbuf], signal=(kc == 15))
                        op(DVE, lambda e, ps=ps, tt=tt, n=n: e.tensor_tensor(out=xres[:, tt, n * 512:(n + 1) * 512], in0=ps[:, :],
                                                                           in1=xres[:, tt, n * 512:(n + 1) * 512], op=ALU.add),
                           reads=[psbuf, B_xres[tt]], writes=[B_xres[tt]])
                    if fq == 3 and n < 3:
                        pro_late()
            def emit_stores(s=s):
                for tt in range(4):
                    dma(POOL, ds_st[tt], y[s * T + tt * 128: s * T + (tt + 1) * 128, :], xres[:, tt, :], reads=[B_xres[tt]])
            if s + 1 < nslots:
                deferred_stores.append(emit_stores)
            else:
                emit_stores()
            assert not early_steps and not late_steps

        for tt in range(4):
            POOL.wait((ds_st[tt], ds_st[tt].count))
        if dbg is not None:
            POOL.wait((ds_dbg, ds_dbg.count))

        block = es.enter_context(nc.Block())

        @block.tensor
        def _(e):
            for f in PE.ops:
                f(e)

        @block.scalar
        def _(e):
            for f in ACT.ops:
                f(e)

        @block.vector
        def _(e):
            for f in DVE.ops:
                f(e)

        @block.gpsimd
        def _(e):
            for f in POOL.ops:
                f(e)

        @block.sync
        def _(e):
            for f in SP.ops:
                f(e)

    return nc


def _t5_bucket(rel):
    nb = 16
    max_exact = 8
    ret = (rel > 0).astype(np.int32) * nb
    n = np.abs(rel).astype(np.int32)
    nf = np.maximum(n, max_exact).astype(np.float32)
    large = max_exact + (np.log(nf / max_exact) / np.log(128 / max_exact) * (nb - max_exact)).astype(np.int32)
    large = np.minimum(large, nb - 1)
    return ret + np.where(n < max_exact, n, large)


def _window_bias_layout(t5_bias):
    kl = np.arange(128)[:, None, None]
    dl = np.arange(-1, 2)[None, :, None]
    ql = np.arange(128)[None, None, :]
    rel = dl * 128 + kl - ql
    idx = _t5_bucket(rel)
    out = t5_bias[idx]
    out = np.where((np.abs(rel) <= 128)[..., None], out, np.float32(NEGM))
    return np.ascontiguousarray(out.transpose(0, 3, 1, 2)).reshape(128, -1).astype(np.float32)


def _na_bias_layout(rpb):
    a = (np.arange(128) // 64)[:, None, None, None]
    kc = (np.arange(128) % 64)[:, None, None, None]
    ur = np.arange(7)[None, :, None, None]
    rr = np.arange(2)[None, None, :, None]
    qc = np.arange(64)[None, None, None, :]
    j = 13 - 2 * ur + rr - a
    dr = 14 - j
    ok_r = (j >= 0) & (j <= 14)
    dcc = np.clip(kc - qc, -15, 15) + 15
    start_c = np.clip(qc - 8, 0, 64 - 16)
    ok_c = (kc >= start_c) & (kc < start_c + 16)
    drc = np.clip(dr, 0, 14)
    drc, dccb = np.broadcast_arrays(drc, dcc)
    g = rpb[:, drc, dccb]
    ok = np.broadcast_to(ok_r & ok_c, g.shape[1:])
    g = np.where(ok[None], g, np.float32(NEGM))
    return np.ascontiguousarray(g.transpose(1, 0, 2, 3, 4)).reshape(128, -1).astype(np.float32)


def _lsel_layout():
    L = np.zeros((128, 8, 128), np.float32)
    for kt in range(8):
        for a in range(2):
            L[2 * kt + a, kt, a * 64:(a + 1) * 64] = 1.0
    return L.reshape(128, -1)


def _mask_tables(cases):
    ns = len(cases)
    wm = np.zeros((128, ns, 2), np.float32)
    mrow = np.zeros((128, ns, 8), np.float32)
    for s, cs in enumerate(cases):
        if "S" in cs:
            wm[:, s, 0] = NEGM
        if "E" in cs:
            wm[:, s, 1] = NEGM
        for r in range(8):
            rs = r - 4
            if "S" in cs:
                rs = max(rs, 0)
            if "E" in cs:
                rs = min(rs, 0)
            for i in range(16):
                kr = i - 4
                mrow[i, s, r] = 0.0 if (rs <= kr < rs + 8) else NEGM
    return wm.reshape(128, -1).astype(np.float32), mrow.reshape(128, -1).astype(np.float32)


_PROG_CACHE = {}


def _get_prog(nslots, slot_rows, slot_types, slot_cases, dbg=None):
    key = (nslots, tuple(slot_rows), tuple(slot_types), tuple(slot_cases), None if dbg is None else (dbg[0], tuple(dbg[1]), dbg[2]))
    if key not in _PROG_CACHE:
        _PROG_CACHE[key] = build_program(nslots, list(slot_rows), list(slot_types), list(slot_cases), dbg)
    return _PROG_CACHE[key]


def _common_inputs(norm_mix_g, w_in, q_norm_a, k_norm_a, t5_bias, sink_a, q_norm_b, k_norm_b, rpb_b,
                   w_br_a, w_br_b, w_out, norm_mlp_g, w_up, w_down):
    f = lambda a: np.ascontiguousarray(np.asarray(a, dtype=np.float32))
    pvec = np.zeros((128, 36), np.float32)
    pvec[:, 0:16] = f(norm_mix_g).reshape(16, 128).T
    pvec[:, 16:32] = f(norm_mlp_g).reshape(16, 128).T
    pvec[:, 32] = f(q_norm_a).reshape(128)
    pvec[:, 33] = f(k_norm_a).reshape(128)
    pvec[:, 34] = f(q_norm_b).reshape(128)
    pvec[:, 35] = f(k_norm_b).reshape(128)
    return {
        "w_in": f(w_in).reshape(D, IN_W),
        "w_bra": f(w_br_a).reshape(1024, D),
        "w_brb": f(w_br_b).reshape(1024, D),
        "w_out": f(w_out).reshape(D, D),
        "w_up": f(w_up).reshape(D, DFF),
        "w_down": f(w_down).reshape(DFF, D),
        "pvec": pvec,
        "sinkrep": np.ascontiguousarray(np.broadcast_to(f(sink_a).reshape(1, 8), (128, 8))),
        "wbias": _window_bias_layout(f(t5_bias)),
        "nbias": _na_bias_layout(f(rpb_b).reshape(8, 15, 31)),
        "ident": np.eye(128, dtype=np.float32),
        "lsel": _lsel_layout(),
    }


def kernel(x_prompt, x_sample, norm_mix_g, w_in, q_norm_a, k_norm_a, t5_bias, sink_a, q_norm_b, k_norm_b,
           rpb_b, w_br_a, w_br_b, w_out, norm_mlp_g, w_up, w_down):
    x_prompt = np.asarray(x_prompt, dtype=np.float32)
    x_sample = np.asarray(x_sample, dtype=np.float32)
    common = _common_inputs(norm_mix_g, w_in, q_norm_a, k_norm_a, t5_bias, sink_a, q_norm_b, k_norm_b, rpb_b,
                            w_br_a, w_br_b, w_out, norm_mlp_g, w_up, w_down)
    nslots = 10
    slot_rows = [512 * s for s in range(8)] + [4608, 4608 + 512]
    in_maps = []
    for c in range(NCORES):
        xin = np.zeros((4608 + 1536, D), np.float32)
        xin[256:256 + 4096] = x_prompt[c]
        sq, half = c // 2, c % 2
        if half == 0:
            xin[4608 + 256: 4608 + 1536] = x_sample[sq, 0:1280]
            scases = ["S", "I"]
        else:
            xin[4608: 4608 + 1280] = x_sample[sq, 768:2048]
            scases = ["I", "E"]
        cases = ["S"] + ["I"] * 6 + ["E"] + scases
        m = dict(common)
        m["xin"] = xin
        m["masks"], m["mrow"] = _mask_tables(cases)
        in_maps.append(m)
    slot_types = ["F"] + ["C"] * 7 + ["F", "C"]
    slot_cases = ["S"] + ["I"] * 6 + ["E"] + [None, None]
    nc = _get_prog(nslots, slot_rows, slot_types, slot_cases)
    res = run_bass_kernel_spmd(nc, in_maps, core_ids=list(range(NCORES)))
    y_prompt = np.empty((8, 4096, D), np.float32)
    y_sample = np.empty((4, 2048, D), np.float32)
    for c in range(NCORES):
        yc = np.asarray(res.results[c]["y"], dtype=np.float32)
        y_prompt[c] = yc[0:4096]
        sq, half = c // 2, c % 2
        y_sample[sq, half * 1024:(half + 1) * 1024] = yc[4096:5120]
    return (y_prompt, y_sample)
```
